# Optimizing a Trainium2 kernel written in Bass

```python
import math
import jax, jax.numpy as jnp
from jax import lax
import numpy as np

D_MODEL = 1024
BATCH = 4
SEQ = 4096
DEPTH = 2
DEC_BATCH = 2
DEC_SEQ = 8192
PAST_LEN = 128

RET_HEADS = 4
RET_HEAD_DIM = 128
RET_DIM = RET_HEADS * RET_HEAD_DIM
RET_CHUNK = 128
CONV_DIM = D_MODEL // 2
CONV_WIDTH = 31
CONV_PAD = CONV_WIDTH // 2
AB_IN_DIM = 4 * RET_DIM + 2 * CONV_DIM
AB_OUT_DIM = RET_DIM + CONV_DIM
DIFF_HEADS = 8
DIFF_QK_DIM = 64
DIFF_V_DIM = 2 * DIFF_QK_DIM
DIFF_QK_TOTAL = DIFF_HEADS * 2 * DIFF_QK_DIM
DIFF_V_TOTAL = DIFF_HEADS * DIFF_V_DIM
C_IN_DIM = 2 * DIFF_QK_TOTAL + DIFF_V_TOTAL
Q_BLOCK = 128
D_FF = 2816
ROPE_THETA = 10000.0
EPS = 1e-6
N_EVEN = (DEPTH + 1) // 2
N_ODD = DEPTH // 2

kernel_name = 'hybrid_retention_conv_diffattn_encoder'


def lambda_init_for(layer):
    return 0.8 - 0.6 * math.exp(-0.3 * layer)


def rms_norm(x, g):
    xf = x.astype(jnp.float32)
    y = xf * lax.rsqrt(jnp.mean(xf * xf, axis=-1, keepdims=True) + EPS)
    return (y * g.astype(jnp.float32)).astype(x.dtype)


def rope(x):
    s, d = x.shape[1], x.shape[-1]
    half = d // 2
    inv = ROPE_THETA ** (-jnp.arange(half, dtype=jnp.float32) / half)
    ang = jnp.arange(s, dtype=jnp.float32)[:, None] * inv[None, :]
    cos = jnp.cos(ang)[None, :, None, :]
    sin = jnp.sin(ang)[None, :, None, :]
    xf = x.astype(jnp.float32)
    x1, x2 = xf[..., :half], xf[..., half:]
    return jnp.concatenate([x1 * cos - x2 * sin, x2 * cos + x1 * sin], axis=-1).astype(x.dtype)


def swiglu(x, w_in, w_out):
    g, u = jnp.split(x @ w_in, 2, axis=-1)
    return (jax.nn.silu(g) * u) @ w_out


def retention_one_way(q, k, v, log_gamma, strict):
    b, h, s, d = q.shape
    c = RET_CHUNK
    nc = s // c
    qc = q.reshape(b, h, nc, c, d)
    kc = k.reshape(b, h, nc, c, d)
    vc = v.reshape(b, h, nc, c, v.shape[-1])
    idx = jnp.arange(c, dtype=jnp.float32)
    diff = idx[:, None] - idx[None, :]
    mask = diff > 0 if strict else diff >= 0
    dmat = jnp.where(mask[None], jnp.exp(log_gamma[:, None, None] * jnp.maximum(diff, 0.0)[None]), 0.0)
    scores = jnp.einsum('bhnid,bhnjd->bhnij', qc, kc) * dmat[None, :, None].astype(q.dtype)
    intra = jnp.einsum('bhnij,bhnje->bhnie', scores, vc)
    k_decay = jnp.exp(log_gamma[:, None] * (c - 1 - idx)[None]).astype(k.dtype)
    kv = jnp.einsum('bhnjd,bhnje->nbhde', kc * k_decay[None, :, None, :, None], vc)
    chunk_decay = jnp.exp(log_gamma * c).astype(kv.dtype)[None, :, None, None]

    def step(state, kv_n):
        return state * chunk_decay + kv_n, state

    _, prev = lax.scan(step, jnp.zeros(kv.shape[1:], kv.dtype), kv)
    q_decay = jnp.exp(log_gamma[:, None] * (idx + 1)[None]).astype(q.dtype)
    inter = jnp.einsum('bhnid,nbhde->bhnie', qc * q_decay[None, :, None, :, None], prev)
    return (intra + inter).reshape(b, h, s, v.shape[-1])


def ab_mixer(hn, w_in, decay, ret_norm_g, conv_w, conv_b, conv_norm_g, w_out):
    b, s, _ = hn.shape
    q, k, v, g, ca, cg = jnp.split(hn @ w_in, [RET_DIM, 2 * RET_DIM, 3 * RET_DIM, 4 * RET_DIM, 4 * RET_DIM + CONV_DIM], axis=-1)
    q = rope(q.reshape(b, s, RET_HEADS, RET_HEAD_DIM)).transpose(0, 2, 1, 3)
    k = (rope(k.reshape(b, s, RET_HEADS, RET_HEAD_DIM)) * RET_HEAD_DIM ** -0.5).transpose(0, 2, 1, 3)
    v = v.reshape(b, s, RET_HEADS, RET_HEAD_DIM).transpose(0, 2, 1, 3)
    log_gamma = -jnp.exp(decay.astype(jnp.float32))
    fwd = retention_one_way(q, k, v, log_gamma[0], False)
    bwd = jnp.flip(retention_one_way(jnp.flip(q, 2), jnp.flip(k, 2), jnp.flip(v, 2), log_gamma[1], True), 2)
    o = (fwd + bwd).transpose(0, 2, 1, 3)
    o = rms_norm(o, ret_norm_g.reshape(RET_HEADS, RET_HEAD_DIM)).reshape(b, s, RET_DIM)
    ret_out = jax.nn.silu(g) * o
    u = ca * jax.nn.sigmoid(cg)
    u = lax.conv_general_dilated(u, conv_w[:, None, :], window_strides=(1,), padding=[(CONV_PAD, CONV_PAD)],
                                 dimension_numbers=('NWC', 'WIO', 'NWC'), feature_group_count=CONV_DIM) + conv_b
    u = jax.nn.silu(rms_norm(u, conv_norm_g))
    return jnp.concatenate([ret_out, u], axis=-1) @ w_out


def diff_attn_mixer(hn, w_in, q_norm_g, k_norm_g, lam, subln_g, w_out, lambda_init):
    b, s, _ = hn.shape
    q, k, v = jnp.split(hn @ w_in, [DIFF_QK_TOTAL, 2 * DIFF_QK_TOTAL], axis=-1)
    q = rope(rms_norm(q.reshape(b, s, 2 * DIFF_HEADS, DIFF_QK_DIM), q_norm_g)) * DIFF_QK_DIM ** -0.5
    k = rope(rms_norm(k.reshape(b, s, 2 * DIFF_HEADS, DIFF_QK_DIM), k_norm_g))
    q = q.reshape(b, s, DIFF_HEADS, 2, DIFF_QK_DIM)
    k = k.reshape(b, s, DIFF_HEADS, 2, DIFF_QK_DIM)
    v = v.reshape(b, s, DIFF_HEADS, DIFF_V_DIM)
    lf = lam.astype(jnp.float32)
    lam_full = jnp.exp(jnp.sum(lf[0] * lf[1])) - jnp.exp(jnp.sum(lf[2] * lf[3])) + lambda_init
    nq = s // Q_BLOCK
    qb = q.reshape(b, nq, Q_BLOCK, DIFF_HEADS, 2, DIFF_QK_DIM).transpose(1, 0, 2, 3, 4, 5)

    def block(q_blk):
        sc = jnp.einsum('bqhcd,bkhcd->bhcqk', q_blk, k).astype(jnp.float32)
        p = jax.nn.softmax(sc, axis=-1)
        w = (p[:, :, 0] - lam_full * p[:, :, 1]).astype(v.dtype)
        return jnp.einsum('bhqk,bkhe->bqhe', w, v)

    o = lax.map(block, qb)
    o = o.transpose(1, 0, 2, 3, 4).reshape(b, s, DIFF_HEADS, DIFF_V_DIM)
    o = rms_norm(o, subln_g) * (1.0 - lambda_init)
    return o.reshape(b, s, DIFF_V_TOTAL) @ w_out


def trunk(x, norm_g, ffn_w_in, ffn_w_out, ab_w_in, ab_decay, ab_ret_norm_g, ab_conv_w, ab_conv_b,
          ab_conv_norm_g, ab_w_out, c_w_in, c_q_norm_g, c_k_norm_g, c_lambda, c_subln_g, c_w_out):
    for layer in range(DEPTH):
        x = x + 0.5 * swiglu(rms_norm(x, norm_g[layer, 0]), ffn_w_in[layer, 0], ffn_w_out[layer, 0])
        hn = rms_norm(x, norm_g[layer, 1])
        i = layer // 2
        if layer % 2 == 0:
            x = x + ab_mixer(hn, ab_w_in[i], ab_decay[i], ab_ret_norm_g[i], ab_conv_w[i], ab_conv_b[i],
                             ab_conv_norm_g[i], ab_w_out[i])
        else:
            x = x + diff_attn_mixer(hn, c_w_in[i], c_q_norm_g[i], c_k_norm_g[i], c_lambda[i], c_subln_g[i],
                                    c_w_out[i], lambda_init_for(layer))
        x = x + 0.5 * swiglu(rms_norm(x, norm_g[layer, 2]), ffn_w_in[layer, 1], ffn_w_out[layer, 1])
    return x


def setup_inputs(seed: int = 0) -> dict:
    key = jax.random.key(seed)
    ks = jax.random.split(key, 20)
    f32 = jnp.float32
    nrm = lambda k, shape, scale: jax.random.normal(k, shape, f32) * scale
    decay0 = np.log(-np.log(1.0 - 2.0 ** (-5.0 - np.arange(RET_HEADS)))).astype(np.float32)
    return {
        'x_prompt': nrm(ks[0], (BATCH, SEQ, D_MODEL), 1.0),
        'x_sample': nrm(ks[1], (DEC_BATCH, DEC_SEQ, D_MODEL), 1.0),
        'norm_g': 1.0 + nrm(ks[2], (DEPTH, 3, D_MODEL), 0.02),
        'ffn_w_in': nrm(ks[3], (DEPTH, 2, D_MODEL, 2 * D_FF), D_MODEL ** -0.5),
        'ffn_w_out': nrm(ks[4], (DEPTH, 2, D_FF, D_MODEL), D_FF ** -0.5),
        'ab_w_in': nrm(ks[5], (N_EVEN, D_MODEL, AB_IN_DIM), D_MODEL ** -0.5),
        'ab_decay': jnp.asarray(decay0)[None, None, :] + nrm(ks[6], (N_EVEN, 2, RET_HEADS), 0.05),
        'ab_ret_norm_g': 1.0 + nrm(ks[7], (N_EVEN, RET_DIM), 0.02),
        'ab_conv_w': nrm(ks[8], (N_EVEN, CONV_WIDTH, CONV_DIM), CONV_WIDTH ** -0.5),
        'ab_conv_b': nrm(ks[9], (N_EVEN, CONV_DIM), 0.01),
        'ab_conv_norm_g': 1.0 + nrm(ks[10], (N_EVEN, CONV_DIM), 0.02),
        'ab_w_out': nrm(ks[11], (N_EVEN, AB_OUT_DIM, D_MODEL), AB_OUT_DIM ** -0.5),
        'c_w_in': nrm(ks[12], (N_ODD, D_MODEL, C_IN_DIM), D_MODEL ** -0.5),
        'c_q_norm_g': 1.0 + nrm(ks[13], (N_ODD, DIFF_QK_DIM), 0.02),
        'c_k_norm_g': 1.0 + nrm(ks[14], (N_ODD, DIFF_QK_DIM), 0.02),
        'c_lambda': nrm(ks[15], (N_ODD, 4, DIFF_QK_DIM), 0.1),
        'c_subln_g': 1.0 + nrm(ks[16], (N_ODD, DIFF_V_DIM), 0.02),
        'c_w_out': nrm(ks[17], (N_ODD, DIFF_V_TOTAL, D_MODEL), DIFF_V_TOTAL ** -0.5),
    }


def reference(x_prompt, x_sample, norm_g, ffn_w_in, ffn_w_out, ab_w_in, ab_decay, ab_ret_norm_g, ab_conv_w,
              ab_conv_b, ab_conv_norm_g, ab_w_out, c_w_in, c_q_norm_g, c_k_norm_g, c_lambda, c_subln_g, c_w_out):
    y_prompt = trunk(x_prompt, norm_g, ffn_w_in, ffn_w_out, ab_w_in, ab_decay, ab_ret_norm_g, ab_conv_w, ab_conv_b,
                     ab_conv_norm_g, ab_w_out, c_w_in, c_q_norm_g, c_k_norm_g, c_lambda, c_subln_g, c_w_out)
    y_sample = trunk(x_sample, norm_g, ffn_w_in, ffn_w_out, ab_w_in, ab_decay, ab_ret_norm_g, ab_conv_w, ab_conv_b,
                     ab_conv_norm_g, ab_w_out, c_w_in, c_q_norm_g, c_k_norm_g, c_lambda, c_subln_g, c_w_out)
    return (y_prompt, y_sample)
```

```python
import math
from contextlib import ExitStack

import numpy as np
import concourse.bass as bass
import concourse.mybir as mybir
from concourse.bass_utils import run_bass_kernel_spmd

F32 = mybir.dt.float32
BF16 = mybir.dt.bfloat16
ALU = mybir.AluOpType
AF = mybir.ActivationFunctionType
AX = mybir.AxisListType

D = 1024
T = 4096
TT = 512
NT = T // TT
DFF = 2816
NJ = DFF // 128
EPS = 1e-6
LAMBDA_INIT = 0.8 - 0.6 * math.exp(-0.3 * 1)
NF = 17 * 1024
NB = 69 * 1024
NEG = -30000.0


class Op:
    __slots__ = ("eng", "fn", "reads", "writes", "dma", "deps", "sig", "val", "eidx", "idx", "persist")


class Prog:
    def __init__(self):
        self.ops = []

    def add(self, eng, fn, reads=(), writes=(), dma=None, persist=False):
        op = Op()
        op.persist = persist
        op.eng = eng
        op.fn = fn
        reads = tuple(reads)
        op.reads = reads
        op.writes = tuple(writes) + tuple(r for r in reads if isinstance(r, tuple) and r[0] == "ps")
        op.dma = dma
        op.deps = set()
        op.sig = False
        op.val = 0
        self.ops.append(op)
        return op

    def barrier(self):
        self.ops.append(None)

    def analyze(self):
        last_writer = {}
        pers_writer = {}
        readers = {}
        last_on_eng = {}
        last_async = {}
        pending = {}
        ecount = {}
        real = []
        for op in self.ops:
            if op is None:
                bd = set(last_on_eng.values()) | set(last_async.values())
                for e in ("pe", "act", "dve", "pool", "sp"):
                    pending[e] = set(bd) | pending.get(e, set())
                last_writer = {}
                readers = {}
                continue
            op.idx = len(real)
            real.append(op)
            op.eidx = ecount.get(op.eng, 0)
            ecount[op.eng] = op.eidx + 1
            deps = set()
            for r in op.reads:
                w = last_writer.get(r)
                if w is not None:
                    deps.add(w)
                w = pers_writer.get(r)
                if w is not None:
                    deps.add(w)
            if op.persist:
                for w_ in op.writes:
                    pers_writer[w_] = op
                op.deps = deps
                continue
            for w_ in op.writes:
                w = last_writer.get(w_)
                if w is not None:
                    deps.add(w)
                for rd in readers.get(w_, ()):
                    deps.add(rd)
            if op.eng in pending:
                deps |= pending.pop(op.eng)
            deps.discard(op)
            op.deps = deps
            for r in op.reads:
                readers.setdefault(r, []).append(op)
            for w_ in op.writes:
                last_writer[w_] = op
                readers[w_] = []
            if op.dma is not None:
                last_async[id(op.dma[0])] = op
            else:
                last_on_eng[op.eng] = op
        self.real = real
        for op in real:
            keep = set()
            for d in op.deps:
                if d.dma is None and d.eng == op.eng:
                    if op.dma is None or True:
                        if op.eng == "pe" or op.eng == "sp":
                            continue
                        if op.eidx - d.eidx > 2:
                            continue
                keep.add(d)
            op.deps = keep
            for d in keep:
                if d.dma is None:
                    d.sig = True
        cnt = {}
        acnt = {}
        for op in real:
            if op.dma is not None:
                k = id(op.dma[0])
                acnt[k] = acnt.get(k, 0) + op.dma[1]
                op.val = acnt[k]
            elif op.sig:
                cnt[op.eng] = cnt.get(op.eng, 0) + 1
                op.val = cnt[op.eng]
        self.final_async = {}
        for op in real:
            if op.dma is not None:
                self.final_async[id(op.dma[0])] = (op.dma[0], op.val, op.eng)

    def emit(self, engname, e, esems):
        waited = {}
        for op in self.real:
            if op.eng != engname:
                continue
            need = {}
            for d in op.deps:
                if d.dma is not None:
                    s = d.dma[0]
                else:
                    s = esems[d.eng]
                k = id(s)
                if k not in need or need[k][1] < d.val:
                    need[k] = (s, d.val)
            for k, (s, v) in need.items():
                if waited.get(k, 0) < v:
                    e.wait_ge(s, v)
                    waited[k] = v
            ins = op.fn(e)
            if op.dma is not None:
                ins.then_inc(op.dma[0], op.dma[1])
            elif op.sig:
                ins.then_inc(esems[op.eng], 1)
        for k, (s, v, eng) in self.final_async.items():
            if eng == engname and waited.get(k, 0) < v:
                e.wait_ge(s, v)


class _Stop(Exception):
    pass


class Arena:
    def __init__(self, t, n):
        self.t = t
        self.n = n
        self.top = 0

    def alloc(self, *shape):
        size = int(np.prod(shape))
        off = self.top
        self.top += size
        assert self.top <= self.n, ("arena overflow", self.top, self.n)
        ap = self.t[:, off:off + size]
        if len(shape) == 2:
            ap = ap.rearrange("p (a b) -> p a b", b=shape[1])
        elif len(shape) == 3:
            ap = ap.rearrange("p (a b c) -> p a b c", b=shape[1], c=shape[2])
        return ap


def build_program(dbg=None):
    nc = bass.Bass("TRN2", target_bir_lowering=False)

    def din(name, shape, dt=F32):
        return nc.dram_tensor(name, list(shape), dt, kind="ExternalInput").ap()

    def dscr(name, shape, dt=BF16):
        kind = "ExternalOutput" if (dbg and name in dbg) else None
        if kind:
            return nc.dram_tensor(name, list(shape), dt, kind=kind).ap()
        return nc.dram_tensor(name, list(shape), dt).ap()

    small = bool((dbg or {}).get("small"))
    xT = din("xT", [D, T])
    if not small:
        w_ffn_in = din("ffn_w_in", [2, 2, D, 2 * DFF])
        w_ffn_out = din("ffn_w_out", [2, 2, DFF, D])
        w_ab_in = din("ab_w_in", [D, 3072])
        w_ab_out = din("ab_w_out", [D, D])
        w_c_in = din("c_w_in", [D, 3072])
        w_c_out = din("c_w_out", [D, D])
    gvec_d = din("gvec", [128, 64])
    convw_d = din("convw", [128, 124])
    decay_d = din("decay", [128, 8])
    lam_d = din("lam", [128, 256])
    gqk_d = din("gqk", [128, 128])
    rope0_d = din("rope0", [2, 128, T])
    rope1_d = din("rope1", [2, 128, T])
    cmask_d = din("cmask", [128, 64])
    cst_d = din("cst", [128, 5 * 128 + 6 * 512 + 2 + 32])
    tpos_d = din("tpos", [128, T])
    yT = nc.dram_tensor("yT", [D, T], F32, kind="ExternalOutput").ap()

    WIN = [[dscr(f"WIN{l}{i}", [NJ, 128, 8 * 256]) for i in range(2)] for l in range(2)]
    WOUT = [[dscr(f"WOUT{l}{i}", [8, 128, NJ * 128]) for i in range(2)] for l in range(2)]
    WAB = dscr("WAB", [6, 128, 8 * 512])
    WABO = dscr("WABO", [2, 128, 8 * 512])
    WC = dscr("WC", [6, 128, 8 * 512])
    WCO = dscr("WCO", [2, 128, 8 * 512])
    X1 = dscr("X1", [D, T], F32)
    X2 = dscr("X2", [D, T], F32)
    X4 = dscr("X4", [D, T], F32)
    QT0 = dscr("QT0", [512, T])
    KT0 = dscr("KT0", [512, T])
    V0 = dscr("V0", [T, 512])
    G0 = dscr("G0", [512, T])
    U0 = dscr("U0", [512, 2 * 2080])
    FS = dscr("FS", [NT, 128, 16 * 128])
    EXin = [dscr(f"EXin{i}", [512, 288], F32) for i in range(2)]
    EXout = [dscr(f"EXout{i}", [2048, 288], F32) for i in range(2)]
    QT1 = dscr("QT1", [8, 128, T])
    KVinA = [dscr(f"KVinA{h}", [256, 2048]) for h in range(8)]
    KVinB = [dscr(f"KVinB{h}", [256, 2048]) for h in range(8)]
    KVoutA = [dscr(f"KVoutA{h}", [1024, 2048]) for h in range(8)]
    KVoutB = [dscr(f"KVoutB{h}", [512, 2048]) for h in range(8)]
    ON = dscr("ON", [8, 128, T])

    es = ExitStack()
    with es:
        AFt = es.enter_context(nc.sbuf_tensor("AF", [128, NF], F32))
        ABt = es.enter_context(nc.sbuf_tensor("AB", [128, NB], BF16))
        PSall = es.enter_context(nc.psum_tensor("psall", [128, 4096], F32))
        PS = [PSall[:, i * 512:(i + 1) * 512] for i in range(8)]
        esems = {e: es.enter_context(nc.semaphore("s_" + e)) for e in ("pe", "act", "dve", "pool", "sp")}
        nsem = [0]

        free_sems = []
        phase_sems = []

        def newsem(persist=False):
            if not persist and free_sems:
                sm = free_sems.pop()
            else:
                nsem[0] += 1
                sm = es.enter_context(nc.semaphore(f"d{nsem[0]}"))
            if not persist:
                phase_sems.append(sm)
            return sm

        def release_phase_sems():
            free_sems.extend(phase_sems)
            del phase_sems[:]

        af = Arena(AFt, NF)
        ab = Arena(ABt, NB)
        prog = Prog()
        stop_at = (dbg or {}).get("stop")

        def stop_point(name):
            if stop_at == name:
                raise _Stop()

        def PE(out, lhsT, rhs, start, stop, R, W):
            prog.add("pe", lambda e: e.matmul(out, lhsT=lhsT, rhs=rhs, start=start, stop=stop), R, W)

        def ACT(out, in_, func, R, W, scale=None, bias=None, eng="act"):
            kw = {}
            if scale is not None:
                kw["scale"] = scale
            if bias is not None:
                kw["bias"] = bias
            prog.add(eng, lambda e: e.activation(out, in_, func, **kw), R, W)

        def TT_(out, a, b, op, R, W, eng="dve"):
            prog.add(eng, lambda e: e.tensor_tensor(out, a, b, op), R, W)

        def TS(out, a, s1, s2, op0, op1, R, W, eng="dve"):
            if s2 is None:
                prog.add(eng, lambda e: e.tensor_scalar(out, a, s1, None, op0), R, W)
            else:
                prog.add(eng, lambda e: e.tensor_scalar(out, a, s1, s2, op0, op1), R, W)

        def STT(out, a, s, b, op0, op1, R, W, eng="dve"):
            prog.add(eng, lambda e: e.scalar_tensor_tensor(out, a, s, b, op0, op1), R, W)

        def CP(out, in_, R, W, eng="dve"):
            prog.add(eng, lambda e: e.tensor_copy(out, in_), R, W)

        def DMA(eng, out, in_, sem, R, W, persist=False):
            prog.add(eng, lambda e: e.dma_start(out=out, in_=in_), R, W, dma=(sem, 16), persist=persist)

        def ps(b):
            return PS[b][:, :]

        def psk(b):
            return ("ps", b)

        try:
            def cast_group(key, pairs):
                s = newsem()
                n = len(pairs)
                for i, (dst, src) in enumerate(pairs):
                    DMA("pool", dst, src, s, [], [key] if i == n - 1 else [])

            def win_pairs(l, i):
                prs = []
                for j in range(NJ):
                    for half in range(2):
                        src = w_ffn_in[l, i][:, half * DFF + j * 128: half * DFF + (j + 1) * 128].rearrange(
                            "(k p) c -> p k c", p=128)
                        dst = WIN[l][i][j].rearrange("p (k c) -> p k c", c=256)[:, :, half * 128:(half + 1) * 128]
                        prs.append((dst, src))
                return prs

            def wout_pairs(l, i):
                prs = []
                for m in range(8):
                    src = w_ffn_out[l, i][:, m * 128:(m + 1) * 128].rearrange("(j p) c -> p j c", p=128)
                    dst = WOUT[l][i][m].rearrange("p (j c) -> p j c", c=128)
                    prs.append((dst, src))
                return prs

            def proj_pairs(dst_t, src_w, ngrp):
                prs = []
                for g in range(ngrp):
                    for hh in range(2):
                        src = src_w[:, g * 512 + hh * 256: g * 512 + (hh + 1) * 256].rearrange("(k p) c -> p k c", p=128)
                        dst = dst_t[g].rearrange("p (k c) -> p k c", c=512)[:, :, hh * 256:(hh + 1) * 256]
                        prs.append((dst, src))
                return prs

            cast_q = []

            def cast_enqueue(key, pairs):
                sm = newsem(persist=True)
                n = len(pairs)
                for i, (dst, src) in enumerate(pairs):
                    cast_q.append((dst, src, sm, [key] if i == n - 1 else []))

            def issue_casts(n):
                for _ in range(min(n, len(cast_q))):
                    dst, src, sm, wk = cast_q.pop(0)
                    DMA("pool", dst, src, sm, [], wk, persist=True)

            if not small:
                wp = win_pairs(0, 0)
                for q4 in range(4):
                    cast_enqueue(("WIN", 0, 0, q4), wp[q4 * 12:(q4 + 1) * 12] if q4 < 3 else wp[36:])
                cast_enqueue(("WOUT", 0, 0), wout_pairs(0, 0))
                cast_enqueue("WAB", proj_pairs(WAB, w_ab_in, 6))
                issue_casts(len(cast_q))
                cast_enqueue("WABO", proj_pairs(WABO, w_ab_out, 2))
                cast_enqueue(("WIN", 0, 1), win_pairs(0, 1))
                cast_enqueue(("WOUT", 0, 1), wout_pairs(0, 1))
                cast_enqueue(("WIN", 1, 0), win_pairs(1, 0))
                cast_enqueue(("WOUT", 1, 0), wout_pairs(1, 0))
                cast_enqueue("WC", proj_pairs(WC, w_c_in, 6))
                cast_enqueue("WCO", proj_pairs(WCO, w_c_out, 2))
                cast_enqueue(("WIN", 1, 1), win_pairs(1, 1))
                cast_enqueue(("WOUT", 1, 1), wout_pairs(1, 1))
            stop_point("cast")

            gvec = af.alloc(64)
            convw = af.alloc(124)
            lg = af.alloc(8)
            cd = af.alloc(8)
            kcol = af.alloc(8)
            wbc = af.alloc(128)
            misc = af.alloc(16)
            cmask = af.alloc(64)
            identf = af.alloc(128)
            cb = ab.alloc(5, 128)
            Dt = ab.alloc(4, 512)
            rowf = ab.alloc(4, 512)
            rowb = ab.alloc(4, 512)
            IDENT, ONES, BONES, R0, R1 = (cb[:, i, :] for i in range(5))
            afm = af.top
            abm = ab.top

            s_c = newsem()
            cst = af.alloc(5 * 128 + 6 * 512 + 2 + 32)
            decay_t = af.alloc(8)
            lam_t = af.alloc(256)
            gqk_t = af.alloc(128)
            stmp = af.alloc(4, 512)
            for dst, src, key in ((gvec, gvec_d, "gvec"), (convw, convw_d, "convw"), (cst, cst_d, "cst"),
                                  (decay_t, decay_d, "decay_t"), (lam_t, lam_d, "lam_t"), (gqk_t, gqk_d, "gqk_t"),
                                  (cmask, cmask_d, "cmask")):
                DMA("sp", dst, src[:, :], newsem(), [], [key])
            o = 5 * 128
            M1 = cst[:, o:o + 512]
            M2 = cst[:, o + 512:o + 1024]
            MLE = cst[:, o + 1024:o + 1536]
            MGT = cst[:, o + 1536:o + 2048]
            IO1 = cst[:, o + 2048:o + 2560]
            IOB = cst[:, o + 2560:o + 3072]
            o2 = o + 3072
            COLA = cst[:, o2:o2 + 1]
            COLB = cst[:, o2 + 1:o2 + 2]
            CH128 = cst[:, o2 + 2:o2 + 34]
            for i in range(5):
                CP(cb[:, i, :], cst[:, i * 128:(i + 1) * 128], ["cst"], ["cb"])
            CP(identf, cst[:, 0:128], ["cst"], ["identf"])
            prog.add("dve", lambda e: e.memset(misc[:, 0:1], EPS), [], ["misc0"])
            stop_point("s1")
            ACT(lg, decay_t, AF.Exp, ["decay_t"], ["lg"])
            TS(lg, lg, -1.0, None, ALU.mult, None, ["lg"], ["lg"])
            ACT(cd, lg, AF.Exp, ["lg"], ["cd"], scale=128.0)
            SC = 128.0 ** -0.5
            for h in range(4):
                ACT(kcol[:, h:h + 1], COLA, AF.Exp, ["lg", "cst"], ["kcol"], scale=lg[:, h:h + 1])
                ACT(kcol[:, 4 + h:5 + h], COLB, AF.Exp, ["lg", "cst"], ["kcol"], scale=lg[:, 4 + h:5 + h])
                ACT(wbc[:, h * 32:(h + 1) * 32], CH128, AF.Exp, ["lg", "cst"], ["wbc"], scale=lg[:, 4 + h:5 + h])
                ACT(rowf[:, h, :], IO1, AF.Exp, ["lg", "cst"], ["rowf"], scale=lg[:, h:h + 1])
                ACT(rowb[:, h, :], IOB, AF.Exp, ["lg", "cst"], ["rowb"], scale=lg[:, 4 + h:5 + h])
            TS(kcol, kcol, SC, None, ALU.mult, None, ["kcol"], ["kcol"])
            stop_point("s2")
            TT_(stmp[:, 2, 0:64], lam_t[:, 0:64], lam_t[:, 64:128], ALU.mult, ["lam_t"], ["st2"])
            prog.add("dve", lambda e: e.reduce_sum(misc[:, 8:9], stmp[:, 2, 0:64], axis=AX.X), ["st2"], ["m8"])
            TT_(stmp[:, 2, 64:128], lam_t[:, 128:192], lam_t[:, 192:256], ALU.mult, ["lam_t"], ["st2b"])
            prog.add("dve", lambda e: e.reduce_sum(misc[:, 9:10], stmp[:, 2, 64:128], axis=AX.X), ["st2b"], ["m9"])
            ACT(misc[:, 8:10], misc[:, 8:10], AF.Exp, ["m8", "m9"], ["m89"])
            TT_(misc[:, 1:2], misc[:, 9:10], misc[:, 8:9], ALU.subtract, ["m89"], ["m1"])
            TS(misc[:, 1:2], misc[:, 1:2], -LAMBDA_INIT, None, ALU.add, None, ["m1"], ["m1"])
            TS(misc[:, 2:3], gvec[:, 62:63], 1.0 - LAMBDA_INIT, None, ALU.mult, None, ["gvec"], ["m2"])
            stop_point("s3")
            prog.add("dve", lambda e: e.reduce_max(misc[:, 10:11], gqk_t[:, 0:64], axis=AX.X, apply_absolute_value=True),
                     ["gqk_t"], ["m10"])
            prog.add("dve", lambda e: e.reduce_max(misc[:, 11:12], gqk_t[:, 64:128], axis=AX.X, apply_absolute_value=True),
                     ["gqk_t"], ["m11"])
            TT_(misc[:, 10:11], misc[:, 10:11], misc[:, 11:12], ALU.mult, ["m10", "m11"], ["m10"])
            STT(misc[:, 3:4], misc[:, 10:11], -8.0, cmask[:, 48:49], ALU.mult, ALU.add, ["m10", "cmask"], ["m3"])
            STT(misc[:, 4:5], misc[:, 10:11], -8.0, cmask[:, 49:50], ALU.mult, ALU.add, ["m10", "cmask"], ["m4"])
            stop_point("s4")
            for h in range(4):
                ACT(stmp[:, 0, :], M1, AF.Exp, ["lg", "cst"], ["st0"], scale=lg[:, h:h + 1])
                TT_(stmp[:, 0, :], stmp[:, 0, :], MLE, ALU.mult, ["st0", "cst"], ["st0"])
                ACT(stmp[:, 1, :], M2, AF.Exp, ["lg", "cst"], ["st1"], scale=lg[:, 4 + h:5 + h])
                TT_(stmp[:, 1, :], stmp[:, 1, :], MGT, ALU.mult, ["st1", "cst"], ["st1"])
                TT_(stmp[:, 0, :], stmp[:, 0, :], stmp[:, 1, :], ALU.add, ["st0", "st1"], ["st0"])
                TS(Dt[:, h, :], stmp[:, 0, :], SC, None, ALU.mult, None, ["st0"], ["Dt"])
            stop_point("setup")
            prog.barrier()
            release_phase_sems()

            class RL:
                pass

            def alloc_rowlocal(nx=2, with_sg=True):
                r = RL()
                r.xres = [af.alloc(8, 512) for _ in range(nx)]
                r.xsem = [newsem() for _ in range(nx)]
                r.rstd = [af.alloc(512) for _ in range(2)]
                r.sg = [af.alloc(512) for _ in range(2)] if with_sg else None
                r.tmp = [af.alloc(512) for _ in range(3)]
                r.xn = ab.alloc(8, 512)
                r.sq = [ab.alloc(512) for _ in range(4)]
                r.cnt = {"rstd": 0, "sq": 0, "sg": 0, "tmp": 0}
                return r

            def alloc_ffn(r):
                r.abuf = ab.alloc(NJ, 512)
                r.win = [ab.alloc(8, 256) for _ in range(3)]
                r.winsem = [newsem() for _ in range(3)]
                r.wout = [ab.alloc(NJ, 128) for _ in range(3)]
                r.woutsem = [newsem() for _ in range(3)]
                r.nwin = 0
                r.nwout = 0

            def rr(r, name, n):
                i = r.cnt[name] % n
                r.cnt[name] += 1
                return i

            def rstd_from(r, psb, scale, W):
                ri = rr(r, "rstd", 2)
                ACT(r.rstd[ri], ps(psb), AF.Ln, [psk(psb), "misc0"], [("rstd", ri)], scale=scale, bias=misc[:, 0:1])
                ACT(r.rstd[ri], r.rstd[ri], AF.Exp, [("rstd", ri)], [("rstd", ri)], scale=-0.5)
                return ri

            def norm_stat(r, xs, k):
                si = rr(r, "sq", 4)
                ACT(r.sq[si], r.xres[xs][:, k, :], AF.Square, [("xres", xs)], [("sq", si)])
                PE(ps(6), ONES, r.sq[si], k == 0, k == 7, [("sq", si), "cb"], [psk(6)])

            def rmsnorm(r, xs, gbase, pre=False):
                xk = ("xres", xs)
                if not pre:
                    for k in range(8):
                        norm_stat(r, xs, k)
                ri = rstd_from(r, 6, 1.0 / D, None)
                for k in range(8):
                    STT(r.xn[:, k, :], r.xres[xs][:, k, :], gvec[:, gbase + k:gbase + k + 1], r.rstd[ri],
                        ALU.mult, ALU.mult, [xk, ("rstd", ri), "gvec"], [("xn", k)])

            XN_ALL = [("xn", k) for k in range(8)]

            def ffn(r, l, i, xs, fuse_norm=False):
                xk = ("xres", xs)
                wkey = ("WIN", l, i)
                okey = ("WOUT", l, i)

                def load_wout(m):
                    sl = r.nwout % 3
                    r.nwout += 1
                    DMA("sp", r.wout[sl], WOUT[l][i][m].rearrange("p (j c) -> p j c", c=128), r.woutsem[sl],
                        [okey], [("wout", sl)])
                    return sl

                def load_win(j):
                    sl = r.nwin % 3
                    r.nwin += 1
                    wk = ("WIN", 0, 0, min(j // 6, 3)) if (l, i) == (0, 0) else wkey
                    DMA("sp", r.win[sl], WIN[l][i][j].rearrange("p (k c) -> p k c", c=256), r.winsem[sl],
                        [wk], [("win", sl)])
                    return sl

                wsl = {}
                wsl[0] = load_win(0)
                wsl[1] = load_win(1)
                osl = {}
                for j in range(NJ):
                    if j + 2 < NJ:
                        wsl[j + 2] = load_win(j + 2)
                    if j == NJ - 4:
                        osl[0] = load_wout(0)
                    if j == NJ - 2:
                        osl[1] = load_wout(1)
                    sl = wsl[j]
                    bg = j % 2
                    bu = 2 + j % 2
                    for half, b in ((0, bg), (1, bu)):
                        for k in range(8):
                            PE(ps(b), r.win[sl][:, k, half * 128:(half + 1) * 128], r.xn[:, k, :], k == 0, k == 7,
                               [("win", sl), ("xn", k)], [psk(b)])
                    si = rr(r, "sg", 2)
                    ACT(r.sg[si], ps(bg), AF.Silu, [psk(bg)], [("sg", si)])
                    TT_(r.abuf[:, j, :], r.sg[si], ps(bu), ALU.mult, [("sg", si), psk(bu)], [("abuf", j)])
                for m in range(8):
                    if m + 2 < 8:
                        osl[m + 2] = load_wout(m + 2)
                    sl = osl[m]
                    b = 4 + m % 2
                    for j in range(NJ):
                        PE(ps(b), r.wout[sl][:, j, :], r.abuf[:, j, :], j == 0, j == NJ - 1,
                           [("wout", sl), ("abuf", j)], [psk(b)])
                    if fuse_norm and m >= 1:
                        norm_stat(r, xs, m - 1)
                    STT(r.xres[xs][:, m, :], ps(b), 0.5, r.xres[xs][:, m, :], ALU.mult, ALU.add,
                        [psk(b), xk], [xk])
                if fuse_norm:
                    norm_stat(r, xs, 7)

            def load_x(r, src, t, xs, key):
                DMA("sp", r.xres[xs], src[:, t * TT:(t + 1) * TT].rearrange("(k p) t -> p k t", p=128), r.xsem[xs],
                    [(key, t)], [("xres", xs)])

            def store_x(r, dst, t, xs, key, sem):
                DMA("pool", dst[:, t * TT:(t + 1) * TT].rearrange("(k p) t -> p k t", p=128), r.xres[xs], sem[xs],
                    [("xres", xs)], [(key, t)])

            def run_items(items):
                n = len(items)
                ms = max(len(x) for x in items)
                for step in range(n + ms - 1):
                    for sidx in range(ms):
                        i = step - sidx
                        if 0 <= i < n and sidx < len(items[i]):
                            items[i][sidx]()

            af.top = afm
            ab.top = abm
            r = alloc_rowlocal(2)
            alloc_ffn(r)
            rope = af.alloc(2, 512)
            rope_sem = newsem()
            Fst = af.alloc(4, 128)
            Btot = af.alloc(4, 128)
            sigb = af.alloc(4, 512)
            edge = [af.alloc(4, 32) for _ in range(2)]
            wproj = [ab.alloc(8, 512) for _ in range(2)]
            wprojsem = [newsem() for _ in range(2)]
            QTs = ab.alloc(4, 512)
            KTs = ab.alloc(4, 512)
            Gs = ab.alloc(4, 512)
            Us = ab.alloc(4, 512)
            Vs = ab.alloc(4, 512)
            tmpb = [ab.alloc(512) for _ in range(2)]
            Kf = ab.alloc(4, 128)
            Kb = ab.alloc(4, 128)
            FSs = ab.alloc(16, 128)
            st_sem = {k: newsem() for k in ("qt", "kt", "g", "u", "v", "fs", "ex")}
            st_sem["x1"] = [newsem(), newsem()]
            ex_sem = [newsem(persist=True) for _ in range(2)]
            nproj = [0]

            def load_wproj(Wt, key, g):
                sl = nproj[0] % 2
                nproj[0] += 1
                DMA("sp", wproj[sl], Wt[g].rearrange("p (k c) -> p k c", c=512), wprojsem[sl], [key], [("wproj", sl)])
                return sl

            def proj_fm(sl, ft, b):
                for k in range(8):
                    PE(ps(b), wproj[sl][:, k, ft * 128:(ft + 1) * 128], r.xn[:, k, :], k == 0, k == 7,
                       [("wproj", sl), ("xn", k)], [psk(b)])

            def proj_tm(sl, c, b):
                for k in range(8):
                    PE(ps(b), r.xn[:, k, c * 128:(c + 1) * 128], wproj[sl][:, k, :], k == 0, k == 7,
                       [("wproj", sl), ("xn", k)], [psk(b)])

            def rope_apply(b, Rm, out_ap, outkey, ropekey):
                tb = rr(r, "tmp", 3)
                bi = r.cnt["sg"] % 2
                r.cnt["sg"] += 1
                ACT(tmpb[bi], ps(b), AF.Copy, [psk(b)], [("tmpb", bi)])
                PE(ps(7), Rm, tmpb[bi], True, True, [("tmpb", bi), "cb"], [psk(7)])
                t1 = r.tmp[tb]
                TT_(t1, ps(b), rope[:, 0, :], ALU.mult, [psk(b), ropekey], [("tmp", tb)])
                tb2 = rr(r, "tmp", 3)
                t2 = r.tmp[tb2]
                TT_(t2, ps(7), rope[:, 1, :], ALU.mult, [psk(7), ropekey], [("tmp", tb2)])
                TT_(out_ap, t1, t2, ALU.add, [("tmp", tb), ("tmp", tb2)], [outkey])

            prog.add("dve", lambda e: e.memset(Fst, 0.0), [], ["Fst"])
            prog.add("dve", lambda e: e.memset(Btot, 0.0), [], ["Btot"])
            prog.add("dve", lambda e: e.memset(edge[0], 0.0), [], [("edge", 0)])
            prog.add("dve", lambda e: e.memset(edge[1], 0.0), [], [("edge", 1)])
            load_x(r, xT, 0, 0, "xin")
            for t in range(NT):
                xs = t % 2
                if t + 1 < NT:
                    load_x(r, xT, t + 1, 1 - xs, "xin")
                rmsnorm(r, xs, 0)
                ffn(r, 0, 0, xs, fuse_norm=True)
                DMA("sp", rope, rope0_d[:, :, t * TT:(t + 1) * TT].rearrange("a p t -> p a t"), rope_sem, [], ["rope"])
                store_x(r, X1, t, xs, "X1", st_sem["x1"])
                rmsnorm(r, xs, 8, pre=True)
                GORD = [1, 2, 0, 3, 5, 4]
                gslot = {}
                gslot[0] = load_wproj(WAB, "WAB", GORD[0])
                gslot[1] = load_wproj(WAB, "WAB", GORD[1])
                pc = {"pb": 0, "rb": 0, "t1": 0}

                def mk_proj(gi, ft, first, tm=False):
                    st = {}

                    def s0():
                        if first and gi + 1 < 6 and gi >= 1:
                            gslot[gi + 1] = load_wproj(WAB, "WAB", GORD[gi + 1])
                        st["b"] = pc["pb"] % 3
                        pc["pb"] += 1
                        if tm:
                            proj_tm(gslot[gi], ft, st["b"])
                        else:
                            proj_fm(gslot[gi], ft, st["b"])
                    return st, s0

                def rope_item(gi, h, first, out_t, okey):
                    st, s0 = mk_proj(gi, h, first)

                    def s1():
                        b = st["b"]
                        bi = r.cnt["sg"] % 2
                        r.cnt["sg"] += 1
                        st["rb"] = 3 + pc["rb"] % 2
                        pc["rb"] += 1
                        st["t1"] = pc["t1"] % 2
                        pc["t1"] += 1
                        ACT(tmpb[bi], ps(b), AF.Copy, [psk(b)], [("tmpb", bi)])
                        PE(ps(st["rb"]), R0, tmpb[bi], True, True, [("tmpb", bi), "cb"], [psk(st["rb"])])
                        TT_(r.tmp[st["t1"]], ps(b), rope[:, 0, :], ALU.mult, [psk(b), "rope"], [("tmp", st["t1"])])

                    def s2():
                        TT_(r.tmp[2], ps(st["rb"]), rope[:, 1, :], ALU.mult, [psk(st["rb"]), "rope"], [("tmp", 2)])
                        TT_(out_t[:, h, :], r.tmp[st["t1"]], r.tmp[2], ALU.add, [("tmp", st["t1"]), ("tmp", 2)],
                            [(okey, h)])
                    return [s0, s1, s2]

                def simple_item(gi, ft, first, evac, tm=False):
                    st, s0 = mk_proj(gi, ft, first, tm)

                    def s1():
                        evac(st["b"])
                    return [s0, s1]

                def kvA(c):
                    def f():
                        for h in range(4):
                            PE(PS[5][:, h * 128:(h + 1) * 128], KTs[:, h, c * 128:(c + 1) * 128], IDENT, True, True,
                               [("KTs", h), "cb"], [psk(5)])
                        for h in range(4):
                            prog.add("act", (lambda h=h: lambda e: e.mul(Kf[:, h, :], PS[5][:, h * 128:(h + 1) * 128],
                                                                          kcol[:, h:h + 1]))(),
                                     [psk(5), "kcol"], [("Kf", h)])
                            TS(Kb[:, h, :], PS[5][:, h * 128:(h + 1) * 128], kcol[:, 4 + h:5 + h], None, ALU.mult, None,
                               [psk(5), "kcol"], [("Kb", h)])
                    return [f]

                def kvB(c):
                    cg_ = (t % 4) * 4 + c

                    def f():
                        for h in range(4):
                            PE(PS[6][:, h * 128:(h + 1) * 128], Kf[:, h, :], Vs[:, c, h * 128:(h + 1) * 128], True, True,
                               [("Kf", h), ("Vs", c)], [psk(6)])
                            PE(PS[7][:, h * 128:(h + 1) * 128], Kb[:, h, :], Vs[:, c, h * 128:(h + 1) * 128], True, True,
                               [("Kb", h), ("Vs", c)], [psk(7)])
                        ACT(FSs[:, c * 4:(c + 1) * 4, :], Fst, AF.Copy, ["Fst"], [("FSs", c)])
                        for h in range(4):
                            STT(Fst[:, h, :], Fst[:, h, :], cd[:, h:h + 1], PS[6][:, h * 128:(h + 1) * 128], ALU.mult,
                                ALU.add, ["Fst", psk(6), "cd"], ["Fst"])
                            STT(Btot[:, h, :], PS[7][:, h * 128:(h + 1) * 128], wbc[:, h * 32 + cg_:h * 32 + cg_ + 1],
                                Btot[:, h, :], ALU.mult, ALU.add, ["Btot", psk(7), "wbc"], ["Btot"])
                    return [f]

                items = []
                for h in range(4):
                    items.append(rope_item(0, h, h == 0, KTs, "KTs"))
                for c in range(4):
                    items.append(simple_item(1, c, c == 0,
                                             (lambda c=c: lambda b: ACT(Vs[:, c, :], ps(b), AF.Copy, [psk(b)],
                                                                        [("Vs", c)]))(), tm=True))
                for h in range(4):
                    items.append(kvA(h))
                    items.append(rope_item(2, h, h == 0, QTs, "QTs"))
                    items.append(kvB(h))
                for h in range(4):
                    items.append(simple_item(3, h, h == 0,
                                             (lambda h=h: lambda b: ACT(Gs[:, h, :], ps(b), AF.Silu, [psk(b)],
                                                                        [("Gs", h)]))()))
                for h in range(4):
                    items.append(simple_item(4, h, h == 0,
                                             (lambda h=h: lambda b: ACT(sigb[:, h, :], ps(b), AF.Sigmoid, [psk(b)],
                                                                        [("sigb", h)]))()))
                for h in range(4):
                    items.append(simple_item(5, h, h == 0,
                                             (lambda h=h: lambda b: TT_(Us[:, h, :], ps(b), sigb[:, h, :], ALU.mult,
                                                                        [psk(b), ("sigb", h)], [("Us", h)]))()))
                run_items(items)
                DMA("pool", KT0[:, t * TT:(t + 1) * TT].rearrange("(h p) t -> p h t", p=128), KTs, st_sem["kt"],
                    [("KTs", h) for h in range(4)], [("KT0", t)])
                DMA("pool", V0[t * TT:(t + 1) * TT, :].rearrange("(c p) e -> p c e", p=128), Vs, st_sem["v"],
                    [("Vs", c) for c in range(4)], [("V0", t)])
                DMA("pool", FS[t].rearrange("p (a e) -> p a e", e=128), FSs, st_sem["fs"],
                    [("FSs", c) for c in range(4)], [("FS", t)])
                DMA("pool", QT0[:, t * TT:(t + 1) * TT].rearrange("(h p) t -> p h t", p=128), QTs, st_sem["qt"],
                    [("QTs", h) for h in range(4)], [("QT0", t)])
                DMA("pool", G0[:, t * TT:(t + 1) * TT].rearrange("(h p) t -> p h t", p=128), Gs, st_sem["g"],
                    [("Gs", h) for h in range(4)], [("G0", t)])
                sg_ = t // 4
                if t % 4 == 0:
                    CP(edge[sg_][:, :, 0:15], Us[:, :, 0:15], [("Us", h) for h in range(4)], [("edge", sg_)])
                if t % 4 == 3:
                    CP(edge[sg_][:, :, 15:30], Us[:, :, 497:512], [("Us", h) for h in range(4)], [("edge", sg_)])
                ub = sg_ * 2080 + 16 + (t % 4) * TT
                DMA("pool", U0[:, ub:ub + TT].rearrange("(h p) t -> p h t", p=128), Us, st_sem["u"],
                    [("Us", h) for h in range(4)], [("U0", t)])
                if t % 4 == 3:
                    DMA("pool", EXin[sg_][:, 0:128].rearrange("(h p) e -> p h e", p=128), Fst, st_sem["ex"], ["Fst"],
                        [("EXin", sg_)])
                    DMA("pool", EXin[sg_][:, 128:256].rearrange("(h p) e -> p h e", p=128), Btot, st_sem["ex"],
                        ["Btot"], [("EXin", sg_)])
                    DMA("pool", EXin[sg_][:, 256:288].rearrange("(h p) e -> p h e", p=128), edge[sg_], st_sem["ex"],
                        [("edge", sg_)], [("EXin", sg_)])
                    prog.add("pool", (lambda sg_=sg_: lambda e: e.collective_compute(
                        "AllGather", ALU.bypass, replica_groups=[[0, 1, 2, 3], [4, 5, 6, 7]],
                        ins=[EXin[sg_].opt()], outs=[EXout[sg_].opt()]))(),
                        [("EXin", sg_)], [("EXout", sg_)], dma=(ex_sem[sg_], 1), persist=True)
                    if t == 3:
                        prog.add("dve", lambda e: e.memset(Fst, 0.0), [], ["Fst"])
                        prog.add("dve", lambda e: e.memset(Btot, 0.0), [], ["Btot"])
                if not small:
                    issue_casts(8 if t < NT - 1 else 0)
            stop_point("P1")
            prog.barrier()
            release_phase_sems()

            af.top = afm
            ab.top = abm
            r = alloc_rowlocal(2, with_sg=False)
            cv = af.alloc(4, 512)
            Bst = af.alloc(4, 128)
            tpos = af.alloc(512)
            coef = af.alloc(64)
            Fin = af.alloc(2, 4, 128)
            Bin = af.alloc(2, 4, 128)
            hal = af.alloc(2, 4, 32)
            qt = ab.alloc(4, 512)
            kt = ab.alloc(4, 512)
            vt = ab.alloc(4, 512)
            gt = ab.alloc(4, 512)
            ut = ab.alloc(4, 544)
            fs = ab.alloc(16, 128)
            Qf = ab.alloc(4, 512)
            Qb = ab.alloc(4, 512)
            Qg = ab.alloc(4, 512)
            gf = [ab.alloc(512) for _ in range(2)]
            Kb2 = ab.alloc(4, 128)
            Bbf = ab.alloc(4, 128)
            Finbf = ab.alloc(2, 4, 128)
            AT = [ab.alloc(4, 128) for _ in range(2)]
            cat = ab.alloc(8, 512)
            diag = ab.alloc(124, 128)
            wabo = ab.alloc(2, 8, 512)
            halo = ab.alloc(2, 4, 32)
            ld = {k: newsem() for k in ("qt", "kt", "vt", "gt", "ut", "fs", "tpos", "ex", "wabo", "halo")}
            st_x2 = [newsem(), newsem()]

            for sg_ in range(2):
                for d in range(2):
                    for h in range(4):
                        o = ((sg_ * 2 + d) * 4 + h) * 4
                        dcol = (sg_ * 2 + d) * 4
                        ACT(coef[:, o:o + 4], cmask[:, dcol:dcol + 4], AF.Exp, ["cmask", "lg"], ["coef"],
                            scale=lg[:, d * 4 + h:d * 4 + h + 1])
                        TT_(coef[:, o:o + 4], coef[:, o:o + 4], cmask[:, 16 + dcol:16 + dcol + 4], ALU.mult,
                            ["coef", "cmask"], ["coef"])
            stage = cv[:, :, 0:288]
            CVK = [("cv", ct_) for ct_ in range(4)]

            def p2a_setup(sg_):
                if True:
                    for i in range(4):
                        DMA("sp", stage, EXout[sg_][i * 512:(i + 1) * 512, 0:288].rearrange(
                            "(h p) e -> p h e", p=128), ld["ex"], [("EXout", sg_)], CVK)
                        for h in range(4):
                            cf = ((sg_ * 2 + 0) * 4 + h) * 4 + i
                            cb_ = ((sg_ * 2 + 1) * 4 + h) * 4 + i
                            if i == 0:
                                TS(Fin[:, sg_, h, :], stage[:, h, 0:128], coef[:, cf:cf + 1], None, ALU.mult, None,
                                   CVK + ["coef"], [("Fin", sg_)])
                                TS(Bin[:, sg_, h, :], stage[:, h, 128:256], coef[:, cb_:cb_ + 1], None, ALU.mult, None,
                                   CVK + ["coef"], [("Bin", sg_)])
                            else:
                                STT(Fin[:, sg_, h, :], stage[:, h, 0:128], coef[:, cf:cf + 1], Fin[:, sg_, h, :], ALU.mult,
                                    ALU.add, CVK + ["coef", ("Fin", sg_)], [("Fin", sg_)])
                                STT(Bin[:, sg_, h, :], stage[:, h, 128:256], coef[:, cb_:cb_ + 1], Bin[:, sg_, h, :],
                                    ALU.mult, ALU.add, CVK + ["coef", ("Bin", sg_)], [("Bin", sg_)])
                        ml = 32 + (sg_ * 2 + 0) * 4 + i
                        mr = 32 + (sg_ * 2 + 1) * 4 + i
                        if i == 0:
                            TS(hal[:, sg_, :, 0:16], stage[:, :, 271:287], cmask[:, ml:ml + 1], None, ALU.mult, None,
                               CVK + ["cmask"], [("hal", sg_)])
                            TS(hal[:, sg_, :, 16:32], stage[:, :, 256:272], cmask[:, mr:mr + 1], None, ALU.mult, None,
                               CVK + ["cmask"], [("hal", sg_)])
                        else:
                            STT(hal[:, sg_, :, 0:16], stage[:, :, 271:287], cmask[:, ml:ml + 1], hal[:, sg_, :, 0:16],
                                ALU.mult, ALU.add, CVK + ["cmask", ("hal", sg_)], [("hal", sg_)])
                            STT(hal[:, sg_, :, 16:32], stage[:, :, 256:272], cmask[:, mr:mr + 1], hal[:, sg_, :, 16:32],
                                ALU.mult, ALU.add, CVK + ["cmask", ("hal", sg_)], [("hal", sg_)])
                    CP(Finbf[:, sg_], Fin[:, sg_], [("Fin", sg_)], [("Finbf", sg_)])
                    CP(halo[:, sg_], hal[:, sg_], [("hal", sg_)], [("halo", sg_)])
                    sbb = sg_ * 2080
                    DMA("pool", U0[:, sbb + 1:sbb + 16].rearrange("(h p) t -> p h t", p=128), halo[:, sg_, :, 0:15],
                        ld["halo"], [("halo", sg_)], ["U0h"])
                    DMA("pool", U0[:, sbb + 16 + 2048:sbb + 16 + 2048 + 15].rearrange("(h p) t -> p h t", p=128),
                        halo[:, sg_, :, 16:31], ld["halo"], [("halo", sg_)], ["U0h"])
            DMA("sp", wabo, WABO.rearrange("g p (k c) -> p g k c", c=512), ld["wabo"], ["WABO"], ["wabo"])
            for ct in range(4):
                for w in range(31):
                    TS(diag[:, ct * 31 + w, :], identf, convw[:, ct * 31 + w:ct * 31 + w + 1], None, ALU.mult, None,
                       ["identf", "convw"], ["diag"])

            def p2a_loads(t):
                cs = slice(t * TT, (t + 1) * TT)
                DMA("sp", qt, QT0[:, cs].rearrange("(h p) t -> p h t", p=128), ld["qt"], [("QT0", t)], ["qt"])
                DMA("sp", kt, KT0[:, cs].rearrange("(h p) t -> p h t", p=128), ld["kt"], [("KT0", t)], ["kt"])
                DMA("sp", vt, V0[cs, :].rearrange("(c p) e -> p c e", p=128), ld["vt"], [("V0", t)], ["vt"])
                DMA("sp", fs, FS[t].rearrange("p (a e) -> p a e", e=128), ld["fs"], [("FS", t)], ["fs"])
                DMA("sp", tpos, tpos_d[:, cs], ld["tpos"], [], ["tpos"])

            def p2a_load_ut(t):
                ub = (t // 4) * 2080 + 1 + (t % 4) * TT
                DMA("sp", ut[:, :, 0:542], U0[:, ub:ub + 542].rearrange("(h p) t -> p h t", p=128),
                    ld["ut"], [("U0", tt_) for tt_ in range(NT)] + ["U0h"], ["ut"])

            def p2a_load_gt(t):
                cs = slice(t * TT, (t + 1) * TT)
                DMA("sp", gt, G0[:, cs].rearrange("(h p) t -> p h t", p=128), ld["gt"], [("G0", t)], ["gt"])

            ORDER = [3, 2, 1, 0, 7, 6, 5, 4]
            p2a_setup(0)
            load_x(r, X1, ORDER[0], 0, "X1")
            p2a_loads(ORDER[0])
            p2a_load_ut(ORDER[0])
            p2a_load_gt(ORDER[0])
            for oi_, t in enumerate(ORDER):
                xs = oi_ % 2
                nxt = ORDER[oi_ + 1] if oi_ + 1 < NT else None
                cs = slice(t * TT, (t + 1) * TT)
                if nxt is not None:
                    load_x(r, X1, nxt, 1 - xs, "X1")
                sg_ = t // 4
                if t % 4 == 3:
                    CP(Bst, Bin[:, sg_], [("Bin", sg_)], ["Bst"])
                    ACT(Bbf, Bst, AF.Copy, ["Bst"], ["Bbf"])
                for h in range(4):
                    gi = h % 2
                    ACT(gf[gi], tpos, AF.Exp, ["tpos", "lg"], [("gf", gi)], scale=lg[:, h:h + 1])
                    TT_(Qg[:, h, :], qt[:, h, :], gf[gi], ALU.mult, ["qt", ("gf", gi)], [("Qg", h)])
                    TT_(Qf[:, h, :], qt[:, h, :], rowf[:, h, :], ALU.mult, ["qt", "rowf"], [("Qf", h)])
                    TT_(Qb[:, h, :], qt[:, h, :], rowb[:, h, :], ALU.mult, ["qt", "rowb"], [("Qb", h)])
                for c in range(3, -1, -1):
                    cc = slice(c * 128, (c + 1) * 128)
                    sb = c % 2
                    ct = 3 - c
                    for h in range(4):
                        PE(PS[7][:, h * 128:(h + 1) * 128], kt[:, h, cc], IDENT, True, True, ["kt", "cb"], [psk(7)])
                    for h in range(4):
                        PE(PS[0][:, h * 128:(h + 1) * 128], kt[:, h, cc], qt[:, h, cc], True, True, ["kt", "qt"],
                           [psk(0)])
                    for h in range(4):
                        TS(Kb2[:, h, :], PS[7][:, h * 128:(h + 1) * 128], kcol[:, 4 + h:5 + h], None, ALU.mult, None,
                           [psk(7), "kcol"], [("Kb2", h)])
                    TT_(AT[sb], PS[0][:, :].rearrange("p (h i) -> p h i", i=128),
                        Dt[:, :, 0:128], ALU.mult, [psk(0), "Dt"], [("AT", sb)])
                    for w in range(0, 16):
                        PE(ps(1), diag[:, ct * 31 + w, :], ut[:, ct, w:w + 512], w == 0, w == 30, ["diag", "ut"], [psk(1)])
                    for h in range(4):
                        hs = slice(h * 128, (h + 1) * 128)
                        ob = 2 + h
                        PE(PS[ob][:, cc], vt[:, c, hs], AT[sb][:, h, :], True, False, ["vt", ("AT", sb)], [psk(ob)])
                        PE(PS[ob][:, cc], fs[:, c * 4 + h, :], Qf[:, h, cc], False, False, ["fs", ("Qf", h)], [psk(ob)])
                        PE(PS[ob][:, cc], Bbf[:, h, :], Qb[:, h, cc], False, False, ["Bbf", ("Qb", h)], [psk(ob)])
                        PE(PS[ob][:, cc], Finbf[:, sg_, h, :], Qg[:, h, cc], False, True, [("Finbf", sg_), ("Qg", h)], [psk(ob)])
                    for h in range(4):
                        hs = slice(h * 128, (h + 1) * 128)
                        PE(PS[6][:, hs], Kb2[:, h, :], vt[:, c, hs], True, True, [("Kb2", h), "vt"], [psk(6)])
                    for w in range(16, 31):
                        PE(ps(1), diag[:, ct * 31 + w, :], ut[:, ct, w:w + 512], w == 0, w == 30, ["diag", "ut"], [psk(1)])
                    for h in range(4):
                        hs = slice(h * 128, (h + 1) * 128)
                        STT(Bst[:, h, :], Bst[:, h, :], cd[:, 4 + h:5 + h], PS[6][:, hs], ALU.mult, ALU.add,
                            ["Bst", psk(6), "cd"], ["Bst"])
                    ACT(Bbf, Bst, AF.Copy, ["Bst"], ["Bbf"])
                    ACT(cv[:, ct, :], ps(1), AF.Identity, [psk(1), "gvec"], [("cv", ct)], bias=gvec[:, 52 + ct:53 + ct])
                if nxt is not None:
                    p2a_loads(nxt)
                    if nxt != 7:
                        p2a_load_ut(nxt)
                for h in range(4):
                    ob = 2 + h
                    si = rr(r, "sq", 4)
                    ACT(r.sq[si], ps(ob), AF.Square, [psk(ob)], [("sq", si)])
                    PE(ps(0), ONES, r.sq[si], True, True, [("sq", si), "cb"], [psk(0)])
                    ri = rstd_from(r, 0, 1.0 / 128, None)
                    tb = rr(r, "tmp", 3)
                    STT(r.tmp[tb], ps(ob), gvec[:, 48 + h:49 + h], r.rstd[ri], ALU.mult, ALU.mult,
                        [psk(ob), ("rstd", ri), "gvec"], [("tmp", tb)])
                    TT_(cat[:, h, :], r.tmp[tb], gt[:, h, :], ALU.mult, [("tmp", tb), "gt"], [("cat", h)])
                if nxt is not None:
                    p2a_load_gt(nxt)
                for ct in range(4):
                    si = rr(r, "sq", 4)
                    ACT(r.sq[si], cv[:, ct, :], AF.Square, [("cv", ct)], [("sq", si)])
                    PE(ps(1), ONES, r.sq[si], ct == 0, ct == 3, [("sq", si), "cb"], [psk(1)])
                ri = rstd_from(r, 1, 1.0 / 512, None)
                for ct in range(4):
                    tb = rr(r, "tmp", 3)
                    STT(r.tmp[tb], cv[:, ct, :], gvec[:, 56 + ct:57 + ct], r.rstd[ri], ALU.mult, ALU.mult,
                        [("cv", ct), ("rstd", ri), "gvec"], [("tmp", tb)])
                    ACT(cat[:, 4 + ct, :], r.tmp[tb], AF.Silu, [("tmp", tb)], [("cat", 4 + ct)])
                if nxt == 7:
                    p2a_setup(1)
                    p2a_load_ut(7)
                for m in range(8):
                    b = m % 2
                    for c8 in range(8):
                        PE(ps(b), wabo[:, m // 4, c8, (m % 4) * 128:(m % 4 + 1) * 128], cat[:, c8, :], c8 == 0, c8 == 7,
                           ["wabo", ("cat", c8)], [psk(b)])
                    TT_(r.xres[xs][:, m, :], ps(b), r.xres[xs][:, m, :], ALU.add, [psk(b), ("xres", xs)], [("xres", xs)])
                store_x(r, X2, t, xs, "X2", st_x2)
                if not small:
                    issue_casts(max(0, min(10, len(cast_q) - 56)))
            stop_point("P2a")
            prog.barrier()
            release_phase_sems()

            af.top = afm
            ab.top = abm
            r = alloc_rowlocal(2)
            alloc_ffn(r)
            rope = af.alloc(2, 512)
            rope_sem = newsem()
            wproj = [ab.alloc(8, 512) for _ in range(2)]
            wprojsem = [newsem() for _ in range(2)]
            Q1s = ab.alloc(8, 512)
            K1s = ab.alloc(8, 512)
            V1s = ab.alloc(4, 1024)
            tmpb = [ab.alloc(512) for _ in range(2)]
            st2 = {k: newsem() for k in ("q", "k", "v")}
            st2["x4"] = [newsem(), newsem()]
            nproj[0] = 0
            kvsem = [newsem(persist=True) for _ in range(16)]

            def gatherA(h):
                prog.add("pool", lambda e: e.collective_compute(
                    "AllGather", ALU.bypass, replica_groups=[[0, 1, 2, 3], [4, 5, 6, 7]],
                    ins=[KVinA[h].opt()], outs=[KVoutA[h].opt()]),
                    [("KVdone", 0, tl_, kv_) for tl_ in range(4) for kv_ in "kv"], [("KVoutA", h)],
                    dma=(kvsem[2 * h], 1), persist=True)

            def gatherB(h):
                prog.add("pool", lambda e: e.collective_compute(
                    "AllGather", ALU.bypass, replica_groups=[[0, 1], [2, 3], [4, 5], [6, 7]],
                    ins=[KVinB[h].opt()], outs=[KVoutB[h].opt()]),
                    [("KVdone", 1, tl_, kv_) for tl_ in range(4) for kv_ in "kv"], [("KVoutB", h)],
                    dma=(kvsem[2 * h + 1], 1), persist=True)

            def qk_post(b, gcol, out_ap, outkey):
                si = rr(r, "sq", 4)
                ACT(r.sq[si], ps(b), AF.Square, [psk(b)], [("sq", si)])
                PE(ps(6), BONES, r.sq[si], True, True, [("sq", si), "cb"], [psk(6)])
                ri = rstd_from(r, 6, 1.0 / 64, None)
                tb = rr(r, "tmp", 3)
                qn = r.tmp[tb]
                STT(qn, ps(b), gvec[:, gcol:gcol + 1], r.rstd[ri], ALU.mult, ALU.mult,
                    [psk(b), ("rstd", ri), "gvec"], [("tmp", tb)])
                bi = r.cnt["sg"] % 2
                r.cnt["sg"] += 1
                ACT(tmpb[bi], qn, AF.Copy, [("tmp", tb)], [("tmpb", bi)])
                PE(ps(7), R1, tmpb[bi], True, True, [("tmpb", bi), "cb"], [psk(7)])
                tb2 = rr(r, "tmp", 3)
                TT_(r.tmp[tb2], ps(7), rope[:, 1, :], ALU.mult, [psk(7), "rope"], [("tmp", tb2)])
                TT_(qn, qn, rope[:, 0, :], ALU.mult, [("tmp", tb), "rope"], [("tmp", tb)])
                TT_(out_ap, qn, r.tmp[tb2], ALU.add, [("tmp", tb), ("tmp", tb2)], [outkey])

            load_x(r, X2, 0, 0, "X2")
            for t in range(NT):
                xs = t % 2
                cs = slice(t * TT, (t + 1) * TT)
                if t + 1 < NT:
                    load_x(r, X2, t + 1, 1 - xs, "X2")
                rmsnorm(r, xs, 16)
                ffn(r, 0, 1, xs, fuse_norm=True)
                rmsnorm(r, xs, 24, pre=True)
                ffn(r, 1, 0, xs, fuse_norm=True)
                DMA("sp", rope, rope1_d[:, :, cs].rearrange("a p t -> p a t"), rope_sem, [], ["rope"])
                store_x(r, X4, t, xs, "X4", st2["x4"])
                rmsnorm(r, xs, 32, pre=True)
                gslot = {}
                gslot[0] = load_wproj(WC, "WC", 0)
                gslot[1] = load_wproj(WC, "WC", 1)
                pc = {"pb": 0, "sb": 0, "rb": 0, "rs": 0, "qn": 0}

                def mk_proj2(g, ft, first, tm=False):
                    st = {}

                    def s0():
                        if first and g + 1 < 6 and g >= 1:
                            gslot[g + 1] = load_wproj(WC, "WC", g + 1)
                        st["b"] = pc["pb"] % 4
                        pc["pb"] += 1
                        if tm:
                            proj_tm(gslot[g], ft, st["b"])
                        else:
                            proj_fm(gslot[g], ft, st["b"])
                    return st, s0

                def qk_item(g, ft, first, gcol, out_t, okey):
                    st, s0 = mk_proj2(g, ft, first)
                    hh = (g % 2) * 4 + ft

                    def s1():
                        b = st["b"]
                        si = rr(r, "sq", 4)
                        st["sb"] = 4 + pc["sb"] % 2
                        pc["sb"] += 1
                        ACT(r.sq[si], ps(b), AF.Square, [psk(b)], [("sq", si)])
                        PE(ps(st["sb"]), BONES, r.sq[si], True, True, [("sq", si), "cb"], [psk(st["sb"])])

                    def s2():
                        ri = pc["rs"] % 2
                        pc["rs"] += 1
                        st["ri"] = ri
                        sb = st["sb"]
                        ACT(r.rstd[ri], ps(sb), AF.Ln, [psk(sb), "misc0"], [("rstd", ri)], scale=1.0 / 64,
                            bias=misc[:, 0:1])
                        ACT(r.rstd[ri], r.rstd[ri], AF.Exp, [("rstd", ri)], [("rstd", ri)], scale=-0.5)

                    def s3():
                        b = st["b"]
                        ri = st["ri"]
                        qi = pc["qn"] % 2
                        pc["qn"] += 1
                        st["qi"] = qi
                        st["rb"] = 6 + pc["rb"] % 2
                        pc["rb"] += 1
                        qn = r.tmp[qi]
                        STT(qn, ps(b), gvec[:, gcol:gcol + 1], r.rstd[ri], ALU.mult, ALU.mult,
                            [psk(b), ("rstd", ri), "gvec"], [("tmp", qi)])
                        bi = r.cnt["sg"] % 2
                        r.cnt["sg"] += 1
                        ACT(tmpb[bi], qn, AF.Copy, [("tmp", qi)], [("tmpb", bi)])
                        PE(ps(st["rb"]), R1, tmpb[bi], True, True, [("tmpb", bi), "cb"], [psk(st["rb"])])
                        TT_(qn, qn, rope[:, 0, :], ALU.mult, [("tmp", qi), "rope"], [("tmp", qi)])

                    def s4():
                        qi = st["qi"]
                        TT_(r.tmp[2], ps(st["rb"]), rope[:, 1, :], ALU.mult, [psk(st["rb"]), "rope"], [("tmp", 2)])
                        TT_(out_t[:, hh, :], r.tmp[qi], r.tmp[2], ALU.add, [("tmp", qi), ("tmp", 2)], [(okey, hh)])
                    return [s0, s1, s2, s3, s4]

                def v_item(g, c, first):
                    st, s0 = mk_proj2(g, c, first, tm=True)

                    def s1():
                        b = st["b"]
                        ACT(V1s[:, c, (g - 4) * 512:(g - 3) * 512], ps(b), AF.Copy, [psk(b)], [("V1s", c)])
                    return [s0, s1]

                items = []
                for g in range(6):
                    for ft in range(4):
                        if g < 2:
                            items.append(qk_item(g, ft, ft == 0, 60, Q1s, "Q1s"))
                        elif g < 4:
                            items.append(qk_item(g, ft, ft == 0, 61, K1s, "K1s"))
                        else:
                            items.append(v_item(g, ft, ft == 0))
                run_items(items)
                DMA("pool", QT1[:, :, cs].rearrange("h p t -> p h t"), Q1s, st2["q"],
                    [("Q1s", h) for h in range(8)], [("QT1", t)])
                KVs = KVinA if t < 4 else KVinB
                tl = t % 4
                for h in range(8):
                    DMA("pool", KVs[h][0:128, tl * TT:(tl + 1) * TT], K1s[:, h, :], st2["k"],
                        [("K1s", h_) for h_ in range(8)], [("KVdone", t // 4, tl, "k")] if h == 7 else [])
                for c in range(4):
                    for h in range(8):
                        dst = KVs[h][128:256, :].rearrange("r (a e) -> (r a) e", e=128)[
                            tl * TT + c * 128:tl * TT + (c + 1) * 128, :]
                        DMA("pool", dst, V1s[:, c, h * 128:(h + 1) * 128], st2["v"],
                            [("V1s", c_) for c_ in range(4)],
                            [("KVdone", t // 4, tl, "v")] if (h == 7 and c == 3) else [])
                if 3 <= t <= 6:
                    for h in (2 * (t - 3), 2 * (t - 3) + 1):
                        gatherA(h)
                if t == NT - 1:
                    gatherB(0)
                    gatherB(1)
            stop_point("P2b")
            prog.barrier()
            release_phase_sems()

            af.top = afm
            ab.top = abm
            ones_f = af.alloc(128)
            acc = [af.alloc(1024) for _ in range(2)]
            rc = af.alloc(1024)
            t01 = af.alloc(1024)
            obuf = [af.alloc(512) for _ in range(2)]
            rstd3 = af.alloc(512)
            KTa = [ab.alloc(6, 2048) for _ in range(2)]
            Va = [ab.alloc(6, 16, 128) for _ in range(2)]
            Qa = [ab.alloc(T) for _ in range(2)]
            PT = [ab.alloc(1024) for _ in range(3)]
            sq3 = [ab.alloc(512) for _ in range(2)]
            ONs = [ab.alloc(512) for _ in range(2)]
            ldk = [[newsem() for _ in range(2)] for _ in range(2)]
            ldv = [[newsem() for _ in range(2)] for _ in range(2)]
            ldq = [newsem() for _ in range(2)]
            st_on = [newsem() for _ in range(2)]
            prog.add("dve", lambda e: e.memset(ones_f, 1.0), [], ["ones_f"])
            SPAIR = [(0, 1), (2, 3)]
            PVPAIR = [(4, 5), (6, 7)]

            def pair_ap(p):
                return PSall[:, p[0] * 512:p[0] * 512 + 1024]

            def pair_keys(p):
                return [psk(p[0]), psk(p[1])]

            state = {"ring": 0, "pt": 0, "on": 0, "ptp": 0}
            steps = []
            pending = None
            for h in range(8):
                for qc in range(8):
                    for kidx in range(64 if qc < 4 else 32):
                        steps.append(("qk", h, qc, kidx))
                        if kidx == 1 and pending is not None:
                            steps.append(pending)
                            pending = None
                    pending = ("sum", h, qc, 0)
            steps.append(pending)

            def load_head(h):
                hs = h % 2
                DMA("sp", Qa[hs], QT1[h], ldq[hs], [("QT1", t) for t in range(NT)], [("Qa", hs)])
                for blk in range(6):
                    part = 0 if blk < 4 else 1
                    src = KVoutA[h] if blk < 4 else KVoutB[h]
                    key = ("KVoutA", h) if blk < 4 else ("KVoutB", h)
                    b0 = (blk if blk < 4 else blk - 4) * 256
                    DMA("sp", KTa[hs][:, blk, :], src[b0:b0 + 128, :], ldk[hs][part], [key], [("KTa", hs, part)])
                    DMA("sp", Va[hs][:, blk, :, :],
                        src[b0 + 128:b0 + 256, :].rearrange("r (a e) -> (r a) e", e=128).rearrange(
                            "(kt p) e -> p kt e", p=128),
                        ldv[hs][part], [key], [("Va", hs, part)])

            def produce(st):
                kind, h, qc, kidx = st
                hs = h % 2
                slot = state["ring"] % 2
                state["ring"] += 1
                sp = SPAIR[slot]
                if kind == "qk":
                    blk, ktile = divmod(kidx + (0 if qc < 4 else 64), 16)
                    ks = slice(ktile * 128, (ktile + 1) * 128)
                    qs = slice(qc * 512, (qc + 1) * 512)
                    pi = state["ptp"] % 3
                    state["ptp"] += 1
                    for comp in range(2):
                        dsl = slice(comp * 64, (comp + 1) * 64)
                        PE(ps(sp[comp]), KTa[hs][dsl, blk, ks], Qa[hs][dsl, qs], True, True,
                           [("KTa", hs, 0 if qc < 4 else 1), ("Qa", hs)],
                           [psk(sp[comp])] + ([("PT", pi)] if comp == 0 else []))
                else:
                    par = qc % 2
                    for comp in range(2):
                        PE(ps(sp[comp]), ones_f, acc[par][:, comp * 512:(comp + 1) * 512], True, True,
                           [("acc", par), "ones_f"], [psk(sp[comp])])
                return slot

            deferred = []

            def part2(h, qc):
                par = qc % 2
                pv = PVPAIR[par]
                qs = slice(qc * 512, (qc + 1) * 512)
                PE(ps(pv[0]), ONES, sq3[par], True, True, [("sq3", par), "cb"], [psk(pv[0])])
                ACT(rstd3, ps(pv[0]), AF.Ln, [psk(pv[0]), "misc0"], ["rstd3"], scale=1.0 / 128, bias=misc[:, 0:1])
                ACT(rstd3, rstd3, AF.Exp, ["rstd3"], ["rstd3"], scale=-0.5)
                oi = state["on"] % 2
                state["on"] += 1
                STT(ONs[oi], obuf[par], misc[:, 2:3], rstd3, ALU.mult, ALU.mult, [("obuf", par), "rstd3", "m2"],
                    [("ONs", oi)])
                DMA("pool", ON[h][:, qs], ONs[oi], st_on[oi], [("ONs", oi)], [("ON", qc)])

            def consume(st, slot, idx):
                kind, h, qc, kidx = st
                hs = h % 2
                sp = SPAIR[slot]
                par = qc % 2
                pv = PVPAIR[par]
                if kind == "qk":
                    blk, ktile = divmod(kidx + (0 if qc < 4 else 64), 16)
                    nk = 64 if qc < 4 else 32
                    pi = state["pt"] % 3
                    state["pt"] += 1
                    ACT(PT[pi], pair_ap(sp), AF.Exp, pair_keys(sp) + ["m3", "m4"], [("PT", pi)], scale=0.125,
                        bias=misc[:, 3:4])
                    if kidx == 0:
                        CP(acc[par], PT[pi], [("PT", pi)], [("acc", par)])
                    else:
                        TT_(acc[par], acc[par], PT[pi], ALU.add, [("PT", pi), ("acc", par)], [("acc", par)])
                    for comp in range(2):
                        half = PT[pi][:, comp * 512:(comp + 1) * 512]
                        PE(ps(pv[comp]), Va[hs][:, blk, ktile, :], half, kidx == 0, kidx == nk - 1,
                           [("Va", hs, 0 if qc < 4 else 1), ("PT", pi)], [psk(pv[comp])])
                else:
                    ACT(rc, pair_ap(sp), AF.Ln, pair_keys(sp), ["rc"])
                    ACT(rc, rc, AF.Exp, ["rc"], ["rc"], scale=-1.0)
                    TT_(t01, pair_ap(pv), rc, ALU.mult, pair_keys(pv) + ["rc"], ["t01"])
                    STT(obuf[par], t01[:, 512:1024], misc[:, 1:2], t01[:, 0:512], ALU.mult, ALU.add, ["t01", "m1"],
                        [("obuf", par)])
                    ACT(sq3[par], obuf[par], AF.Square, [("obuf", par)], [("sq3", par)])
                    deferred.append((idx + 3, h, qc))

            load_head(0)
            load_head(1)
            if not small:
                issue_casts(len(cast_q))
            nst = len(steps)
            slots = {}
            slots[0] = produce(steps[0])
            for i in range(nst):
                st = steps[i]
                if st[0] == "qk" and st[2] == 0 and st[3] == 0:
                    if st[1] + 2 < 8:
                        gatherB(st[1] + 2)
                    if st[1] >= 1 and st[1] + 1 < 8:
                        load_head(st[1] + 1)
                nxt = steps[i + 1] if i + 1 < nst else None
                late = nxt is not None and nxt[0] == "sum" and st[0] == "qk" and st[1:3] == nxt[1:3]
                if nxt is not None and not late:
                    slots[i + 1] = produce(nxt)
                consume(st, slots[i], i)
                if nxt is not None and late:
                    slots[i + 1] = produce(nxt)
                while deferred and deferred[0][0] <= i:
                    _, dh, dq = deferred.pop(0)
                    part2(dh, dq)
            while deferred:
                _, dh, dq = deferred.pop(0)
                part2(dh, dq)
            stop_point("P3")
            prog.barrier()
            release_phase_sems()

            af.top = afm
            ab.top = abm
            r = alloc_rowlocal(2)
            alloc_ffn(r)
            wco = ab.alloc(2, 8, 512)
            ont = [ab.alloc(8, 512) for _ in range(2)]
            ldo = [newsem() for _ in range(2)]
            ldw = newsem()
            st_y = [newsem(), newsem()]
            DMA("sp", wco, WCO.rearrange("g p (k c) -> p g k c", c=512), ldw, ["WCO"], ["wco"])
            load_x(r, X4, 0, 0, "X4")
            DMA("sp", ont[0], ON[:, :, 0:TT].rearrange("h p t -> p h t"), ldo[0], [("ON", qc) for qc in range(8)],
                [("ont", 0)])
            for t in range(NT):
                xs = t % 2
                if t + 1 < NT:
                    load_x(r, X4, t + 1, 1 - xs, "X4")
                    DMA("sp", ont[1 - xs], ON[:, :, (t + 1) * TT:(t + 2) * TT].rearrange("h p t -> p h t"), ldo[1 - xs],
                        [("ON", qc) for qc in range(8)], [("ont", 1 - xs)])
                for m in range(8):
                    b = m % 2
                    for c8 in range(8):
                        PE(ps(b), wco[:, m // 4, c8, (m % 4) * 128:(m % 4 + 1) * 128], ont[xs][:, c8, :], c8 == 0, c8 == 7,
                           ["wco", ("ont", xs)], [psk(b)])
                    if m >= 1:
                        norm_stat(r, xs, m - 1)
                    TT_(r.xres[xs][:, m, :], ps(b), r.xres[xs][:, m, :], ALU.add, [psk(b), ("xres", xs)], [("xres", xs)])
                norm_stat(r, xs, 7)
                rmsnorm(r, xs, 40, pre=True)
                ffn(r, 1, 1, xs)
                store_x(r, yT, t, xs, "yT", st_y)


        except _Stop:
            pass
        prog.analyze()
        with nc.Block() as block:
            @block.tensor
            def _(e):
                prog.emit("pe", e, esems)

            @block.scalar
            def _(e):
                prog.emit("act", e, esems)

            @block.vector
            def _(e):
                prog.emit("dve", e, esems)

            @block.gpsimd
            def _(e):
                prog.emit("pool", e, esems)

            @block.sync
            def _(e):
                prog.emit("sp", e, esems)
    return nc


def _rope_tables(half, rep, pos):
    inv = (np.float32(10000.0) ** (-np.arange(half, dtype=np.float32) / np.float32(half))).astype(np.float32)
    pos = np.asarray(pos, dtype=np.float32)
    ang = (pos[:, None] * inv[None, :]).astype(np.float32)
    cos = np.cos(ang).astype(np.float32).T
    sin = np.sin(ang).astype(np.float32).T
    C = np.concatenate([cos, cos], axis=0)
    S = np.concatenate([-sin, sin], axis=0)
    C = np.tile(C, (rep, 1))
    S = np.tile(S, (rep, 1))
    return np.ascontiguousarray(np.stack([C, S], axis=0))


def _consts():
    p = np.arange(128)
    ident = np.eye(128, dtype=np.float32)
    ones = np.ones((128, 128), np.float32)
    bones = np.zeros((128, 128), np.float32)
    bones[:64, :64] = 1
    bones[64:, 64:] = 1
    R0 = np.zeros((128, 128), np.float32)
    R0[(p + 64) % 128, p] = 1
    R1 = np.zeros((128, 128), np.float32)
    src = np.where((p % 64) < 32, p + 32, p - 32)
    R1[src, p] = 1
    j = p[:, None].astype(np.float32)
    i = np.arange(128)[None, :].astype(np.float32)
    M1 = np.maximum(i - j, 0)
    M2 = np.maximum(j - i, 0)
    MLE = (j <= i).astype(np.float32)
    MGT = (j > i).astype(np.float32)
    IO1 = np.broadcast_to(i + 1, (128, 128))
    IOB = np.broadcast_to(128 - i, (128, 128))
    t4 = lambda a: np.tile(a, (1, 4))
    colA = (127 - p).astype(np.float32)[:, None]
    colB = p.astype(np.float32)[:, None]
    ch = np.broadcast_to((128.0 * np.arange(32))[None, :], (128, 32)).astype(np.float32)
    cst = np.concatenate([ident, ones, bones, R0, R1, t4(M1), t4(M2), t4(MLE), t4(MGT), t4(IO1), t4(IOB),
                          colA, colB, ch], axis=1).astype(np.float32)
    tpos = np.broadcast_to(((np.arange(T) % 2048).astype(np.float32) + 1.0)[None, :], (128, T))
    return np.ascontiguousarray(cst), np.ascontiguousarray(tpos)


_NC_CACHE = {}


def _host_inputs(inputs):
    f = lambda a: np.ascontiguousarray(np.asarray(a, dtype=np.float32))
    xp = f(inputs["x_prompt"])
    xsm = f(inputs["x_sample"])
    norm_g = f(inputs["norm_g"])
    gvec = np.zeros((128, 64), np.float32)
    for l in range(2):
        for i in range(3):
            gvec[:, (l * 3 + i) * 8:(l * 3 + i) * 8 + 8] = norm_g[l, i].reshape(8, 128).T
    gvec[:, 48:52] = f(inputs["ab_ret_norm_g"])[0].reshape(4, 128).T
    gvec[:, 52:56] = f(inputs["ab_conv_b"])[0].reshape(4, 128).T
    gvec[:, 56:60] = f(inputs["ab_conv_norm_g"])[0].reshape(4, 128).T
    gq = f(inputs["c_q_norm_g"])[0]
    gk = f(inputs["c_k_norm_g"])[0]
    gvec[:, 60] = np.tile(gq, 2)
    gvec[:, 61] = np.tile(gk, 2)
    gvec[:, 62] = f(inputs["c_subln_g"])[0]
    cw = f(inputs["ab_conv_w"])[0]
    convw = np.ascontiguousarray(cw.T.reshape(4, 128, 31).transpose(1, 0, 2).reshape(128, 124))
    decay = np.ascontiguousarray(np.broadcast_to(f(inputs["ab_decay"])[0].reshape(1, 8), (128, 8)))
    lam = np.ascontiguousarray(np.broadcast_to(f(inputs["c_lambda"])[0].reshape(1, 256), (128, 256)))
    gqk = np.ascontiguousarray(np.broadcast_to(np.concatenate([gq, gk])[None, :], (128, 128)))
    cst, tpos = _consts()
    shared = {
        "ffn_w_in": f(inputs["ffn_w_in"]), "ffn_w_out": f(inputs["ffn_w_out"]),
        "ab_w_in": f(inputs["ab_w_in"])[0], "ab_w_out": f(inputs["ab_w_out"])[0],
        "c_w_in": f(inputs["c_w_in"])[0], "c_w_out": f(inputs["c_w_out"])[0],
        "gvec": gvec, "convw": convw, "decay": decay, "lam": lam, "gqk": gqk, "cst": cst, "tpos": tpos,
    }
    in_maps = []
    SEG = 2048
    for c in range(8):
        g, j = divmod(c, 4)
        hb = j % 2
        pidx = 2 * g + j // 2
        segA = xsm[g, j * SEG:(j + 1) * SEG]
        segB = xp[pidx, hb * SEG:(hb + 1) * SEG]
        xt = np.concatenate([segA, segB], axis=0).T
        pos = np.concatenate([j * SEG + np.arange(SEG), hb * SEG + np.arange(SEG)]).astype(np.float32)
        tab = np.zeros(64, np.float32)
        for i in range(4):
            if i < j:
                tab[0 + i] = SEG * (j - 1 - i)
                tab[16 + 0 + i] = 1.0
            if i > j:
                tab[4 + i] = SEG * (i - j - 1)
                tab[16 + 4 + i] = 1.0
            if hb == 1 and i == j - 1:
                tab[16 + 8 + i] = 1.0
            if hb == 0 and i == j + 1:
                tab[16 + 12 + i] = 1.0
            if i == j - 1:
                tab[32 + 0 + i] = 1.0
            if i == j + 1:
                tab[32 + 4 + i] = 1.0
            if hb == 1 and i == j - 1:
                tab[32 + 8 + i] = 1.0
            if hb == 0 and i == j + 1:
                tab[32 + 12 + i] = 1.0
        m = dict(shared)
        m["xT"] = np.ascontiguousarray(xt)
        m["rope0"] = _rope_tables(64, 1, pos)
        m["rope1"] = _rope_tables(32, 2, pos)
        m["cmask"] = np.ascontiguousarray(np.broadcast_to(tab[None, :], (128, 64)))
        in_maps.append(m)
    return in_maps


def kernel(**inputs):
    in_maps = _host_inputs(inputs)
    if "nc" not in _NC_CACHE:
        _NC_CACHE["nc"] = build_program()
    nc = _NC_CACHE["nc"]
    res = run_bass_kernel_spmd(nc, in_maps, core_ids=list(range(8)))
    outs = [np.asarray(res.results[c]["yT"], dtype=np.float32) for c in range(8)]
    SEG = 2048
    y_prompt = np.zeros((4, 4096, D), np.float32)
    y_sample = np.zeros((2, 8192, D), np.float32)
    for c in range(8):
        g, j = divmod(c, 4)
        hb = j % 2
        pidx = 2 * g + j // 2
        o = outs[c].T
        y_sample[g, j * SEG:(j + 1) * SEG] = o[0:SEG]
        y_prompt[pidx, hb * SEG:(hb + 1) * SEG] = o[SEG:2 * SEG]
    return (y_prompt, y_sample)
```

```python
import math
from contextlib import ExitStack

import numpy as np
import concourse.bass as bass
import concourse.mybir as mybir
from concourse.bass_utils import run_bass_kernel_spmd

F32 = mybir.dt.float32
BF16 = mybir.dt.bfloat16
ALU = mybir.AluOpType
AF = mybir.ActivationFunctionType
AX = mybir.AxisListType

D = 1024
T = 4096
TT = 512
NT = T // TT
DFF = 2816
NJ = DFF // 128
EPS = 1e-6
LAMBDA_INIT = 0.8 - 0.6 * math.exp(-0.3 * 1)
NF = 17 * 1024
NB = 69 * 1024
NEG = -30000.0


class Op:
    __slots__ = ("eng", "fn", "reads", "writes", "dma", "deps", "sig", "val", "eidx", "idx", "persist")


class Prog:
    def __init__(self):
        self.ops = []

    def add(self, eng, fn, reads=(), writes=(), dma=None, persist=False):
        op = Op()
        op.persist = persist
        op.eng = eng
        op.fn = fn
        reads = tuple(reads)
        op.reads = reads
        op.writes = tuple(writes) + tuple(r for r in reads if isinstance(r, tuple) and r[0] == "ps")
        op.dma = dma
        op.deps = set()
        op.sig = False
        op.val = 0
        self.ops.append(op)
        return op

    def barrier(self):
        self.ops.append(None)

    def analyze(self):
        last_writer = {}
        pers_writer = {}
        readers = {}
        last_on_eng = {}
        last_async = {}
        pending = {}
        ecount = {}
        real = []
        for op in self.ops:
            if op is None:
                bd = set(last_on_eng.values()) | set(last_async.values())
                for e in ("pe", "act", "dve", "pool", "sp"):
                    pending[e] = set(bd) | pending.get(e, set())
                last_writer = {}
                readers = {}
                continue
            op.idx = len(real)
            real.append(op)
            op.eidx = ecount.get(op.eng, 0)
            ecount[op.eng] = op.eidx + 1
            deps = set()
            for r in op.reads:
                w = last_writer.get(r)
                if w is not None:
                    deps.add(w)
                w = pers_writer.get(r)
                if w is not None:
                    deps.add(w)
            if op.persist:
                for w_ in op.writes:
                    pers_writer[w_] = op
                op.deps = deps
                continue
            for w_ in op.writes:
                w = last_writer.get(w_)
                if w is not None:
                    deps.add(w)
                for rd in readers.get(w_, ()):
                    deps.add(rd)
            if op.eng in pending:
                deps |= pending.pop(op.eng)
            deps.discard(op)
            op.deps = deps
            for r in op.reads:
                readers.setdefault(r, []).append(op)
            for w_ in op.writes:
                last_writer[w_] = op
                readers[w_] = []
            if op.dma is not None:
                last_async[id(op.dma[0])] = op
            else:
                last_on_eng[op.eng] = op
        self.real = real
        for op in real:
            keep = set()
            for d in op.deps:
                if d.dma is None and d.eng == op.eng:
                    if op.dma is None or True:
                        if op.eng == "pe" or op.eng == "sp":
                            continue
                        if op.eidx - d.eidx > 2:
                            continue
                keep.add(d)
            op.deps = keep
            for d in keep:
                if d.dma is None:
                    d.sig = True
        cnt = {}
        acnt = {}
        for op in real:
            if op.dma is not None:
                k = id(op.dma[0])
                acnt[k] = acnt.get(k, 0) + op.dma[1]
                op.val = acnt[k]
            elif op.sig:
                cnt[op.eng] = cnt.get(op.eng, 0) + 1
                op.val = cnt[op.eng]
        self.final_async = {}
        for op in real:
            if op.dma is not None:
                self.final_async[id(op.dma[0])] = (op.dma[0], op.val, op.eng)

    def emit(self, engname, e, esems):
        waited = {}
        for op in self.real:
            if op.eng != engname:
                continue
            need = {}
            for d in op.deps:
                if d.dma is not None:
                    s = d.dma[0]
                else:
                    s = esems[d.eng]
                k = id(s)
                if k not in need or need[k][1] < d.val:
                    need[k] = (s, d.val)
            for k, (s, v) in need.items():
                if waited.get(k, 0) < v:
                    e.wait_ge(s, v)
                    waited[k] = v
            ins = op.fn(e)
            if op.dma is not None:
                ins.then_inc(op.dma[0], op.dma[1])
            elif op.sig:
                ins.then_inc(esems[op.eng], 1)
        for k, (s, v, eng) in self.final_async.items():
            if eng == engname and waited.get(k, 0) < v:
                e.wait_ge(s, v)


class _Stop(Exception):
    pass


class Arena:
    def __init__(self, t, n):
        self.t = t
        self.n = n
        self.top = 0

    def alloc(self, *shape):
        size = int(np.prod(shape))
        off = self.top
        self.top += size
        assert self.top <= self.n, ("arena overflow", self.top, self.n)
        ap = self.t[:, off:off + size]
        if len(shape) == 2:
            ap = ap.rearrange("p (a b) -> p a b", b=shape[1])
        elif len(shape) == 3:
            ap = ap.rearrange("p (a b c) -> p a b c", b=shape[1], c=shape[2])
        return ap


def build_program(dbg=None):
    nc = bass.Bass("TRN2", target_bir_lowering=False)

    def din(name, shape, dt=F32):
        return nc.dram_tensor(name, list(shape), dt, kind="ExternalInput").ap()

    def dscr(name, shape, dt=BF16):
        kind = "ExternalOutput" if (dbg and name in dbg) else None
        if kind:
            return nc.dram_tensor(name, list(shape), dt, kind=kind).ap()
        return nc.dram_tensor(name, list(shape), dt).ap()

    small = bool((dbg or {}).get("small"))
    xT = din("xT", [D, T])
    if not small:
        w_ffn_in = din("ffn_w_in", [2, 2, D, 2 * DFF])
        w_ffn_out = din("ffn_w_out", [2, 2, DFF, D])
        w_ab_in = din("ab_w_in", [D, 3072])
        w_ab_out = din("ab_w_out", [D, D])
        w_c_in = din("c_w_in", [D, 3072])
        w_c_out = din("c_w_out", [D, D])
    gvec_d = din("gvec", [128, 64])
    convw_d = din("convw", [128, 124])
    decay_d = din("decay", [128, 8])
    lam_d = din("lam", [128, 256])
    gqk_d = din("gqk", [128, 128])
    rope0_d = din("rope0", [2, 128, T])
    rope1_d = din("rope1", [2, 128, T])
    cmask_d = din("cmask", [128, 64])
    cst_d = din("cst", [128, 5 * 128 + 6 * 512 + 2 + 32])
    tpos_d = din("tpos", [128, T])
    yT = nc.dram_tensor("yT", [D, T], F32, kind="ExternalOutput").ap()

    WIN = [[dscr(f"WIN{l}{i}", [NJ, 128, 8 * 256]) for i in range(2)] for l in range(2)]
    WOUT = [[dscr(f"WOUT{l}{i}", [8, 128, NJ * 128]) for i in range(2)] for l in range(2)]
    WAB = dscr("WAB", [6, 128, 8 * 512])
    WABO = dscr("WABO", [2, 128, 8 * 512])
    WC = dscr("WC", [6, 128, 8 * 512])
    WCO = dscr("WCO", [2, 128, 8 * 512])
    X1 = dscr("X1", [D, T], F32)
    X2 = dscr("X2", [D, T], F32)
    X4 = dscr("X4", [D, T], F32)
    QT0 = dscr("QT0", [512, T])
    KT0 = dscr("KT0", [512, T])
    V0 = dscr("V0", [T, 512])
    G0 = dscr("G0", [512, T])
    U0 = dscr("U0", [512, 2 * 2080])
    FS = dscr("FS", [NT, 128, 16 * 128])
    EXin = [dscr(f"EXin{i}", [512, 288], F32) for i in range(2)]
    EXout = [dscr(f"EXout{i}", [2048, 288], F32) for i in range(2)]
    QT1 = dscr("QT1", [8, 128, T])
    KVinA = [dscr(f"KVinA{h}", [256, 2048]) for h in range(8)]
    KVinB = [dscr(f"KVinB{h}", [256, 2048]) for h in range(8)]
    KVoutA = [dscr(f"KVoutA{h}", [1024, 2048]) for h in range(8)]
    KVoutB = [dscr(f"KVoutB{h}", [512, 2048]) for h in range(8)]
    ON = dscr("ON", [8, 128, T])

    es = ExitStack()
    with es:
        AFt = es.enter_context(nc.sbuf_tensor("AF", [128, NF], F32))
        ABt = es.enter_context(nc.sbuf_tensor("AB", [128, NB], BF16))
        PSall = es.enter_context(nc.psum_tensor("psall", [128, 4096], F32))
        PS = [PSall[:, i * 512:(i + 1) * 512] for i in range(8)]
        esems = {e: es.enter_context(nc.semaphore("s_" + e)) for e in ("pe", "act", "dve", "pool", "sp")}
        nsem = [0]

        free_sems = []
        phase_sems = []

        def newsem(persist=False):
            if not persist and free_sems:
                sm = free_sems.pop()
            else:
                nsem[0] += 1
                sm = es.enter_context(nc.semaphore(f"d{nsem[0]}"))
            if not persist:
                phase_sems.append(sm)
            return sm

        def release_phase_sems():
            free_sems.extend(phase_sems)
            del phase_sems[:]

        af = Arena(AFt, NF)
        ab = Arena(ABt, NB)
        prog = Prog()
        stop_at = (dbg or {}).get("stop")

        def stop_point(name):
            if stop_at == name:
                raise _Stop()

        def PE(out, lhsT, rhs, start, stop, R, W):
            prog.add("pe", lambda e: e.matmul(out, lhsT=lhsT, rhs=rhs, start=start, stop=stop), R, W)

        def ACT(out, in_, func, R, W, scale=None, bias=None, eng="act"):
            kw = {}
            if scale is not None:
                kw["scale"] = scale
            if bias is not None:
                kw["bias"] = bias
            prog.add(eng, lambda e: e.activation(out, in_, func, **kw), R, W)

        def TT_(out, a, b, op, R, W, eng="dve"):
            prog.add(eng, lambda e: e.tensor_tensor(out, a, b, op), R, W)

        def TS(out, a, s1, s2, op0, op1, R, W, eng="dve"):
            if s2 is None:
                prog.add(eng, lambda e: e.tensor_scalar(out, a, s1, None, op0), R, W)
            else:
                prog.add(eng, lambda e: e.tensor_scalar(out, a, s1, s2, op0, op1), R, W)

        def STT(out, a, s, b, op0, op1, R, W, eng="dve"):
            prog.add(eng, lambda e: e.scalar_tensor_tensor(out, a, s, b, op0, op1), R, W)

        def CP(out, in_, R, W, eng="dve"):
            prog.add(eng, lambda e: e.tensor_copy(out, in_), R, W)

        def DMA(eng, out, in_, sem, R, W, persist=False):
            prog.add(eng, lambda e: e.dma_start(out=out, in_=in_), R, W, dma=(sem, 16), persist=persist)

        def ps(b):
            return PS[b][:, :]

        def psk(b):
            return ("ps", b)

        try:
            def cast_group(key, pairs):
                s = newsem()
                n = len(pairs)
                for i, (dst, src) in enumerate(pairs):
                    DMA("pool", dst, src, s, [], [key] if i == n - 1 else [])

            def win_pairs(l, i):
                prs = []
                for j in range(NJ):
                    for half in range(2):
                        src = w_ffn_in[l, i][:, half * DFF + j * 128: half * DFF + (j + 1) * 128].rearrange(
                            "(k p) c -> p k c", p=128)
                        dst = WIN[l][i][j].rearrange("p (k c) -> p k c", c=256)[:, :, half * 128:(half + 1) * 128]
                        prs.append((dst, src))
                return prs

            def wout_pairs(l, i):
                prs = []
                for m in range(8):
                    src = w_ffn_out[l, i][:, m * 128:(m + 1) * 128].rearrange("(j p) c -> p j c", p=128)
                    dst = WOUT[l][i][m].rearrange("p (j c) -> p j c", c=128)
                    prs.append((dst, src))
                return prs

            def proj_pairs(dst_t, src_w, ngrp):
                prs = []
                for g in range(ngrp):
                    for hh in range(2):
                        src = src_w[:, g * 512 + hh * 256: g * 512 + (hh + 1) * 256].rearrange("(k p) c -> p k c", p=128)
                        dst = dst_t[g].rearrange("p (k c) -> p k c", c=512)[:, :, hh * 256:(hh + 1) * 256]
                        prs.append((dst, src))
                return prs

            cast_q = []

            def cast_enqueue(key, pairs):
                sm = newsem(persist=True)
                n = len(pairs)
                for i, (dst, src) in enumerate(pairs):
                    cast_q.append((dst, src, sm, [key] if i == n - 1 else []))

            def issue_casts(n):
                for _ in range(min(n, len(cast_q))):
                    dst, src, sm, wk = cast_q.pop(0)
                    DMA("pool", dst, src, sm, [], wk, persist=True)

            if not small:
                wp = win_pairs(0, 0)
                for q4 in range(4):
                    cast_enqueue(("WIN", 0, 0, q4), wp[q4 * 12:(q4 + 1) * 12] if q4 < 3 else wp[36:])
                cast_enqueue(("WOUT", 0, 0), wout_pairs(0, 0))
                cast_enqueue("WAB", proj_pairs(WAB, w_ab_in, 6))
                issue_casts(len(cast_q))
                cast_enqueue("WABO", proj_pairs(WABO, w_ab_out, 2))
                cast_enqueue(("WIN", 0, 1), win_pairs(0, 1))
                cast_enqueue(("WOUT", 0, 1), wout_pairs(0, 1))
                cast_enqueue(("WIN", 1, 0), win_pairs(1, 0))
                cast_enqueue(("WOUT", 1, 0), wout_pairs(1, 0))
                cast_enqueue("WC", proj_pairs(WC, w_c_in, 6))
                cast_enqueue("WCO", proj_pairs(WCO, w_c_out, 2))
                cast_enqueue(("WIN", 1, 1), win_pairs(1, 1))
                cast_enqueue(("WOUT", 1, 1), wout_pairs(1, 1))
            stop_point("cast")

            gvec = af.alloc(64)
            convw = af.alloc(124)
            lg = af.alloc(8)
            cd = af.alloc(8)
            kcol = af.alloc(8)
            wbc = af.alloc(128)
            misc = af.alloc(16)
            cmask = af.alloc(64)
            identf = af.alloc(128)
            cb = ab.alloc(5, 128)
            Dt = ab.alloc(4, 512)
            rowf = ab.alloc(4, 512)
            rowb = ab.alloc(4, 512)
            IDENT, ONES, BONES, R0, R1 = (cb[:, i, :] for i in range(5))
            afm = af.top
            abm = ab.top

            s_c = newsem()
            cst = af.alloc(5 * 128 + 6 * 512 + 2 + 32)
            decay_t = af.alloc(8)
            lam_t = af.alloc(256)
            gqk_t = af.alloc(128)
            stmp = af.alloc(4, 512)
            for dst, src, key in ((gvec, gvec_d, "gvec"), (convw, convw_d, "convw"), (cst, cst_d, "cst"),
                                  (decay_t, decay_d, "decay_t"), (lam_t, lam_d, "lam_t"), (gqk_t, gqk_d, "gqk_t"),
                                  (cmask, cmask_d, "cmask")):
                DMA("sp", dst, src[:, :], newsem(), [], [key])
            o = 5 * 128
            M1 = cst[:, o:o + 512]
            M2 = cst[:, o + 512:o + 1024]
            MLE = cst[:, o + 1024:o + 1536]
            MGT = cst[:, o + 1536:o + 2048]
            IO1 = cst[:, o + 2048:o + 2560]
            IOB = cst[:, o + 2560:o + 3072]
            o2 = o + 3072
            COLA = cst[:, o2:o2 + 1]
            COLB = cst[:, o2 + 1:o2 + 2]
            CH128 = cst[:, o2 + 2:o2 + 34]
            for i in range(5):
                CP(cb[:, i, :], cst[:, i * 128:(i + 1) * 128], ["cst"], ["cb"])
            CP(identf, cst[:, 0:128], ["cst"], ["identf"])
            prog.add("dve", lambda e: e.memset(misc[:, 0:1], EPS), [], ["misc0"])
            stop_point("s1")
            ACT(lg, decay_t, AF.Exp, ["decay_t"], ["lg"])
            TS(lg, lg, -1.0, None, ALU.mult, None, ["lg"], ["lg"])
            ACT(cd, lg, AF.Exp, ["lg"], ["cd"], scale=128.0)
            SC = 128.0 ** -0.5
            for h in range(4):
                ACT(kcol[:, h:h + 1], COLA, AF.Exp, ["lg", "cst"], ["kcol"], scale=lg[:, h:h + 1])
                ACT(kcol[:, 4 + h:5 + h], COLB, AF.Exp, ["lg", "cst"], ["kcol"], scale=lg[:, 4 + h:5 + h])
                ACT(wbc[:, h * 32:(h + 1) * 32], CH128, AF.Exp, ["lg", "cst"], ["wbc"], scale=lg[:, 4 + h:5 + h])
                ACT(rowf[:, h, :], IO1, AF.Exp, ["lg", "cst"], ["rowf"], scale=lg[:, h:h + 1])
                ACT(rowb[:, h, :], IOB, AF.Exp, ["lg", "cst"], ["rowb"], scale=lg[:, 4 + h:5 + h])
            TS(kcol, kcol, SC, None, ALU.mult, None, ["kcol"], ["kcol"])
            stop_point("s2")
            TT_(stmp[:, 2, 0:64], lam_t[:, 0:64], lam_t[:, 64:128], ALU.mult, ["lam_t"], ["st2"])
            prog.add("dve", lambda e: e.reduce_sum(misc[:, 8:9], stmp[:, 2, 0:64], axis=AX.X), ["st2"], ["m8"])
            TT_(stmp[:, 2, 64:128], lam_t[:, 128:192], lam_t[:, 192:256], ALU.mult, ["lam_t"], ["st2b"])
            prog.add("dve", lambda e: e.reduce_sum(misc[:, 9:10], stmp[:, 2, 64:128], axis=AX.X), ["st2b"], ["m9"])
            ACT(misc[:, 8:10], misc[:, 8:10], AF.Exp, ["m8", "m9"], ["m89"])
            TT_(misc[:, 1:2], misc[:, 9:10], misc[:, 8:9], ALU.subtract, ["m89"], ["m1"])
            TS(misc[:, 1:2], misc[:, 1:2], -LAMBDA_INIT, None, ALU.add, None, ["m1"], ["m1"])
            TS(misc[:, 2:3], gvec[:, 62:63], 1.0 - LAMBDA_INIT, None, ALU.mult, None, ["gvec"], ["m2"])
            stop_point("s3")
            prog.add("dve", lambda e: e.reduce_max(misc[:, 10:11], gqk_t[:, 0:64], axis=AX.X, apply_absolute_value=True),
                     ["gqk_t"], ["m10"])
            prog.add("dve", lambda e: e.reduce_max(misc[:, 11:12], gqk_t[:, 64:128], axis=AX.X, apply_absolute_value=True),
                     ["gqk_t"], ["m11"])
            TT_(misc[:, 10:11], misc[:, 10:11], misc[:, 11:12], ALU.mult, ["m10", "m11"], ["m10"])
            STT(misc[:, 3:4], misc[:, 10:11], -8.0, cmask[:, 48:49], ALU.mult, ALU.add, ["m10", "cmask"], ["m3"])
            STT(misc[:, 4:5], misc[:, 10:11], -8.0, cmask[:, 49:50], ALU.mult, ALU.add, ["m10", "cmask"], ["m4"])
            stop_point("s4")
            for h in range(4):
                ACT(stmp[:, 0, :], M1, AF.Exp, ["lg", "cst"], ["st0"], scale=lg[:, h:h + 1])
                TT_(stmp[:, 0, :], stmp[:, 0, :], MLE, ALU.mult, ["st0", "cst"], ["st0"])
                ACT(stmp[:, 1, :], M2, AF.Exp, ["lg", "cst"], ["st1"], scale=lg[:, 4 + h:5 + h])
                TT_(stmp[:, 1, :], stmp[:, 1, :], MGT, ALU.mult, ["st1", "cst"], ["st1"])
                TT_(stmp[:, 0, :], stmp[:, 0, :], stmp[:, 1, :], ALU.add, ["st0", "st1"], ["st0"])
                TS(Dt[:, h, :], stmp[:, 0, :], SC, None, ALU.mult, None, ["st0"], ["Dt"])
            stop_point("setup")
            prog.barrier()
            release_phase_sems()

            class RL:
                pass

            def alloc_rowlocal(nx=2, with_sg=True):
                r = RL()
                r.xres = [af.alloc(8, 512) for _ in range(nx)]
                r.xsem = [newsem() for _ in range(nx)]
                r.rstd = [af.alloc(512) for _ in range(2)]
                r.sg = [af.alloc(512) for _ in range(2)] if with_sg else None
                r.tmp = [af.alloc(512) for _ in range(3)]
                r.xn = ab.alloc(8, 512)
                r.sq = [ab.alloc(512) for _ in range(4)]
                r.cnt = {"rstd": 0, "sq": 0, "sg": 0, "tmp": 0}
                return r

            def alloc_ffn(r):
                r.abuf = ab.alloc(NJ, 512)
                r.win = [ab.alloc(8, 256) for _ in range(3)]
                r.winsem = [newsem() for _ in range(3)]
                r.wout = [ab.alloc(NJ, 128) for _ in range(3)]
                r.woutsem = [newsem() for _ in range(3)]
                r.nwin = 0
                r.nwout = 0

            def rr(r, name, n):
                i = r.cnt[name] % n
                r.cnt[name] += 1
                return i

            def rstd_from(r, psb, scale, W):
                ri = rr(r, "rstd", 2)
                ACT(r.rstd[ri], ps(psb), AF.Ln, [psk(psb), "misc0"], [("rstd", ri)], scale=scale, bias=misc[:, 0:1])
                ACT(r.rstd[ri], r.rstd[ri], AF.Exp, [("rstd", ri)], [("rstd", ri)], scale=-0.5)
                return ri

            def norm_stat(r, xs, k):
                si = rr(r, "sq", 4)
                ACT(r.sq[si], r.xres[xs][:, k, :], AF.Square, [("xres", xs)], [("sq", si)])
                PE(ps(6), ONES, r.sq[si], k == 0, k == 7, [("sq", si), "cb"], [psk(6)])

            def rmsnorm(r, xs, gbase, pre=False):
                xk = ("xres", xs)
                if not pre:
                    for k in range(8):
                        norm_stat(r, xs, k)
                ri = rstd_from(r, 6, 1.0 / D, None)
                for k in range(8):
                    STT(r.xn[:, k, :], r.xres[xs][:, k, :], gvec[:, gbase + k:gbase + k + 1], r.rstd[ri],
                        ALU.mult, ALU.mult, [xk, ("rstd", ri), "gvec"], [("xn", k)])

            XN_ALL = [("xn", k) for k in range(8)]

            def ffn(r, l, i, xs, fuse_norm=False):
                xk = ("xres", xs)
                wkey = ("WIN", l, i)
                okey = ("WOUT", l, i)

                def load_wout(m):
                    sl = r.nwout % 3
                    r.nwout += 1
                    DMA("sp", r.wout[sl], WOUT[l][i][m].rearrange("p (j c) -> p j c", c=128), r.woutsem[sl],
                        [okey], [("wout", sl)])
                    return sl

                def load_win(j):
                    sl = r.nwin % 3
                    r.nwin += 1
                    wk = ("WIN", 0, 0, min(j // 6, 3)) if (l, i) == (0, 0) else wkey
                    DMA("sp", r.win[sl], WIN[l][i][j].rearrange("p (k c) -> p k c", c=256), r.winsem[sl],
                        [wk], [("win", sl)])
                    return sl

                wsl = {}
                wsl[0] = load_win(0)
                wsl[1] = load_win(1)
                osl = {}
                for j in range(NJ):
                    if j + 2 < NJ:
                        wsl[j + 2] = load_win(j + 2)
                    if j == NJ - 4:
                        osl[0] = load_wout(0)
                    if j == NJ - 2:
                        osl[1] = load_wout(1)
                    sl = wsl[j]
                    bg = j % 2
                    bu = 2 + j % 2
                    for half, b in ((0, bg), (1, bu)):
                        for k in range(8):
                            PE(ps(b), r.win[sl][:, k, half * 128:(half + 1) * 128], r.xn[:, k, :], k == 0, k == 7,
                               [("win", sl), ("xn", k)], [psk(b)])
                    si = rr(r, "sg", 2)
                    ACT(r.sg[si], ps(bg), AF.Silu, [psk(bg)], [("sg", si)])
                    TT_(r.abuf[:, j, :], r.sg[si], ps(bu), ALU.mult, [("sg", si), psk(bu)], [("abuf", j)])
                for m in range(8):
                    if m + 2 < 8:
                        osl[m + 2] = load_wout(m + 2)
                    sl = osl[m]
                    b = 4 + m % 2
                    for j in range(NJ):
                        PE(ps(b), r.wout[sl][:, j, :], r.abuf[:, j, :], j == 0, j == NJ - 1,
                           [("wout", sl), ("abuf", j)], [psk(b)])
                    if fuse_norm and m >= 1:
                        norm_stat(r, xs, m - 1)
                    STT(r.xres[xs][:, m, :], ps(b), 0.5, r.xres[xs][:, m, :], ALU.mult, ALU.add,
                        [psk(b), xk], [xk])
                if fuse_norm:
                    norm_stat(r, xs, 7)

            def load_x(r, src, t, xs, key):
                DMA("sp", r.xres[xs], src[:, t * TT:(t + 1) * TT].rearrange("(k p) t -> p k t", p=128), r.xsem[xs],
                    [(key, t)], [("xres", xs)])

            def store_x(r, dst, t, xs, key, sem):
                DMA("pool", dst[:, t * TT:(t + 1) * TT].rearrange("(k p) t -> p k t", p=128), r.xres[xs], sem[xs],
                    [("xres", xs)], [(key, t)])

            def run_items(items):
                n = len(items)
                ms = max(len(x) for x in items)
                for step in range(n + ms - 1):
                    for sidx in range(ms):
                        i = step - sidx
                        if 0 <= i < n and sidx < len(items[i]):
                            items[i][sidx]()

            af.top = afm
            ab.top = abm
            r = alloc_rowlocal(2)
            alloc_ffn(r)
            rope = af.alloc(2, 512)
            rope_sem = newsem()
            Fst = af.alloc(4, 128)
            Btot = af.alloc(4, 128)
            sigb = af.alloc(4, 512)
            edge = [af.alloc(4, 32) for _ in range(2)]
            wproj = [ab.alloc(8, 512) for _ in range(2)]
            wprojsem = [newsem() for _ in range(2)]
            QTs = ab.alloc(4, 512)
            KTs = ab.alloc(4, 512)
            Gs = ab.alloc(4, 512)
            Us = ab.alloc(4, 512)
            Vs = ab.alloc(4, 512)
            tmpb = [ab.alloc(512) for _ in range(2)]
            Kf = ab.alloc(4, 128)
            Kb = ab.alloc(4, 128)
            FSs = ab.alloc(16, 128)
            st_sem = {k: newsem() for k in ("qt", "kt", "g", "u", "v", "fs", "ex")}
            st_sem["x1"] = [newsem(), newsem()]
            ex_sem = [newsem(persist=True) for _ in range(2)]
            nproj = [0]

            def load_wproj(Wt, key, g):
                sl = nproj[0] % 2
                nproj[0] += 1
                DMA("sp", wproj[sl], Wt[g].rearrange("p (k c) -> p k c", c=512), wprojsem[sl], [key], [("wproj", sl)])
                return sl

            def proj_fm(sl, ft, b):
                for k in range(8):
                    PE(ps(b), wproj[sl][:, k, ft * 128:(ft + 1) * 128], r.xn[:, k, :], k == 0, k == 7,
                       [("wproj", sl), ("xn", k)], [psk(b)])

            def proj_tm(sl, c, b):
                for k in range(8):
                    PE(ps(b), r.xn[:, k, c * 128:(c + 1) * 128], wproj[sl][:, k, :], k == 0, k == 7,
                       [("wproj", sl), ("xn", k)], [psk(b)])

            def rope_apply(b, Rm, out_ap, outkey, ropekey):
                tb = rr(r, "tmp", 3)
                bi = r.cnt["sg"] % 2
                r.cnt["sg"] += 1
                ACT(tmpb[bi], ps(b), AF.Copy, [psk(b)], [("tmpb", bi)])
                PE(ps(7), Rm, tmpb[bi], True, True, [("tmpb", bi), "cb"], [psk(7)])
                t1 = r.tmp[tb]
                TT_(t1, ps(b), rope[:, 0, :], ALU.mult, [psk(b), ropekey], [("tmp", tb)])
                tb2 = rr(r, "tmp", 3)
                t2 = r.tmp[tb2]
                TT_(t2, ps(7), rope[:, 1, :], ALU.mult, [psk(7), ropekey], [("tmp", tb2)])
                TT_(out_ap, t1, t2, ALU.add, [("tmp", tb), ("tmp", tb2)], [outkey])

            prog.add("dve", lambda e: e.memset(Fst, 0.0), [], ["Fst"])
            prog.add("dve", lambda e: e.memset(Btot, 0.0), [], ["Btot"])
            prog.add("dve", lambda e: e.memset(edge[0], 0.0), [], [("edge", 0)])
            prog.add("dve", lambda e: e.memset(edge[1], 0.0), [], [("edge", 1)])
            load_x(r, xT, 0, 0, "xin")
            for t in range(NT):
                xs = t % 2
                if t + 1 < NT:
                    load_x(r, xT, t + 1, 1 - xs, "xin")
                rmsnorm(r, xs, 0)
                ffn(r, 0, 0, xs, fuse_norm=True)
                DMA("sp", rope, rope0_d[:, :, t * TT:(t + 1) * TT].rearrange("a p t -> p a t"), rope_sem, [], ["rope"])
                store_x(r, X1, t, xs, "X1", st_sem["x1"])
                rmsnorm(r, xs, 8, pre=True)
                GORD = [1, 2, 0, 3, 5, 4]
                gslot = {}
                gslot[0] = load_wproj(WAB, "WAB", GORD[0])
                gslot[1] = load_wproj(WAB, "WAB", GORD[1])
                pc = {"pb": 0, "rb": 0, "t1": 0}

                def mk_proj(gi, ft, first, tm=False):
                    st = {}

                    def s0():
                        if first and gi + 1 < 6 and gi >= 1:
                            gslot[gi + 1] = load_wproj(WAB, "WAB", GORD[gi + 1])
                        st["b"] = pc["pb"] % 3
                        pc["pb"] += 1
                        if tm:
                            proj_tm(gslot[gi], ft, st["b"])
                        else:
                            proj_fm(gslot[gi], ft, st["b"])
                    return st, s0

                def rope_item(gi, h, first, out_t, okey):
                    st, s0 = mk_proj(gi, h, first)

                    def s1():
                        b = st["b"]
                        bi = r.cnt["sg"] % 2
                        r.cnt["sg"] += 1
                        st["rb"] = 3 + pc["rb"] % 2
                        pc["rb"] += 1
                        st["t1"] = pc["t1"] % 2
                        pc["t1"] += 1
                        ACT(tmpb[bi], ps(b), AF.Copy, [psk(b)], [("tmpb", bi)])
                        PE(ps(st["rb"]), R0, tmpb[bi], True, True, [("tmpb", bi), "cb"], [psk(st["rb"])])
                        TT_(r.tmp[st["t1"]], ps(b), rope[:, 0, :], ALU.mult, [psk(b), "rope"], [("tmp", st["t1"])])

                    def s2():
                        TT_(r.tmp[2], ps(st["rb"]), rope[:, 1, :], ALU.mult, [psk(st["rb"]), "rope"], [("tmp", 2)])
                        TT_(out_t[:, h, :], r.tmp[st["t1"]], r.tmp[2], ALU.add, [("tmp", st["t1"]), ("tmp", 2)],
                            [(okey, h)])
                    return [s0, s1, s2]

                def simple_item(gi, ft, first, evac, tm=False):
                    st, s0 = mk_proj(gi, ft, first, tm)

                    def s1():
                        evac(st["b"])
                    return [s0, s1]

                def kvA(c):
                    def f():
                        for h in range(4):
                            PE(PS[5][:, h * 128:(h + 1) * 128], KTs[:, h, c * 128:(c + 1) * 128], IDENT, True, True,
                               [("KTs", h), "cb"], [psk(5)])
                        for h in range(4):
                            prog.add("act", (lambda h=h: lambda e: e.mul(Kf[:, h, :], PS[5][:, h * 128:(h + 1) * 128],
                                                                          kcol[:, h:h + 1]))(),
                                     [psk(5), "kcol"], [("Kf", h)])
                            TS(Kb[:, h, :], PS[5][:, h * 128:(h + 1) * 128], kcol[:, 4 + h:5 + h], None, ALU.mult, None,
                               [psk(5), "kcol"], [("Kb", h)])
                    return [f]

                def kvB(c):
                    cg_ = (t % 4) * 4 + c

                    def f():
                        for h in range(4):
                            PE(PS[6][:, h * 128:(h + 1) * 128], Kf[:, h, :], Vs[:, c, h * 128:(h + 1) * 128], True, True,
                               [("Kf", h), ("Vs", c)], [psk(6)])
                            PE(PS[7][:, h * 128:(h + 1) * 128], Kb[:, h, :], Vs[:, c, h * 128:(h + 1) * 128], True, True,
                               [("Kb", h), ("Vs", c)], [psk(7)])
                        ACT(FSs[:, c * 4:(c + 1) * 4, :], Fst, AF.Copy, ["Fst"], [("FSs", c)])
                        for h in range(4):
                            STT(Fst[:, h, :], Fst[:, h, :], cd[:, h:h + 1], PS[6][:, h * 128:(h + 1) * 128], ALU.mult,
                                ALU.add, ["Fst", psk(6), "cd"], ["Fst"])
                            STT(Btot[:, h, :], PS[7][:, h * 128:(h + 1) * 128], wbc[:, h * 32 + cg_:h * 32 + cg_ + 1],
                                Btot[:, h, :], ALU.mult, ALU.add, ["Btot", psk(7), "wbc"], ["Btot"])
                    return [f]

                items = []
                for h in range(4):
                    items.append(rope_item(0, h, h == 0, KTs, "KTs"))
                for c in range(4):
                    items.append(simple_item(1, c, c == 0,
                                             (lambda c=c: lambda b: ACT(Vs[:, c, :], ps(b), AF.Copy, [psk(b)],
                                                                        [("Vs", c)]))(), tm=True))
                for h in range(4):
                    items.append(kvA(h))
                    items.append(rope_item(2, h, h == 0, QTs, "QTs"))
                    items.append(kvB(h))
                for h in range(4):
                    items.append(simple_item(3, h, h == 0,
                                             (lambda h=h: lambda b: ACT(Gs[:, h, :], ps(b), AF.Silu, [psk(b)],
                                                                        [("Gs", h)]))()))
                for h in range(4):
                    items.append(simple_item(4, h, h == 0,
                                             (lambda h=h: lambda b: ACT(sigb[:, h, :], ps(b), AF.Sigmoid, [psk(b)],
                                                                        [("sigb", h)]))()))
                for h in range(4):
                    items.append(simple_item(5, h, h == 0,
                                             (lambda h=h: lambda b: TT_(Us[:, h, :], ps(b), sigb[:, h, :], ALU.mult,
                                                                        [psk(b), ("sigb", h)], [("Us", h)]))()))
                run_items(items)
                DMA("pool", KT0[:, t * TT:(t + 1) * TT].rearrange("(h p) t -> p h t", p=128), KTs, st_sem["kt"],
                    [("KTs", h) for h in range(4)], [("KT0", t)])
                DMA("pool", V0[t * TT:(t + 1) * TT, :].rearrange("(c p) e -> p c e", p=128), Vs, st_sem["v"],
                    [("Vs", c) for c in range(4)], [("V0", t)])
                DMA("pool", FS[t].rearrange("p (a e) -> p a e", e=128), FSs, st_sem["fs"],
                    [("FSs", c) for c in range(4)], [("FS", t)])
                DMA("pool", QT0[:, t * TT:(t + 1) * TT].rearrange("(h p) t -> p h t", p=128), QTs, st_sem["qt"],
                    [("QTs", h) for h in range(4)], [("QT0", t)])
                DMA("pool", G0[:, t * TT:(t + 1) * TT].rearrange("(h p) t -> p h t", p=128), Gs, st_sem["g"],
                    [("Gs", h) for h in range(4)], [("G0", t)])
                sg_ = t // 4
                if t % 4 == 0:
                    CP(edge[sg_][:, :, 0:15], Us[:, :, 0:15], [("Us", h) for h in range(4)], [("edge", sg_)])
                if t % 4 == 3:
                    CP(edge[sg_][:, :, 15:30], Us[:, :, 497:512], [("Us", h) for h in range(4)], [("edge", sg_)])
                ub = sg_ * 2080 + 16 + (t % 4) * TT
                DMA("pool", U0[:, ub:ub + TT].rearrange("(h p) t -> p h t", p=128), Us, st_sem["u"],
                    [("Us", h) for h in range(4)], [("U0", t)])
                if t % 4 == 3:
                    DMA("pool", EXin[sg_][:, 0:128].rearrange("(h p) e -> p h e", p=128), Fst, st_sem["ex"], ["Fst"],
                        [("EXin", sg_)])
                    DMA("pool", EXin[sg_][:, 128:256].rearrange("(h p) e -> p h e", p=128), Btot, st_sem["ex"],
                        ["Btot"], [("EXin", sg_)])
                    DMA("pool", EXin[sg_][:, 256:288].rearrange("(h p) e -> p h e", p=128), edge[sg_], st_sem["ex"],
                        [("edge", sg_)], [("EXin", sg_)])
                    prog.add("pool", (lambda sg_=sg_: lambda e: e.collective_compute(
                        "AllGather", ALU.bypass, replica_groups=[[0, 1, 2, 3], [4, 5, 6, 7]],
                        ins=[EXin[sg_].opt()], outs=[EXout[sg_].opt()]))(),
                        [("EXin", sg_)], [("EXout", sg_)], dma=(ex_sem[sg_], 1), persist=True)
                    if t == 3:
                        prog.add("dve", lambda e: e.memset(Fst, 0.0), [], ["Fst"])
                        prog.add("dve", lambda e: e.memset(Btot, 0.0), [], ["Btot"])
                if not small:
                    issue_casts(8 if t < NT - 1 else 0)
            stop_point("P1")
            prog.barrier()
            release_phase_sems()

            af.top = afm
            ab.top = abm
            r = alloc_rowlocal(2, with_sg=False)
            cv = af.alloc(4, 512)
            Bst = af.alloc(4, 128)
            tpos = af.alloc(512)
            coef = af.alloc(64)
            Fin = af.alloc(2, 4, 128)
            Bin = af.alloc(2, 4, 128)
            hal = af.alloc(2, 4, 32)
            qt = ab.alloc(4, 512)
            kt = ab.alloc(4, 512)
            vt = ab.alloc(4, 512)
            gt = ab.alloc(4, 512)
            ut = ab.alloc(4, 544)
            fs = ab.alloc(16, 128)
            Qf = ab.alloc(4, 512)
            Qb = ab.alloc(4, 512)
            Qg = ab.alloc(4, 512)
            gf = [ab.alloc(512) for _ in range(2)]
            Kb2 = ab.alloc(4, 128)
            Bbf = ab.alloc(4, 128)
            Finbf = ab.alloc(2, 4, 128)
            AT = [ab.alloc(4, 128) for _ in range(2)]
            cat = ab.alloc(8, 512)
            diag = ab.alloc(124, 128)
            wabo = ab.alloc(2, 8, 512)
            halo = ab.alloc(2, 4, 32)
            ld = {k: newsem() for k in ("qt", "kt", "vt", "gt", "ut", "fs", "tpos", "ex", "wabo", "halo")}
            st_x2 = [newsem(), newsem()]

            for sg_ in range(2):
                for d in range(2):
                    for h in range(4):
                        o = ((sg_ * 2 + d) * 4 + h) * 4
                        dcol = (sg_ * 2 + d) * 4
                        ACT(coef[:, o:o + 4], cmask[:, dcol:dcol + 4], AF.Exp, ["cmask", "lg"], ["coef"],
                            scale=lg[:, d * 4 + h:d * 4 + h + 1])
                        TT_(coef[:, o:o + 4], coef[:, o:o + 4], cmask[:, 16 + dcol:16 + dcol + 4], ALU.mult,
                            ["coef", "cmask"], ["coef"])
            stage = cv[:, :, 0:288]
            CVK = [("cv", ct_) for ct_ in range(4)]

            def p2a_setup(sg_):
                if True:
                    for i in range(4):
                        DMA("sp", stage, EXout[sg_][i * 512:(i + 1) * 512, 0:288].rearrange(
                            "(h p) e -> p h e", p=128), ld["ex"], [("EXout", sg_)], CVK)
                        for h in range(4):
                            cf = ((sg_ * 2 + 0) * 4 + h) * 4 + i
                            cb_ = ((sg_ * 2 + 1) * 4 + h) * 4 + i
                            if i == 0:
                                TS(Fin[:, sg_, h, :], stage[:, h, 0:128], coef[:, cf:cf + 1], None, ALU.mult, None,
                                   CVK + ["coef"], [("Fin", sg_)])
                                TS(Bin[:, sg_, h, :], stage[:, h, 128:256], coef[:, cb_:cb_ + 1], None, ALU.mult, None,
                                   CVK + ["coef"], [("Bin", sg_)])
                            else:
                                STT(Fin[:, sg_, h, :], stage[:, h, 0:128], coef[:, cf:cf + 1], Fin[:, sg_, h, :], ALU.mult,
                                    ALU.add, CVK + ["coef", ("Fin", sg_)], [("Fin", sg_)])
                                STT(Bin[:, sg_, h, :], stage[:, h, 128:256], coef[:, cb_:cb_ + 1], Bin[:, sg_, h, :],
                                    ALU.mult, ALU.add, CVK + ["coef", ("Bin", sg_)], [("Bin", sg_)])
                        ml = 32 + (sg_ * 2 + 0) * 4 + i
                        mr = 32 + (sg_ * 2 + 1) * 4 + i
                        if i == 0:
                            TS(hal[:, sg_, :, 0:16], stage[:, :, 271:287], cmask[:, ml:ml + 1], None, ALU.mult, None,
                               CVK + ["cmask"], [("hal", sg_)])
                            TS(hal[:, sg_, :, 16:32], stage[:, :, 256:272], cmask[:, mr:mr + 1], None, ALU.mult, None,
                               CVK + ["cmask"], [("hal", sg_)])
                        else:
                            STT(hal[:, sg_, :, 0:16], stage[:, :, 271:287], cmask[:, ml:ml + 1], hal[:, sg_, :, 0:16],
                                ALU.mult, ALU.add, CVK + ["cmask", ("hal", sg_)], [("hal", sg_)])
                            STT(hal[:, sg_, :, 16:32], stage[:, :, 256:272], cmask[:, mr:mr + 1], hal[:, sg_, :, 16:32],
                                ALU.mult, ALU.add, CVK + ["cmask", ("hal", sg_)], [("hal", sg_)])
                    CP(Finbf[:, sg_], Fin[:, sg_], [("Fin", sg_)], [("Finbf", sg_)])
                    CP(halo[:, sg_], hal[:, sg_], [("hal", sg_)], [("halo", sg_)])
                    sbb = sg_ * 2080
                    DMA("pool", U0[:, sbb + 1:sbb + 16].rearrange("(h p) t -> p h t", p=128), halo[:, sg_, :, 0:15],
                        ld["halo"], [("halo", sg_)], ["U0h"])
                    DMA("pool", U0[:, sbb + 16 + 2048:sbb + 16 + 2048 + 15].rearrange("(h p) t -> p h t", p=128),
                        halo[:, sg_, :, 16:31], ld["halo"], [("halo", sg_)], ["U0h"])
            DMA("sp", wabo, WABO.rearrange("g p (k c) -> p g k c", c=512), ld["wabo"], ["WABO"], ["wabo"])
            for ct in range(4):
                for w in range(31):
                    TS(diag[:, ct * 31 + w, :], identf, convw[:, ct * 31 + w:ct * 31 + w + 1], None, ALU.mult, None,
                       ["identf", "convw"], ["diag"])

            def p2a_loads(t):
                cs = slice(t * TT, (t + 1) * TT)
                DMA("sp", qt, QT0[:, cs].rearrange("(h p) t -> p h t", p=128), ld["qt"], [("QT0", t)], ["qt"])
                DMA("sp", kt, KT0[:, cs].rearrange("(h p) t -> p h t", p=128), ld["kt"], [("KT0", t)], ["kt"])
                DMA("sp", vt, V0[cs, :].rearrange("(c p) e -> p c e", p=128), ld["vt"], [("V0", t)], ["vt"])
                DMA("sp", fs, FS[t].rearrange("p (a e) -> p a e", e=128), ld["fs"], [("FS", t)], ["fs"])
                DMA("sp", tpos, tpos_d[:, cs], ld["tpos"], [], ["tpos"])

            def p2a_load_ut(t):
                ub = (t // 4) * 2080 + 1 + (t % 4) * TT
                DMA("sp", ut[:, :, 0:542], U0[:, ub:ub + 542].rearrange("(h p) t -> p h t", p=128),
                    ld["ut"], [("U0", tt_) for tt_ in range(NT)] + ["U0h"], ["ut"])

            def p2a_load_gt(t):
                cs = slice(t * TT, (t + 1) * TT)
                DMA("sp", gt, G0[:, cs].rearrange("(h p) t -> p h t", p=128), ld["gt"], [("G0", t)], ["gt"])

            ORDER = [3, 2, 1, 0, 7, 6, 5, 4]
            p2a_setup(0)
            load_x(r, X1, ORDER[0], 0, "X1")
            p2a_loads(ORDER[0])
            p2a_load_ut(ORDER[0])
            p2a_load_gt(ORDER[0])
            for oi_, t in enumerate(ORDER):
                xs = oi_ % 2
                nxt = ORDER[oi_ + 1] if oi_ + 1 < NT else None
                cs = slice(t * TT, (t + 1) * TT)
                if nxt is not None:
                    load_x(r, X1, nxt, 1 - xs, "X1")
                sg_ = t // 4
                if t % 4 == 3:
                    CP(Bst, Bin[:, sg_], [("Bin", sg_)], ["Bst"])
                    ACT(Bbf, Bst, AF.Copy, ["Bst"], ["Bbf"])
                for h in range(4):
                    gi = h % 2
                    ACT(gf[gi], tpos, AF.Exp, ["tpos", "lg"], [("gf", gi)], scale=lg[:, h:h + 1])
                    TT_(Qg[:, h, :], qt[:, h, :], gf[gi], ALU.mult, ["qt", ("gf", gi)], [("Qg", h)])
                    TT_(Qf[:, h, :], qt[:, h, :], rowf[:, h, :], ALU.mult, ["qt", "rowf"], [("Qf", h)])
                    TT_(Qb[:, h, :], qt[:, h, :], rowb[:, h, :], ALU.mult, ["qt", "rowb"], [("Qb", h)])
                for c in range(3, -1, -1):
                    cc = slice(c * 128, (c + 1) * 128)
                    sb = c % 2
                    ct = 3 - c
                    for h in range(4):
                        PE(PS[7][:, h * 128:(h + 1) * 128], kt[:, h, cc], IDENT, True, True, ["kt", "cb"], [psk(7)])
                    for h in range(4):
                        PE(PS[0][:, h * 128:(h + 1) * 128], kt[:, h, cc], qt[:, h, cc], True, True, ["kt", "qt"],
                           [psk(0)])
                    for h in range(4):
                        TS(Kb2[:, h, :], PS[7][:, h * 128:(h + 1) * 128], kcol[:, 4 + h:5 + h], None, ALU.mult, None,
                           [psk(7), "kcol"], [("Kb2", h)])
                    TT_(AT[sb], PS[0][:, :].rearrange("p (h i) -> p h i", i=128),
                        Dt[:, :, 0:128], ALU.mult, [psk(0), "Dt"], [("AT", sb)])
                    for w in range(0, 16):
                        PE(ps(1), diag[:, ct * 31 + w, :], ut[:, ct, w:w + 512], w == 0, w == 30, ["diag", "ut"], [psk(1)])
                    for h in range(4):
                        hs = slice(h * 128, (h + 1) * 128)
                        ob = 2 + h
                        PE(PS[ob][:, cc], vt[:, c, hs], AT[sb][:, h, :], True, False, ["vt", ("AT", sb)], [psk(ob)])
                        PE(PS[ob][:, cc], fs[:, c * 4 + h, :], Qf[:, h, cc], False, False, ["fs", ("Qf", h)], [psk(ob)])
                        PE(PS[ob][:, cc], Bbf[:, h, :], Qb[:, h, cc], False, False, ["Bbf", ("Qb", h)], [psk(ob)])
                        PE(PS[ob][:, cc], Finbf[:, sg_, h, :], Qg[:, h, cc], False, True, [("Finbf", sg_), ("Qg", h)], [psk(ob)])
                    for h in range(4):
                        hs = slice(h * 128, (h + 1) * 128)
                        PE(PS[6][:, hs], Kb2[:, h, :], vt[:, c, hs], True, True, [("Kb2", h), "vt"], [psk(6)])
                    for w in range(16, 31):
                        PE(ps(1), diag[:, ct * 31 + w, :], ut[:, ct, w:w + 512], w == 0, w == 30, ["diag", "ut"], [psk(1)])
                    for h in range(4):
                        hs = slice(h * 128, (h + 1) * 128)
                        STT(Bst[:, h, :], Bst[:, h, :], cd[:, 4 + h:5 + h], PS[6][:, hs], ALU.mult, ALU.add,
                            ["Bst", psk(6), "cd"], ["Bst"])
                    ACT(Bbf, Bst, AF.Copy, ["Bst"], ["Bbf"])
                    ACT(cv[:, ct, :], ps(1), AF.Identity, [psk(1), "gvec"], [("cv", ct)], bias=gvec[:, 52 + ct:53 + ct])
                if nxt is not None:
                    p2a_loads(nxt)
                    if nxt != 7:
                        p2a_load_ut(nxt)
                for h in range(4):
                    ob = 2 + h
                    si = rr(r, "sq", 4)
                    ACT(r.sq[si], ps(ob), AF.Square, [psk(ob)], [("sq", si)])
                    PE(ps(0), ONES, r.sq[si], True, True, [("sq", si), "cb"], [psk(0)])
                    ri = rstd_from(r, 0, 1.0 / 128, None)
                    tb = rr(r, "tmp", 3)
                    STT(r.tmp[tb], ps(ob), gvec[:, 48 + h:49 + h], r.rstd[ri], ALU.mult, ALU.mult,
                        [psk(ob), ("rstd", ri), "gvec"], [("tmp", tb)])
                    TT_(cat[:, h, :], r.tmp[tb], gt[:, h, :], ALU.mult, [("tmp", tb), "gt"], [("cat", h)])
                if nxt is not None:
                    p2a_load_gt(nxt)
                for ct in range(4):
                    si = rr(r, "sq", 4)
                    ACT(r.sq[si], cv[:, ct, :], AF.Square, [("cv", ct)], [("sq", si)])
                    PE(ps(1), ONES, r.sq[si], ct == 0, ct == 3, [("sq", si), "cb"], [psk(1)])
                ri = rstd_from(r, 1, 1.0 / 512, None)
                for ct in range(4):
                    tb = rr(r, "tmp", 3)
                    STT(r.tmp[tb], cv[:, ct, :], gvec[:, 56 + ct:57 + ct], r.rstd[ri], ALU.mult, ALU.mult,
                        [("cv", ct), ("rstd", ri), "gvec"], [("tmp", tb)])
                    ACT(cat[:, 4 + ct, :], r.tmp[tb], AF.Silu, [("tmp", tb)], [("cat", 4 + ct)])
                if nxt == 7:
                    p2a_setup(1)
                    p2a_load_ut(7)
                for m in range(8):
                    b = m % 2
                    for c8 in range(8):
                        PE(ps(b), wabo[:, m // 4, c8, (m % 4) * 128:(m % 4 + 1) * 128], cat[:, c8, :], c8 == 0, c8 == 7,
                           ["wabo", ("cat", c8)], [psk(b)])
                    TT_(r.xres[xs][:, m, :], ps(b), r.xres[xs][:, m, :], ALU.add, [psk(b), ("xres", xs)], [("xres", xs)])
                store_x(r, X2, t, xs, "X2", st_x2)
                if not small:
                    issue_casts(max(0, min(10, len(cast_q) - 56)))
            stop_point("P2a")
            prog.barrier()
            release_phase_sems()

            af.top = afm
            ab.top = abm
            r = alloc_rowlocal(2)
            alloc_ffn(r)
            rope = af.alloc(2, 512)
            rope_sem = newsem()
            wproj = [ab.alloc(8, 512) for _ in range(2)]
            wprojsem = [newsem() for _ in range(2)]
            Q1s = ab.alloc(8, 512)
            K1s = ab.alloc(8, 512)
            V1s = ab.alloc(4, 1024)
            tmpb = [ab.alloc(512) for _ in range(2)]
            st2 = {k: newsem() for k in ("q", "k", "v")}
            st2["x4"] = [newsem(), newsem()]
            nproj[0] = 0
            kvsem = [newsem(persist=True) for _ in range(16)]

            def gatherA(h):
                prog.add("pool", lambda e: e.collective_compute(
                    "AllGather", ALU.bypass, replica_groups=[[0, 1, 2, 3], [4, 5, 6, 7]],
                    ins=[KVinA[h].opt()], outs=[KVoutA[h].opt()]),
                    [("KVdone", 0, tl_, kv_) for tl_ in range(4) for kv_ in "kv"], [("KVoutA", h)],
                    dma=(kvsem[2 * h], 1), persist=True)

            def gatherB(h):
                prog.add("pool", lambda e: e.collective_compute(
                    "AllGather", ALU.bypass, replica_groups=[[0, 1], [2, 3], [4, 5], [6, 7]],
                    ins=[KVinB[h].opt()], outs=[KVoutB[h].opt()]),
                    [("KVdone", 1, tl_, kv_) for tl_ in range(4) for kv_ in "kv"], [("KVoutB", h)],
                    dma=(kvsem[2 * h + 1], 1), persist=True)

            def qk_post(b, gcol, out_ap, outkey):
                si = rr(r, "sq", 4)
                ACT(r.sq[si], ps(b), AF.Square, [psk(b)], [("sq", si)])
                PE(ps(6), BONES, r.sq[si], True, True, [("sq", si), "cb"], [psk(6)])
                ri = rstd_from(r, 6, 1.0 / 64, None)
                tb = rr(r, "tmp", 3)
                qn = r.tmp[tb]
                STT(qn, ps(b), gvec[:, gcol:gcol + 1], r.rstd[ri], ALU.mult, ALU.mult,
                    [psk(b), ("rstd", ri), "gvec"], [("tmp", tb)])
                bi = r.cnt["sg"] % 2
                r.cnt["sg"] += 1
                ACT(tmpb[bi], qn, AF.Copy, [("tmp", tb)], [("tmpb", bi)])
                PE(ps(7), R1, tmpb[bi], True, True, [("tmpb", bi), "cb"], [psk(7)])
                tb2 = rr(r, "tmp", 3)
                TT_(r.tmp[tb2], ps(7), rope[:, 1, :], ALU.mult, [psk(7), "rope"], [("tmp", tb2)])
                TT_(qn, qn, rope[:, 0, :], ALU.mult, [("tmp", tb), "rope"], [("tmp", tb)])
                TT_(out_ap, qn, r.tmp[tb2], ALU.add, [("tmp", tb), ("tmp", tb2)], [outkey])

            load_x(r, X2, 0, 0, "X2")
            for t in range(NT):
                xs = t % 2
                cs = slice(t * TT, (t + 1) * TT)
                if t + 1 < NT:
                    load_x(r, X2, t + 1, 1 - xs, "X2")
                rmsnorm(r, xs, 16)
                ffn(r, 0, 1, xs, fuse_norm=True)
                rmsnorm(r, xs, 24, pre=True)
                ffn(r, 1, 0, xs, fuse_norm=True)
                DMA("sp", rope, rope1_d[:, :, cs].rearrange("a p t -> p a t"), rope_sem, [], ["rope"])
                store_x(r, X4, t, xs, "X4", st2["x4"])
                rmsnorm(r, xs, 32, pre=True)
                gslot = {}
                gslot[0] = load_wproj(WC, "WC", 0)
                gslot[1] = load_wproj(WC, "WC", 1)
                pc = {"pb": 0, "sb": 0, "rb": 0, "rs": 0, "qn": 0}

                def mk_proj2(g, ft, first, tm=False):
                    st = {}

                    def s0():
                        if first and g + 1 < 6 and g >= 1:
                            gslot[g + 1] = load_wproj(WC, "WC", g + 1)
                        st["b"] = pc["pb"] % 4
                        pc["pb"] += 1
                        if tm:
                            proj_tm(gslot[g], ft, st["b"])
                        else:
                            proj_fm(gslot[g], ft, st["b"])
                    return st, s0

                def qk_item(g, ft, first, gcol, out_t, okey):
                    st, s0 = mk_proj2(g, ft, first)
                    hh = (g % 2) * 4 + ft

                    def s1():
                        b = st["b"]
                        si = rr(r, "sq", 4)
                        st["sb"] = 4 + pc["sb"] % 2
                        pc["sb"] += 1
                        ACT(r.sq[si], ps(b), AF.Square, [psk(b)], [("sq", si)])
                        PE(ps(st["sb"]), BONES, r.sq[si], True, True, [("sq", si), "cb"], [psk(st["sb"])])

                    def s2():
                        ri = pc["rs"] % 2
                        pc["rs"] += 1
                        st["ri"] = ri
                        sb = st["sb"]
                        ACT(r.rstd[ri], ps(sb), AF.Ln, [psk(sb), "misc0"], [("rstd", ri)], scale=1.0 / 64,
                            bias=misc[:, 0:1])
                        ACT(r.rstd[ri], r.rstd[ri], AF.Exp, [("rstd", ri)], [("rstd", ri)], scale=-0.5)

                    def s3():
                        b = st["b"]
                        ri = st["ri"]
                        qi = pc["qn"] % 2
                        pc["qn"] += 1
                        st["qi"] = qi
                        st["rb"] = 6 + pc["rb"] % 2
                        pc["rb"] += 1
                        qn = r.tmp[qi]
                        STT(qn, ps(b), gvec[:, gcol:gcol + 1], r.rstd[ri], ALU.mult, ALU.mult,
                            [psk(b), ("rstd", ri), "gvec"], [("tmp", qi)])
                        bi = r.cnt["sg"] % 2
                        r.cnt["sg"] += 1
                        ACT(tmpb[bi], qn, AF.Copy, [("tmp", qi)], [("tmpb", bi)])
                        PE(ps(st["rb"]), R1, tmpb[bi], True, True, [("tmpb", bi), "cb"], [psk(st["rb"])])
                        TT_(qn, qn, rope[:, 0, :], ALU.mult, [("tmp", qi), "rope"], [("tmp", qi)])

                    def s4():
                        qi = st["qi"]
                        TT_(r.tmp[2], ps(st["rb"]), rope[:, 1, :], ALU.mult, [psk(st["rb"]), "rope"], [("tmp", 2)])
                        TT_(out_t[:, hh, :], r.tmp[qi], r.tmp[2], ALU.add, [("tmp", qi), ("tmp", 2)], [(okey, hh)])
                    return [s0, s1, s2, s3, s4]

                def v_item(g, c, first):
                    st, s0 = mk_proj2(g, c, first, tm=True)

                    def s1():
                        b = st["b"]
                        ACT(V1s[:, c, (g - 4) * 512:(g - 3) * 512], ps(b), AF.Copy, [psk(b)], [("V1s", c)])
                    return [s0, s1]

                items = []
                for g in range(6):
                    for ft in range(4):
                        if g < 2:
                            items.append(qk_item(g, ft, ft == 0, 60, Q1s, "Q1s"))
                        elif g < 4:
                            items.append(qk_item(g, ft, ft == 0, 61, K1s, "K1s"))
                        else:
                            items.append(v_item(g, ft, ft == 0))
                run_items(items)
                DMA("pool", QT1[:, :, cs].rearrange("h p t -> p h t"), Q1s, st2["q"],
                    [("Q1s", h) for h in range(8)], [("QT1", t)])
                KVs = KVinA if t < 4 else KVinB
                tl = t % 4
                for h in range(8):
                    DMA("pool", KVs[h][0:128, tl * TT:(tl + 1) * TT], K1s[:, h, :], st2["k"],
                        [("K1s", h_) for h_ in range(8)], [("KVdone", t // 4, tl, "k")] if h == 7 else [])
                for c in range(4):
                    for h in range(8):
                        dst = KVs[h][128:256, :].rearrange("r (a e) -> (r a) e", e=128)[
                            tl * TT + c * 128:tl * TT + (c + 1) * 128, :]
                        DMA("pool", dst, V1s[:, c, h * 128:(h + 1) * 128], st2["v"],
                            [("V1s", c_) for c_ in range(4)],
                            [("KVdone", t // 4, tl, "v")] if (h == 7 and c == 3) else [])
                if t == NT - 1:
                    gatherA(0)
                    gatherB(0)
            stop_point("P2b")
            prog.barrier()
            release_phase_sems()

            af.top = afm
            ab.top = abm
            ones_f = af.alloc(128)
            acc = [af.alloc(1024) for _ in range(2)]
            rc = af.alloc(1024)
            t01 = af.alloc(1024)
            obuf = [af.alloc(512) for _ in range(2)]
            rstd3 = af.alloc(512)
            KTa = [ab.alloc(6, 2048) for _ in range(2)]
            Va = [ab.alloc(6, 16, 128) for _ in range(2)]
            Qa = [ab.alloc(T) for _ in range(2)]
            PT = [ab.alloc(1024) for _ in range(3)]
            sq3 = [ab.alloc(512) for _ in range(2)]
            ONs = [ab.alloc(512) for _ in range(4)]
            ldk = [[newsem() for _ in range(2)] for _ in range(2)]
            ldv = [[newsem() for _ in range(2)] for _ in range(2)]
            ldq = [newsem() for _ in range(2)]
            st_on = [newsem() for _ in range(4)]
            prog.add("dve", lambda e: e.memset(ones_f, 1.0), [], ["ones_f"])
            SPAIR = [(0, 1), (2, 3)]
            PVPAIR = [(4, 5), (6, 7)]

            def pair_ap(p):
                return PSall[:, p[0] * 512:p[0] * 512 + 1024]

            def pair_keys(p):
                return [psk(p[0]), psk(p[1])]

            state = {"ring": 0, "pt": 0, "on": 0, "ptp": 0}
            steps = []
            pending = None
            for h in range(8):
                for qc in range(8):
                    for kidx in range(64 if qc < 4 else 32):
                        steps.append(("qk", h, qc, kidx))
                        if kidx == 1 and pending is not None:
                            steps.append(pending)
                            pending = None
                    pending = ("sum", h, qc, 0)
            steps.append(pending)

            def load_head(h):
                hs = h % 2
                DMA("sp", Qa[hs], QT1[h], ldq[hs], [("QT1", t) for t in range(NT)], [("Qa", hs)])
                for blk in range(6):
                    part = 0 if blk < 4 else 1
                    src = KVoutA[h] if blk < 4 else KVoutB[h]
                    key = ("KVoutA", h) if blk < 4 else ("KVoutB", h)
                    b0 = (blk if blk < 4 else blk - 4) * 256
                    DMA("sp", KTa[hs][:, blk, :], src[b0:b0 + 128, :], ldk[hs][part], [key], [("KTa", hs, part)])
                    DMA("sp", Va[hs][:, blk, :, :],
                        src[b0 + 128:b0 + 256, :].rearrange("r (a e) -> (r a) e", e=128).rearrange(
                            "(kt p) e -> p kt e", p=128),
                        ldv[hs][part], [key], [("Va", hs, part)])

            def produce(st):
                kind, h, qc, kidx = st
                hs = h % 2
                slot = state["ring"] % 2
                state["ring"] += 1
                sp = SPAIR[slot]
                if kind == "qk":
                    blk, ktile = divmod(kidx + (0 if qc < 4 else 64), 16)
                    ks = slice(ktile * 128, (ktile + 1) * 128)
                    qs = slice(qc * 512, (qc + 1) * 512)
                    pi = state["ptp"] % 3
                    state["ptp"] += 1
                    for comp in range(2):
                        dsl = slice(comp * 64, (comp + 1) * 64)
                        PE(ps(sp[comp]), KTa[hs][dsl, blk, ks], Qa[hs][dsl, qs], True, True,
                           [("KTa", hs, 0 if qc < 4 else 1), ("Qa", hs)],
                           [psk(sp[comp])] + ([("PT", pi)] if comp == 0 else []))
                else:
                    par = qc % 2
                    for comp in range(2):
                        PE(ps(sp[comp]), ones_f, acc[par][:, comp * 512:(comp + 1) * 512], True, True,
                           [("acc", par), "ones_f"], [psk(sp[comp])])
                return slot

            deferred = []

            def part2(h, qc):
                par = qc % 2
                pv = PVPAIR[par]
                qs = slice(qc * 512, (qc + 1) * 512)
                PE(ps(pv[0]), ONES, sq3[par], True, True, [("sq3", par), "cb"], [psk(pv[0])])
                ACT(rstd3, ps(pv[0]), AF.Ln, [psk(pv[0]), "misc0"], ["rstd3"], scale=1.0 / 128, bias=misc[:, 0:1])
                ACT(rstd3, rstd3, AF.Exp, ["rstd3"], ["rstd3"], scale=-0.5)
                oi = state["on"] % 4
                state["on"] += 1
                STT(ONs[oi], obuf[par], misc[:, 2:3], rstd3, ALU.mult, ALU.mult, [("obuf", par), "rstd3", "m2"],
                    [("ONs", oi)])
                DMA("pool", ON[h][:, qs], ONs[oi], st_on[oi], [("ONs", oi)], [("ON", qc)])

            def consume(st, slot, idx):
                kind, h, qc, kidx = st
                hs = h % 2
                sp = SPAIR[slot]
                par = qc % 2
                pv = PVPAIR[par]
                if kind == "qk":
                    blk, ktile = divmod(kidx + (0 if qc < 4 else 64), 16)
                    nk = 64 if qc < 4 else 32
                    pi = state["pt"] % 3
                    state["pt"] += 1
                    ACT(PT[pi], pair_ap(sp), AF.Exp, pair_keys(sp) + ["m3", "m4"], [("PT", pi)], scale=0.125,
                        bias=misc[:, 3:4])
                    if kidx == 0:
                        CP(acc[par], PT[pi], [("PT", pi)], [("acc", par)])
                    else:
                        TT_(acc[par], acc[par], PT[pi], ALU.add, [("PT", pi), ("acc", par)], [("acc", par)])
                    for comp in range(2):
                        half = PT[pi][:, comp * 512:(comp + 1) * 512]
                        PE(ps(pv[comp]), Va[hs][:, blk, ktile, :], half, kidx == 0, kidx == nk - 1,
                           [("Va", hs, 0 if qc < 4 else 1), ("PT", pi)], [psk(pv[comp])])
                else:
                    ACT(rc, pair_ap(sp), AF.Ln, pair_keys(sp), ["rc"])
                    ACT(rc, rc, AF.Exp, ["rc"], ["rc"], scale=-1.0)
                    TT_(t01, pair_ap(pv), rc, ALU.mult, pair_keys(pv) + ["rc"], ["t01"])
                    STT(obuf[par], t01[:, 512:1024], misc[:, 1:2], t01[:, 0:512], ALU.mult, ALU.add, ["t01", "m1"],
                        [("obuf", par)])
                    ACT(sq3[par], obuf[par], AF.Square, [("obuf", par)], [("sq3", par)])
                    deferred.append((idx + 3, h, qc))

            load_head(0)
            if not small:
                issue_casts(len(cast_q))
            nst = len(steps)
            slots = {}
            slots[0] = produce(steps[0])
            for i in range(nst):
                st = steps[i]
                if st[0] == "qk" and st[2] == 0 and st[3] == 0 and st[1] + 1 < 8:
                    gatherA(st[1] + 1)
                    gatherB(st[1] + 1)
                    load_head(st[1] + 1)
                nxt = steps[i + 1] if i + 1 < nst else None
                late = nxt is not None and nxt[0] == "sum" and st[0] == "qk" and st[1:3] == nxt[1:3]
                if nxt is not None and not late:
                    slots[i + 1] = produce(nxt)
                consume(st, slots[i], i)
                if nxt is not None and late:
                    slots[i + 1] = produce(nxt)
                while deferred and deferred[0][0] <= i:
                    _, dh, dq = deferred.pop(0)
                    part2(dh, dq)
            while deferred:
                _, dh, dq = deferred.pop(0)
                part2(dh, dq)
            stop_point("P3")
            prog.barrier()
            release_phase_sems()

            af.top = afm
            ab.top = abm
            r = alloc_rowlocal(2)
            alloc_ffn(r)
            wco = ab.alloc(2, 8, 512)
            ont = [ab.alloc(8, 512) for _ in range(2)]
            ldo = [newsem() for _ in range(2)]
            ldw = newsem()
            st_y = [newsem(), newsem()]
            DMA("sp", wco, WCO.rearrange("g p (k c) -> p g k c", c=512), ldw, ["WCO"], ["wco"])
            load_x(r, X4, 0, 0, "X4")
            DMA("sp", ont[0], ON[:, :, 0:TT].rearrange("h p t -> p h t"), ldo[0], [("ON", qc) for qc in range(8)],
                [("ont", 0)])
            for t in range(NT):
                xs = t % 2
                if t + 1 < NT:
                    load_x(r, X4, t + 1, 1 - xs, "X4")
                    DMA("sp", ont[1 - xs], ON[:, :, (t + 1) * TT:(t + 2) * TT].rearrange("h p t -> p h t"), ldo[1 - xs],
                        [("ON", qc) for qc in range(8)], [("ont", 1 - xs)])
                for m in range(8):
                    b = m % 2
                    for c8 in range(8):
                        PE(ps(b), wco[:, m // 4, c8, (m % 4) * 128:(m % 4 + 1) * 128], ont[xs][:, c8, :], c8 == 0, c8 == 7,
                           ["wco", ("ont", xs)], [psk(b)])
                    if m >= 1:
                        norm_stat(r, xs, m - 1)
                    TT_(r.xres[xs][:, m, :], ps(b), r.xres[xs][:, m, :], ALU.add, [psk(b), ("xres", xs)], [("xres", xs)])
                norm_stat(r, xs, 7)
                rmsnorm(r, xs, 40, pre=True)
                ffn(r, 1, 1, xs)
                store_x(r, yT, t, xs, "yT", st_y)


        except _Stop:
            pass
        prog.analyze()
        with nc.Block() as block:
            @block.tensor
            def _(e):
                prog.emit("pe", e, esems)

            @block.scalar
            def _(e):
                prog.emit("act", e, esems)

            @block.vector
            def _(e):
                prog.emit("dve", e, esems)

            @block.gpsimd
            def _(e):
                prog.emit("pool", e, esems)

            @block.sync
            def _(e):
                prog.emit("sp", e, esems)
    return nc


def _rope_tables(half, rep, pos):
    inv = (np.float32(10000.0) ** (-np.arange(half, dtype=np.float32) / np.float32(half))).astype(np.float32)
    pos = np.asarray(pos, dtype=np.float32)
    ang = (pos[:, None] * inv[None, :]).astype(np.float32)
    cos = np.cos(ang).astype(np.float32).T
    sin = np.sin(ang).astype(np.float32).T
    C = np.concatenate([cos, cos], axis=0)
    S = np.concatenate([-sin, sin], axis=0)
    C = np.tile(C, (rep, 1))
    S = np.tile(S, (rep, 1))
    return np.ascontiguousarray(np.stack([C, S], axis=0))


def _consts():
    p = np.arange(128)
    ident = np.eye(128, dtype=np.float32)
    ones = np.ones((128, 128), np.float32)
    bones = np.zeros((128, 128), np.float32)
    bones[:64, :64] = 1
    bones[64:, 64:] = 1
    R0 = np.zeros((128, 128), np.float32)
    R0[(p + 64) % 128, p] = 1
    R1 = np.zeros((128, 128), np.float32)
    src = np.where((p % 64) < 32, p + 32, p - 32)
    R1[src, p] = 1
    j = p[:, None].astype(np.float32)
    i = np.arange(128)[None, :].astype(np.float32)
    M1 = np.maximum(i - j, 0)
    M2 = np.maximum(j - i, 0)
    MLE = (j <= i).astype(np.float32)
    MGT = (j > i).astype(np.float32)
    IO1 = np.broadcast_to(i + 1, (128, 128))
    IOB = np.broadcast_to(128 - i, (128, 128))
    t4 = lambda a: np.tile(a, (1, 4))
    colA = (127 - p).astype(np.float32)[:, None]
    colB = p.astype(np.float32)[:, None]
    ch = np.broadcast_to((128.0 * np.arange(32))[None, :], (128, 32)).astype(np.float32)
    cst = np.concatenate([ident, ones, bones, R0, R1, t4(M1), t4(M2), t4(MLE), t4(MGT), t4(IO1), t4(IOB),
                          colA, colB, ch], axis=1).astype(np.float32)
    tpos = np.broadcast_to(((np.arange(T) % 2048).astype(np.float32) + 1.0)[None, :], (128, T))
    return np.ascontiguousarray(cst), np.ascontiguousarray(tpos)


_NC_CACHE = {}


def _host_inputs(inputs):
    f = lambda a: np.ascontiguousarray(np.asarray(a, dtype=np.float32))
    xp = f(inputs["x_prompt"])
    xsm = f(inputs["x_sample"])
    norm_g = f(inputs["norm_g"])
    gvec = np.zeros((128, 64), np.float32)
    for l in range(2):
        for i in range(3):
            gvec[:, (l * 3 + i) * 8:(l * 3 + i) * 8 + 8] = norm_g[l, i].reshape(8, 128).T
    gvec[:, 48:52] = f(inputs["ab_ret_norm_g"])[0].reshape(4, 128).T
    gvec[:, 52:56] = f(inputs["ab_conv_b"])[0].reshape(4, 128).T
    gvec[:, 56:60] = f(inputs["ab_conv_norm_g"])[0].reshape(4, 128).T
    gq = f(inputs["c_q_norm_g"])[0]
    gk = f(inputs["c_k_norm_g"])[0]
    gvec[:, 60] = np.tile(gq, 2)
    gvec[:, 61] = np.tile(gk, 2)
    gvec[:, 62] = f(inputs["c_subln_g"])[0]
    cw = f(inputs["ab_conv_w"])[0]
    convw = np.ascontiguousarray(cw.T.reshape(4, 128, 31).transpose(1, 0, 2).reshape(128, 124))
    decay = np.ascontiguousarray(np.broadcast_to(f(inputs["ab_decay"])[0].reshape(1, 8), (128, 8)))
    lam = np.ascontiguousarray(np.broadcast_to(f(inputs["c_lambda"])[0].reshape(1, 256), (128, 256)))
    gqk = np.ascontiguousarray(np.broadcast_to(np.concatenate([gq, gk])[None, :], (128, 128)))
    cst, tpos = _consts()
    shared = {
        "ffn_w_in": f(inputs["ffn_w_in"]), "ffn_w_out": f(inputs["ffn_w_out"]),
        "ab_w_in": f(inputs["ab_w_in"])[0], "ab_w_out": f(inputs["ab_w_out"])[0],
        "c_w_in": f(inputs["c_w_in"])[0], "c_w_out": f(inputs["c_w_out"])[0],
        "gvec": gvec, "convw": convw, "decay": decay, "lam": lam, "gqk": gqk, "cst": cst, "tpos": tpos,
    }
    in_maps = []
    SEG = 2048
    for c in range(8):
        g, j = divmod(c, 4)
        hb = j % 2
        pidx = 2 * g + j // 2
        segA = xsm[g, j * SEG:(j + 1) * SEG]
        segB = xp[pidx, hb * SEG:(hb + 1) * SEG]
        xt = np.concatenate([segA, segB], axis=0).T
        pos = np.concatenate([j * SEG + np.arange(SEG), hb * SEG + np.arange(SEG)]).astype(np.float32)
        tab = np.zeros(64, np.float32)
        for i in range(4):
            if i < j:
                tab[0 + i] = SEG * (j - 1 - i)
                tab[16 + 0 + i] = 1.0
            if i > j:
                tab[4 + i] = SEG * (i - j - 1)
                tab[16 + 4 + i] = 1.0
            if hb == 1 and i == j - 1:
                tab[16 + 8 + i] = 1.0
            if hb == 0 and i == j + 1:
                tab[16 + 12 + i] = 1.0
            if i == j - 1:
                tab[32 + 0 + i] = 1.0
            if i == j + 1:
                tab[32 + 4 + i] = 1.0
            if hb == 1 and i == j - 1:
                tab[32 + 8 + i] = 1.0
            if hb == 0 and i == j + 1:
                tab[32 + 12 + i] = 1.0
        m = dict(shared)
        m["xT"] = np.ascontiguousarray(xt)
        m["rope0"] = _rope_tables(64, 1, pos)
        m["rope1"] = _rope_tables(32, 2, pos)
        m["cmask"] = np.ascontiguousarray(np.broadcast_to(tab[None, :], (128, 64)))
        in_maps.append(m)
    return in_maps


def kernel(**inputs):
    in_maps = _host_inputs(inputs)
    if "nc" not in _NC_CACHE:
        _NC_CACHE["nc"] = build_program()
    nc = _NC_CACHE["nc"]
    res = run_bass_kernel_spmd(nc, in_maps, core_ids=list(range(8)))
    outs = [np.asarray(res.results[c]["yT"], dtype=np.float32) for c in range(8)]
    SEG = 2048
    y_prompt = np.zeros((4, 4096, D), np.float32)
    y_sample = np.zeros((2, 8192, D), np.float32)
    for c in range(8):
        g, j = divmod(c, 4)
        hb = j % 2
        pidx = 2 * g + j // 2
        o = outs[c].T
        y_sample[g, j * SEG:(j + 1) * SEG] = o[0:SEG]
        y_prompt[pidx, hb * SEG:(hb + 1) * SEG] = o[SEG:2 * SEG]
    return (y_prompt, y_sample)
```

```python
import math
from contextlib import ExitStack

import numpy as np
import concourse.bass as bass
import concourse.mybir as mybir
from concourse.bass_utils import run_bass_kernel_spmd

F32 = mybir.dt.float32
BF16 = mybir.dt.bfloat16
ALU = mybir.AluOpType
AF = mybir.ActivationFunctionType
AX = mybir.AxisListType

D = 1024
T = 4096
TT = 512
NT = T // TT
DFF = 2816
NJ = DFF // 128
EPS = 1e-6
LAMBDA_INIT = 0.8 - 0.6 * math.exp(-0.3 * 1)
NF = 17 * 1024
NB = 69 * 1024
NEG = -30000.0


class Op:
    __slots__ = ("eng", "fn", "reads", "writes", "dma", "deps", "sig", "val", "eidx", "idx", "persist")


class Prog:
    def __init__(self):
        self.ops = []

    def add(self, eng, fn, reads=(), writes=(), dma=None, persist=False):
        op = Op()
        op.persist = persist
        op.eng = eng
        op.fn = fn
        reads = tuple(reads)
        op.reads = reads
        op.writes = tuple(writes) + tuple(r for r in reads if isinstance(r, tuple) and r[0] == "ps")
        op.dma = dma
        op.deps = set()
        op.sig = False
        op.val = 0
        self.ops.append(op)
        return op

    def barrier(self):
        self.ops.append(None)

    def analyze(self):
        last_writer = {}
        pers_writer = {}
        readers = {}
        last_on_eng = {}
        last_async = {}
        pending = {}
        ecount = {}
        real = []
        for op in self.ops:
            if op is None:
                bd = set(last_on_eng.values()) | set(last_async.values())
                for e in ("pe", "act", "dve", "pool", "sp"):
                    pending[e] = set(bd) | pending.get(e, set())
                last_writer = {}
                readers = {}
                continue
            op.idx = len(real)
            real.append(op)
            op.eidx = ecount.get(op.eng, 0)
            ecount[op.eng] = op.eidx + 1
            deps = set()
            for r in op.reads:
                w = last_writer.get(r)
                if w is not None:
                    deps.add(w)
                w = pers_writer.get(r)
                if w is not None:
                    deps.add(w)
            if op.persist:
                for w_ in op.writes:
                    pers_writer[w_] = op
                op.deps = deps
                continue
            for w_ in op.writes:
                w = last_writer.get(w_)
                if w is not None:
                    deps.add(w)
                for rd in readers.get(w_, ()):
                    deps.add(rd)
            if op.eng in pending:
                deps |= pending.pop(op.eng)
            deps.discard(op)
            op.deps = deps
            for r in op.reads:
                readers.setdefault(r, []).append(op)
            for w_ in op.writes:
                last_writer[w_] = op
                readers[w_] = []
            if op.dma is not None:
                last_async[id(op.dma[0])] = op
            else:
                last_on_eng[op.eng] = op
        self.real = real
        for op in real:
            keep = set()
            for d in op.deps:
                if d.dma is None and d.eng == op.eng:
                    if op.dma is None or True:
                        if op.eng == "pe" or op.eng == "sp":
                            continue
                        if op.eidx - d.eidx > 2:
                            continue
                keep.add(d)
            op.deps = keep
            for d in keep:
                if d.dma is None:
                    d.sig = True
        cnt = {}
        acnt = {}
        for op in real:
            if op.dma is not None:
                k = id(op.dma[0])
                acnt[k] = acnt.get(k, 0) + op.dma[1]
                op.val = acnt[k]
            elif op.sig:
                cnt[op.eng] = cnt.get(op.eng, 0) + 1
                op.val = cnt[op.eng]
        self.final_async = {}
        for op in real:
            if op.dma is not None:
                self.final_async[id(op.dma[0])] = (op.dma[0], op.val, op.eng)

    def emit(self, engname, e, esems):
        waited = {}
        for op in self.real:
            if op.eng != engname:
                continue
            need = {}
            for d in op.deps:
                if d.dma is not None:
                    s = d.dma[0]
                else:
                    s = esems[d.eng]
                k = id(s)
                if k not in need or need[k][1] < d.val:
                    need[k] = (s, d.val)
            for k, (s, v) in need.items():
                if waited.get(k, 0) < v:
                    e.wait_ge(s, v)
                    waited[k] = v
            ins = op.fn(e)
            if op.dma is not None:
                ins.then_inc(op.dma[0], op.dma[1])
            elif op.sig:
                ins.then_inc(esems[op.eng], 1)
        for k, (s, v, eng) in self.final_async.items():
            if eng == engname and waited.get(k, 0) < v:
                e.wait_ge(s, v)


class _Stop(Exception):
    pass


class Arena:
    def __init__(self, t, n):
        self.t = t
        self.n = n
        self.top = 0

    def alloc(self, *shape):
        size = int(np.prod(shape))
        off = self.top
        self.top += size
        assert self.top <= self.n, ("arena overflow", self.top, self.n)
        ap = self.t[:, off:off + size]
        if len(shape) == 2:
            ap = ap.rearrange("p (a b) -> p a b", b=shape[1])
        elif len(shape) == 3:
            ap = ap.rearrange("p (a b c) -> p a b c", b=shape[1], c=shape[2])
        return ap


def build_program(dbg=None):
    nc = bass.Bass("TRN2", target_bir_lowering=False)

    def din(name, shape, dt=F32):
        return nc.dram_tensor(name, list(shape), dt, kind="ExternalInput").ap()

    def dscr(name, shape, dt=BF16):
        kind = "ExternalOutput" if (dbg and name in dbg) else None
        if kind:
            return nc.dram_tensor(name, list(shape), dt, kind=kind).ap()
        return nc.dram_tensor(name, list(shape), dt).ap()

    small = bool((dbg or {}).get("small"))
    xT = din("xT", [D, T])
    if not small:
        w_ffn_in = din("ffn_w_in", [2, 2, D, 2 * DFF])
        w_ffn_out = din("ffn_w_out", [2, 2, DFF, D])
        w_ab_in = din("ab_w_in", [D, 3072])
        w_ab_out = din("ab_w_out", [D, D])
        w_c_in = din("c_w_in", [D, 3072])
        w_c_out = din("c_w_out", [D, D])
    gvec_d = din("gvec", [128, 64])
    convw_d = din("convw", [128, 124])
    decay_d = din("decay", [128, 8])
    lam_d = din("lam", [128, 256])
    gqk_d = din("gqk", [128, 128])
    rope0_d = din("rope0", [2, 128, T])
    rope1_d = din("rope1", [2, 128, T])
    cmask_d = din("cmask", [128, 64])
    cst_d = din("cst", [128, 5 * 128 + 6 * 512 + 2 + 32])
    tpos_d = din("tpos", [128, T])
    yT = nc.dram_tensor("yT", [D, T], F32, kind="ExternalOutput").ap()

    WIN = [[dscr(f"WIN{l}{i}", [NJ, 128, 8 * 256]) for i in range(2)] for l in range(2)]
    WOUT = [[dscr(f"WOUT{l}{i}", [8, 128, NJ * 128]) for i in range(2)] for l in range(2)]
    WAB = dscr("WAB", [6, 128, 8 * 512])
    WABO = dscr("WABO", [2, 128, 8 * 512])
    WC = dscr("WC", [6, 128, 8 * 512])
    WCO = dscr("WCO", [2, 128, 8 * 512])
    X1 = dscr("X1", [D, T], F32)
    X2 = dscr("X2", [D, T], F32)
    X4 = dscr("X4", [D, T], F32)
    QT0 = dscr("QT0", [512, T])
    KT0 = dscr("KT0", [512, T])
    V0 = dscr("V0", [T, 512])
    G0 = dscr("G0", [512, T])
    U0 = dscr("U0", [512, 2 * 2080])
    FS = dscr("FS", [NT, 128, 16 * 128])
    EXin = [dscr(f"EXin{i}", [512, 288], F32) for i in range(2)]
    EXout = [dscr(f"EXout{i}", [2048, 288], F32) for i in range(2)]
    QT1 = dscr("QT1", [8, 128, T])
    KVinA = [dscr(f"KVinA{h}", [256, 2048]) for h in range(8)]
    KVinB = [dscr(f"KVinB{h}", [256, 2048]) for h in range(8)]
    KVoutA = [dscr(f"KVoutA{h}", [1024, 2048]) for h in range(8)]
    KVoutB = [dscr(f"KVoutB{h}", [512, 2048]) for h in range(8)]
    ON = dscr("ON", [8, 128, T])

    es = ExitStack()
    with es:
        AFt = es.enter_context(nc.sbuf_tensor("AF", [128, NF], F32))
        ABt = es.enter_context(nc.sbuf_tensor("AB", [128, NB], BF16))
        PSall = es.enter_context(nc.psum_tensor("psall", [128, 4096], F32))
        PS = [PSall[:, i * 512:(i + 1) * 512] for i in range(8)]
        esems = {e: es.enter_context(nc.semaphore("s_" + e)) for e in ("pe", "act", "dve", "pool", "sp")}
        nsem = [0]

        free_sems = {False: [], True: []}
        phase_sems = {False: [], True: []}

        def newsem(persist=False, sw=False):
            if not persist and free_sems[sw]:
                sm = free_sems[sw].pop()
            else:
                nsem[0] += 1
                sm = es.enter_context(nc.semaphore(f"d{nsem[0]}"))
            if not persist:
                phase_sems[sw].append(sm)
            return sm

        def release_phase_sems():
            for k_ in (False, True):
                free_sems[k_].extend(phase_sems[k_])
                del phase_sems[k_][:]

        af = Arena(AFt, NF)
        ab = Arena(ABt, NB)
        prog = Prog()
        stop_at = (dbg or {}).get("stop")

        def stop_point(name):
            if stop_at == name:
                raise _Stop()

        def PE(out, lhsT, rhs, start, stop, R, W):
            prog.add("pe", lambda e: e.matmul(out, lhsT=lhsT, rhs=rhs, start=start, stop=stop), R, W)

        def ACT(out, in_, func, R, W, scale=None, bias=None, eng="act"):
            kw = {}
            if scale is not None:
                kw["scale"] = scale
            if bias is not None:
                kw["bias"] = bias
            prog.add(eng, lambda e: e.activation(out, in_, func, **kw), R, W)

        def TT_(out, a, b, op, R, W, eng="dve"):
            prog.add(eng, lambda e: e.tensor_tensor(out, a, b, op), R, W)

        def TS(out, a, s1, s2, op0, op1, R, W, eng="dve"):
            if s2 is None:
                prog.add(eng, lambda e: e.tensor_scalar(out, a, s1, None, op0), R, W)
            else:
                prog.add(eng, lambda e: e.tensor_scalar(out, a, s1, s2, op0, op1), R, W)

        def STT(out, a, s, b, op0, op1, R, W, eng="dve"):
            prog.add(eng, lambda e: e.scalar_tensor_tensor(out, a, s, b, op0, op1), R, W)

        def CP(out, in_, R, W, eng="dve"):
            prog.add(eng, lambda e: e.tensor_copy(out, in_), R, W)

        def DMA(eng, out, in_, sem, R, W, persist=False):
            prog.add(eng, lambda e: e.dma_start(out=out, in_=in_), R, W, dma=(sem, 16), persist=persist)

        def ps(b):
            return PS[b][:, :]

        def psk(b):
            return ("ps", b)

        try:
            def cast_group(key, pairs):
                s = newsem()
                n = len(pairs)
                for i, (dst, src) in enumerate(pairs):
                    DMA("pool", dst, src, s, [], [key] if i == n - 1 else [])

            def win_pairs(l, i):
                prs = []
                for j in range(NJ):
                    for half in range(2):
                        src = w_ffn_in[l, i][:, half * DFF + j * 128: half * DFF + (j + 1) * 128].rearrange(
                            "(k p) c -> p k c", p=128)
                        dst = WIN[l][i][j].rearrange("p (k c) -> p k c", c=256)[:, :, half * 128:(half + 1) * 128]
                        prs.append((dst, src))
                return prs

            def wout_pairs(l, i):
                prs = []
                for m in range(8):
                    src = w_ffn_out[l, i][:, m * 128:(m + 1) * 128].rearrange("(j p) c -> p j c", p=128)
                    dst = WOUT[l][i][m].rearrange("p (j c) -> p j c", c=128)
                    prs.append((dst, src))
                return prs

            def proj_pairs(dst_t, src_w, ngrp):
                prs = []
                for g in range(ngrp):
                    for hh in range(2):
                        src = src_w[:, g * 512 + hh * 256: g * 512 + (hh + 1) * 256].rearrange("(k p) c -> p k c", p=128)
                        dst = dst_t[g].rearrange("p (k c) -> p k c", c=512)[:, :, hh * 256:(hh + 1) * 256]
                        prs.append((dst, src))
                return prs

            cast_q = []

            def cast_enqueue(key, pairs):
                sm = newsem(persist=True)
                n = len(pairs)
                for i, (dst, src) in enumerate(pairs):
                    cast_q.append((dst, src, sm, [key] if i == n - 1 else []))

            def issue_casts(n):
                for _ in range(min(n, len(cast_q))):
                    dst, src, sm, wk = cast_q.pop(0)
                    DMA("pool", dst, src, sm, [], wk, persist=True)

            if not small:
                wp = win_pairs(0, 0)
                for q4 in range(4):
                    cast_enqueue(("WIN", 0, 0, q4), wp[q4 * 12:(q4 + 1) * 12] if q4 < 3 else wp[36:])
                cast_enqueue(("WOUT", 0, 0), wout_pairs(0, 0))
                cast_enqueue("WAB", proj_pairs(WAB, w_ab_in, 6))
                issue_casts(len(cast_q))
                cast_enqueue("WABO", proj_pairs(WABO, w_ab_out, 2))
                cast_enqueue(("WIN", 0, 1), win_pairs(0, 1))
                cast_enqueue(("WOUT", 0, 1), wout_pairs(0, 1))
                cast_enqueue(("WIN", 1, 0), win_pairs(1, 0))
                cast_enqueue(("WOUT", 1, 0), wout_pairs(1, 0))
                cast_enqueue("WC", proj_pairs(WC, w_c_in, 6))
                cast_enqueue("WCO", proj_pairs(WCO, w_c_out, 2))
                cast_enqueue(("WIN", 1, 1), win_pairs(1, 1))
                cast_enqueue(("WOUT", 1, 1), wout_pairs(1, 1))
            stop_point("cast")

            gvec = af.alloc(64)
            convw = af.alloc(124)
            lg = af.alloc(8)
            cd = af.alloc(8)
            kcol = af.alloc(8)
            wbc = af.alloc(128)
            misc = af.alloc(16)
            cmask = af.alloc(64)
            identf = af.alloc(128)
            cb = ab.alloc(5, 128)
            Dt = ab.alloc(4, 512)
            rowf = ab.alloc(4, 512)
            rowb = ab.alloc(4, 512)
            IDENT, ONES, BONES, R0, R1 = (cb[:, i, :] for i in range(5))
            afm = af.top
            abm = ab.top

            s_c = newsem()
            cst = af.alloc(5 * 128 + 6 * 512 + 2 + 32)
            decay_t = af.alloc(8)
            lam_t = af.alloc(256)
            gqk_t = af.alloc(128)
            stmp = af.alloc(4, 512)
            for dst, src, key in ((gvec, gvec_d, "gvec"), (convw, convw_d, "convw"), (cst, cst_d, "cst"),
                                  (decay_t, decay_d, "decay_t"), (lam_t, lam_d, "lam_t"), (gqk_t, gqk_d, "gqk_t"),
                                  (cmask, cmask_d, "cmask")):
                DMA("sp", dst, src[:, :], newsem(), [], [key])
            o = 5 * 128
            M1 = cst[:, o:o + 512]
            M2 = cst[:, o + 512:o + 1024]
            MLE = cst[:, o + 1024:o + 1536]
            MGT = cst[:, o + 1536:o + 2048]
            IO1 = cst[:, o + 2048:o + 2560]
            IOB = cst[:, o + 2560:o + 3072]
            o2 = o + 3072
            COLA = cst[:, o2:o2 + 1]
            COLB = cst[:, o2 + 1:o2 + 2]
            CH128 = cst[:, o2 + 2:o2 + 34]
            for i in range(5):
                CP(cb[:, i, :], cst[:, i * 128:(i + 1) * 128], ["cst"], ["cb"])
            CP(identf, cst[:, 0:128], ["cst"], ["identf"])
            prog.add("dve", lambda e: e.memset(misc[:, 0:1], EPS), [], ["misc0"])
            stop_point("s1")
            ACT(lg, decay_t, AF.Exp, ["decay_t"], ["lg"])
            TS(lg, lg, -1.0, None, ALU.mult, None, ["lg"], ["lg"])
            ACT(cd, lg, AF.Exp, ["lg"], ["cd"], scale=128.0)
            SC = 128.0 ** -0.5
            for h in range(4):
                ACT(kcol[:, h:h + 1], COLA, AF.Exp, ["lg", "cst"], ["kcol"], scale=lg[:, h:h + 1])
                ACT(kcol[:, 4 + h:5 + h], COLB, AF.Exp, ["lg", "cst"], ["kcol"], scale=lg[:, 4 + h:5 + h])
                ACT(wbc[:, h * 32:(h + 1) * 32], CH128, AF.Exp, ["lg", "cst"], ["wbc"], scale=lg[:, 4 + h:5 + h])
                ACT(rowf[:, h, :], IO1, AF.Exp, ["lg", "cst"], ["rowf"], scale=lg[:, h:h + 1])
                ACT(rowb[:, h, :], IOB, AF.Exp, ["lg", "cst"], ["rowb"], scale=lg[:, 4 + h:5 + h])
            TS(kcol, kcol, SC, None, ALU.mult, None, ["kcol"], ["kcol"])
            stop_point("s2")
            TT_(stmp[:, 2, 0:64], lam_t[:, 0:64], lam_t[:, 64:128], ALU.mult, ["lam_t"], ["st2"])
            prog.add("dve", lambda e: e.reduce_sum(misc[:, 8:9], stmp[:, 2, 0:64], axis=AX.X), ["st2"], ["m8"])
            TT_(stmp[:, 2, 64:128], lam_t[:, 128:192], lam_t[:, 192:256], ALU.mult, ["lam_t"], ["st2b"])
            prog.add("dve", lambda e: e.reduce_sum(misc[:, 9:10], stmp[:, 2, 64:128], axis=AX.X), ["st2b"], ["m9"])
            ACT(misc[:, 8:10], misc[:, 8:10], AF.Exp, ["m8", "m9"], ["m89"])
            TT_(misc[:, 1:2], misc[:, 9:10], misc[:, 8:9], ALU.subtract, ["m89"], ["m1"])
            TS(misc[:, 1:2], misc[:, 1:2], -LAMBDA_INIT, None, ALU.add, None, ["m1"], ["m1"])
            TS(misc[:, 2:3], gvec[:, 62:63], 1.0 - LAMBDA_INIT, None, ALU.mult, None, ["gvec"], ["m2"])
            stop_point("s3")
            prog.add("dve", lambda e: e.reduce_max(misc[:, 10:11], gqk_t[:, 0:64], axis=AX.X, apply_absolute_value=True),
                     ["gqk_t"], ["m10"])
            prog.add("dve", lambda e: e.reduce_max(misc[:, 11:12], gqk_t[:, 64:128], axis=AX.X, apply_absolute_value=True),
                     ["gqk_t"], ["m11"])
            TT_(misc[:, 10:11], misc[:, 10:11], misc[:, 11:12], ALU.mult, ["m10", "m11"], ["m10"])
            STT(misc[:, 3:4], misc[:, 10:11], -8.0, cmask[:, 48:49], ALU.mult, ALU.add, ["m10", "cmask"], ["m3"])
            STT(misc[:, 4:5], misc[:, 10:11], -8.0, cmask[:, 49:50], ALU.mult, ALU.add, ["m10", "cmask"], ["m4"])
            stop_point("s4")
            for h in range(4):
                ACT(stmp[:, 0, :], M1, AF.Exp, ["lg", "cst"], ["st0"], scale=lg[:, h:h + 1])
                TT_(stmp[:, 0, :], stmp[:, 0, :], MLE, ALU.mult, ["st0", "cst"], ["st0"])
                ACT(stmp[:, 1, :], M2, AF.Exp, ["lg", "cst"], ["st1"], scale=lg[:, 4 + h:5 + h])
                TT_(stmp[:, 1, :], stmp[:, 1, :], MGT, ALU.mult, ["st1", "cst"], ["st1"])
                TT_(stmp[:, 0, :], stmp[:, 0, :], stmp[:, 1, :], ALU.add, ["st0", "st1"], ["st0"])
                TS(Dt[:, h, :], stmp[:, 0, :], SC, None, ALU.mult, None, ["st0"], ["Dt"])
            stop_point("setup")
            prog.barrier()
            release_phase_sems()

            class RL:
                pass

            def alloc_rowlocal(nx=2, with_sg=True):
                r = RL()
                r.xres = [af.alloc(8, 512) for _ in range(nx)]
                r.xsem = [newsem() for _ in range(nx)]
                r.rstd = [af.alloc(512) for _ in range(2)]
                r.sg = [af.alloc(512) for _ in range(2)] if with_sg else None
                r.tmp = [af.alloc(512) for _ in range(3)]
                r.xn = ab.alloc(8, 512)
                r.sq = [ab.alloc(512) for _ in range(4)]
                r.cnt = {"rstd": 0, "sq": 0, "sg": 0, "tmp": 0}
                return r

            def alloc_ffn(r):
                r.abuf = ab.alloc(NJ, 512)
                r.win = [ab.alloc(8, 256) for _ in range(3)]
                r.winsem = [newsem() for _ in range(3)]
                r.wout = [ab.alloc(NJ, 128) for _ in range(3)]
                r.woutsem = [newsem() for _ in range(3)]
                r.nwin = 0
                r.nwout = 0

            def rr(r, name, n):
                i = r.cnt[name] % n
                r.cnt[name] += 1
                return i

            def rstd_from(r, psb, scale, W):
                ri = rr(r, "rstd", 2)
                ACT(r.rstd[ri], ps(psb), AF.Ln, [psk(psb), "misc0"], [("rstd", ri)], scale=scale, bias=misc[:, 0:1])
                ACT(r.rstd[ri], r.rstd[ri], AF.Exp, [("rstd", ri)], [("rstd", ri)], scale=-0.5)
                return ri

            def norm_stat(r, xs, k):
                si = rr(r, "sq", 4)
                ACT(r.sq[si], r.xres[xs][:, k, :], AF.Square, [("xres", xs)], [("sq", si)])
                PE(ps(6), ONES, r.sq[si], k == 0, k == 7, [("sq", si), "cb"], [psk(6)])

            def rmsnorm(r, xs, gbase, pre=False):
                xk = ("xres", xs)
                if not pre:
                    for k in range(8):
                        norm_stat(r, xs, k)
                ri = rstd_from(r, 6, 1.0 / D, None)
                for k in range(8):
                    STT(r.xn[:, k, :], r.xres[xs][:, k, :], gvec[:, gbase + k:gbase + k + 1], r.rstd[ri],
                        ALU.mult, ALU.mult, [xk, ("rstd", ri), "gvec"], [("xn", k)])

            XN_ALL = [("xn", k) for k in range(8)]

            def ffn(r, l, i, xs, fuse_norm=False):
                xk = ("xres", xs)
                wkey = ("WIN", l, i)
                okey = ("WOUT", l, i)

                def load_wout(m):
                    sl = r.nwout % 3
                    r.nwout += 1
                    DMA("sp", r.wout[sl], WOUT[l][i][m].rearrange("p (j c) -> p j c", c=128), r.woutsem[sl],
                        [okey], [("wout", sl)])
                    return sl

                def load_win(j):
                    sl = r.nwin % 3
                    r.nwin += 1
                    wk = ("WIN", 0, 0, min(j // 6, 3)) if (l, i) == (0, 0) else wkey
                    DMA("sp", r.win[sl], WIN[l][i][j].rearrange("p (k c) -> p k c", c=256), r.winsem[sl],
                        [wk], [("win", sl)])
                    return sl

                wsl = {}
                wsl[0] = load_win(0)
                wsl[1] = load_win(1)
                osl = {}
                for j in range(NJ):
                    if j + 2 < NJ:
                        wsl[j + 2] = load_win(j + 2)
                    if j == NJ - 4:
                        osl[0] = load_wout(0)
                    if j == NJ - 2:
                        osl[1] = load_wout(1)
                    sl = wsl[j]
                    bg = j % 2
                    bu = 2 + j % 2
                    for half, b in ((0, bg), (1, bu)):
                        for k in range(8):
                            PE(ps(b), r.win[sl][:, k, half * 128:(half + 1) * 128], r.xn[:, k, :], k == 0, k == 7,
                               [("win", sl), ("xn", k)], [psk(b)])
                    si = rr(r, "sg", 2)
                    ACT(r.sg[si], ps(bg), AF.Silu, [psk(bg)], [("sg", si)])
                    TT_(r.abuf[:, j, :], r.sg[si], ps(bu), ALU.mult, [("sg", si), psk(bu)], [("abuf", j)])
                for m in range(8):
                    if m + 2 < 8:
                        osl[m + 2] = load_wout(m + 2)
                    sl = osl[m]
                    b = 4 + m % 2
                    for j in range(NJ):
                        PE(ps(b), r.wout[sl][:, j, :], r.abuf[:, j, :], j == 0, j == NJ - 1,
                           [("wout", sl), ("abuf", j)], [psk(b)])
                    if fuse_norm and m >= 1:
                        norm_stat(r, xs, m - 1)
                    STT(r.xres[xs][:, m, :], ps(b), 0.5, r.xres[xs][:, m, :], ALU.mult, ALU.add,
                        [psk(b), xk], [xk])
                if fuse_norm:
                    norm_stat(r, xs, 7)

            def load_x(r, src, t, xs, key):
                DMA("sp", r.xres[xs], src[:, t * TT:(t + 1) * TT].rearrange("(k p) t -> p k t", p=128), r.xsem[xs],
                    [(key, t)], [("xres", xs)])

            def store_x(r, dst, t, xs, key, sem):
                DMA("pool", dst[:, t * TT:(t + 1) * TT].rearrange("(k p) t -> p k t", p=128), r.xres[xs], sem[xs],
                    [("xres", xs)], [(key, t)])

            def run_items(items):
                n = len(items)
                ms = max(len(x) for x in items)
                for step in range(n + ms - 1):
                    for sidx in range(ms):
                        i = step - sidx
                        if 0 <= i < n and sidx < len(items[i]):
                            items[i][sidx]()

            af.top = afm
            ab.top = abm
            r = alloc_rowlocal(2)
            alloc_ffn(r)
            rope = af.alloc(2, 512)
            rope_sem = newsem()
            Fst = af.alloc(4, 128)
            Btot = af.alloc(4, 128)
            sigb = af.alloc(4, 512)
            edge = [af.alloc(4, 32) for _ in range(2)]
            wproj = [ab.alloc(8, 512) for _ in range(2)]
            wprojsem = [newsem() for _ in range(2)]
            QTs = ab.alloc(4, 512)
            KTs = ab.alloc(4, 512)
            Gs = ab.alloc(4, 512)
            Us = ab.alloc(4, 512)
            Vs = ab.alloc(4, 512)
            tmpb = [ab.alloc(512) for _ in range(2)]
            Kf = ab.alloc(4, 128)
            Kb = ab.alloc(4, 128)
            FSs = ab.alloc(16, 128)
            st_sem = {k: newsem(sw=True) for k in ("qt", "kt", "g", "u", "v", "fs", "ex")}
            st_sem["x1"] = [newsem(sw=True), newsem(sw=True)]
            ex_sem = [newsem(persist=True) for _ in range(2)]
            nproj = [0]

            def load_wproj(Wt, key, g):
                sl = nproj[0] % 2
                nproj[0] += 1
                DMA("sp", wproj[sl], Wt[g].rearrange("p (k c) -> p k c", c=512), wprojsem[sl], [key], [("wproj", sl)])
                return sl

            def proj_fm(sl, ft, b):
                for k in range(8):
                    PE(ps(b), wproj[sl][:, k, ft * 128:(ft + 1) * 128], r.xn[:, k, :], k == 0, k == 7,
                       [("wproj", sl), ("xn", k)], [psk(b)])

            def proj_tm(sl, c, b):
                for k in range(8):
                    PE(ps(b), r.xn[:, k, c * 128:(c + 1) * 128], wproj[sl][:, k, :], k == 0, k == 7,
                       [("wproj", sl), ("xn", k)], [psk(b)])

            def rope_apply(b, Rm, out_ap, outkey, ropekey):
                tb = rr(r, "tmp", 3)
                bi = r.cnt["sg"] % 2
                r.cnt["sg"] += 1
                ACT(tmpb[bi], ps(b), AF.Copy, [psk(b)], [("tmpb", bi)])
                PE(ps(7), Rm, tmpb[bi], True, True, [("tmpb", bi), "cb"], [psk(7)])
                t1 = r.tmp[tb]
                TT_(t1, ps(b), rope[:, 0, :], ALU.mult, [psk(b), ropekey], [("tmp", tb)])
                tb2 = rr(r, "tmp", 3)
                t2 = r.tmp[tb2]
                TT_(t2, ps(7), rope[:, 1, :], ALU.mult, [psk(7), ropekey], [("tmp", tb2)])
                TT_(out_ap, t1, t2, ALU.add, [("tmp", tb), ("tmp", tb2)], [outkey])

            prog.add("dve", lambda e: e.memset(Fst, 0.0), [], ["Fst"])
            prog.add("dve", lambda e: e.memset(Btot, 0.0), [], ["Btot"])
            prog.add("dve", lambda e: e.memset(edge[0], 0.0), [], [("edge", 0)])
            prog.add("dve", lambda e: e.memset(edge[1], 0.0), [], [("edge", 1)])
            load_x(r, xT, 0, 0, "xin")
            for t in range(NT):
                xs = t % 2
                if t + 1 < NT:
                    load_x(r, xT, t + 1, 1 - xs, "xin")
                rmsnorm(r, xs, 0)
                ffn(r, 0, 0, xs, fuse_norm=True)
                DMA("sp", rope, rope0_d[:, :, t * TT:(t + 1) * TT].rearrange("a p t -> p a t"), rope_sem, [], ["rope"])
                store_x(r, X1, t, xs, "X1", st_sem["x1"])
                rmsnorm(r, xs, 8, pre=True)
                GORD = [1, 2, 0, 3, 5, 4]
                gslot = {}
                gslot[0] = load_wproj(WAB, "WAB", GORD[0])
                gslot[1] = load_wproj(WAB, "WAB", GORD[1])
                pc = {"pb": 0, "rb": 0, "t1": 0}

                def mk_proj(gi, ft, first, tm=False):
                    st = {}

                    def s0():
                        if first and gi + 1 < 6 and gi >= 1:
                            gslot[gi + 1] = load_wproj(WAB, "WAB", GORD[gi + 1])
                        st["b"] = pc["pb"] % 3
                        pc["pb"] += 1
                        if tm:
                            proj_tm(gslot[gi], ft, st["b"])
                        else:
                            proj_fm(gslot[gi], ft, st["b"])
                    return st, s0

                def rope_item(gi, h, first, out_t, okey):
                    st, s0 = mk_proj(gi, h, first)

                    def s1():
                        b = st["b"]
                        bi = r.cnt["sg"] % 2
                        r.cnt["sg"] += 1
                        st["rb"] = 3 + pc["rb"] % 2
                        pc["rb"] += 1
                        st["t1"] = pc["t1"] % 2
                        pc["t1"] += 1
                        ACT(tmpb[bi], ps(b), AF.Copy, [psk(b)], [("tmpb", bi)])
                        PE(ps(st["rb"]), R0, tmpb[bi], True, True, [("tmpb", bi), "cb"], [psk(st["rb"])])
                        TT_(r.tmp[st["t1"]], ps(b), rope[:, 0, :], ALU.mult, [psk(b), "rope"], [("tmp", st["t1"])])

                    def s2():
                        TT_(r.tmp[2], ps(st["rb"]), rope[:, 1, :], ALU.mult, [psk(st["rb"]), "rope"], [("tmp", 2)])
                        TT_(out_t[:, h, :], r.tmp[st["t1"]], r.tmp[2], ALU.add, [("tmp", st["t1"]), ("tmp", 2)],
                            [(okey, h)])
                    return [s0, s1, s2]

                def simple_item(gi, ft, first, evac, tm=False):
                    st, s0 = mk_proj(gi, ft, first, tm)

                    def s1():
                        evac(st["b"])
                    return [s0, s1]

                def kvA(c):
                    def f():
                        for h in range(4):
                            PE(PS[5][:, h * 128:(h + 1) * 128], KTs[:, h, c * 128:(c + 1) * 128], IDENT, True, True,
                               [("KTs", h), "cb"], [psk(5)])
                        for h in range(4):
                            prog.add("act", (lambda h=h: lambda e: e.mul(Kf[:, h, :], PS[5][:, h * 128:(h + 1) * 128],
                                                                          kcol[:, h:h + 1]))(),
                                     [psk(5), "kcol"], [("Kf", h)])
                            TS(Kb[:, h, :], PS[5][:, h * 128:(h + 1) * 128], kcol[:, 4 + h:5 + h], None, ALU.mult, None,
                               [psk(5), "kcol"], [("Kb", h)])
                    return [f]

                def kvB(c):
                    cg_ = (t % 4) * 4 + c

                    def f():
                        for h in range(4):
                            PE(PS[6][:, h * 128:(h + 1) * 128], Kf[:, h, :], Vs[:, c, h * 128:(h + 1) * 128], True, True,
                               [("Kf", h), ("Vs", c)], [psk(6)])
                            PE(PS[7][:, h * 128:(h + 1) * 128], Kb[:, h, :], Vs[:, c, h * 128:(h + 1) * 128], True, True,
                               [("Kb", h), ("Vs", c)], [psk(7)])
                        ACT(FSs[:, c * 4:(c + 1) * 4, :], Fst, AF.Copy, ["Fst"], [("FSs", c)])
                        for h in range(4):
                            STT(Fst[:, h, :], Fst[:, h, :], cd[:, h:h + 1], PS[6][:, h * 128:(h + 1) * 128], ALU.mult,
                                ALU.add, ["Fst", psk(6), "cd"], ["Fst"])
                            STT(Btot[:, h, :], PS[7][:, h * 128:(h + 1) * 128], wbc[:, h * 32 + cg_:h * 32 + cg_ + 1],
                                Btot[:, h, :], ALU.mult, ALU.add, ["Btot", psk(7), "wbc"], ["Btot"])
                    return [f]

                items = []
                for h in range(4):
                    items.append(rope_item(0, h, h == 0, KTs, "KTs"))
                for c in range(4):
                    items.append(simple_item(1, c, c == 0,
                                             (lambda c=c: lambda b: ACT(Vs[:, c, :], ps(b), AF.Copy, [psk(b)],
                                                                        [("Vs", c)]))(), tm=True))
                for h in range(4):
                    items.append(kvA(h))
                    items.append(rope_item(2, h, h == 0, QTs, "QTs"))
                    items.append(kvB(h))
                for h in range(4):
                    items.append(simple_item(3, h, h == 0,
                                             (lambda h=h: lambda b: ACT(Gs[:, h, :], ps(b), AF.Silu, [psk(b)],
                                                                        [("Gs", h)]))()))
                for h in range(4):
                    items.append(simple_item(4, h, h == 0,
                                             (lambda h=h: lambda b: ACT(sigb[:, h, :], ps(b), AF.Sigmoid, [psk(b)],
                                                                        [("sigb", h)]))()))
                for h in range(4):
                    items.append(simple_item(5, h, h == 0,
                                             (lambda h=h: lambda b: TT_(Us[:, h, :], ps(b), sigb[:, h, :], ALU.mult,
                                                                        [psk(b), ("sigb", h)], [("Us", h)]))()))
                run_items(items)
                DMA("pool", KT0[:, t * TT:(t + 1) * TT].rearrange("(h p) t -> p h t", p=128), KTs, st_sem["kt"],
                    [("KTs", h) for h in range(4)], [("KT0", t)])
                DMA("pool", V0[t * TT:(t + 1) * TT, :].rearrange("(c p) e -> p c e", p=128), Vs, st_sem["v"],
                    [("Vs", c) for c in range(4)], [("V0", t)])
                DMA("pool", FS[t].rearrange("p (a e) -> p a e", e=128), FSs, st_sem["fs"],
                    [("FSs", c) for c in range(4)], [("FS", t)])
                DMA("pool", QT0[:, t * TT:(t + 1) * TT].rearrange("(h p) t -> p h t", p=128), QTs, st_sem["qt"],
                    [("QTs", h) for h in range(4)], [("QT0", t)])
                DMA("pool", G0[:, t * TT:(t + 1) * TT].rearrange("(h p) t -> p h t", p=128), Gs, st_sem["g"],
                    [("Gs", h) for h in range(4)], [("G0", t)])
                sg_ = t // 4
                if t % 4 == 0:
                    CP(edge[sg_][:, :, 0:15], Us[:, :, 0:15], [("Us", h) for h in range(4)], [("edge", sg_)])
                if t % 4 == 3:
                    CP(edge[sg_][:, :, 15:30], Us[:, :, 497:512], [("Us", h) for h in range(4)], [("edge", sg_)])
                ub = sg_ * 2080 + 16 + (t % 4) * TT
                DMA("pool", U0[:, ub:ub + TT].rearrange("(h p) t -> p h t", p=128), Us, st_sem["u"],
                    [("Us", h) for h in range(4)], [("U0", t)])
                if t % 4 == 3:
                    DMA("pool", EXin[sg_][:, 0:128].rearrange("(h p) e -> p h e", p=128), Fst, st_sem["ex"], ["Fst"],
                        [("EXin", sg_)])
                    DMA("pool", EXin[sg_][:, 128:256].rearrange("(h p) e -> p h e", p=128), Btot, st_sem["ex"],
                        ["Btot"], [("EXin", sg_)])
                    DMA("pool", EXin[sg_][:, 256:288].rearrange("(h p) e -> p h e", p=128), edge[sg_], st_sem["ex"],
                        [("edge", sg_)], [("EXin", sg_)])
                    prog.add("pool", (lambda sg_=sg_: lambda e: e.collective_compute(
                        "AllGather", ALU.bypass, replica_groups=[[0, 1, 2, 3], [4, 5, 6, 7]],
                        ins=[EXin[sg_].opt()], outs=[EXout[sg_].opt()]))(),
                        [("EXin", sg_)], [("EXout", sg_)], dma=(ex_sem[sg_], 1), persist=True)
                    if t == 3:
                        prog.add("dve", lambda e: e.memset(Fst, 0.0), [], ["Fst"])
                        prog.add("dve", lambda e: e.memset(Btot, 0.0), [], ["Btot"])
                if not small:
                    issue_casts(8 if t < NT - 1 else 0)
            stop_point("P1")
            prog.barrier()
            release_phase_sems()

            af.top = afm
            ab.top = abm
            r = alloc_rowlocal(2, with_sg=False)
            cv = af.alloc(4, 512)
            Bst = af.alloc(4, 128)
            tpos = af.alloc(512)
            coef = af.alloc(64)
            Fin = af.alloc(2, 4, 128)
            Bin = af.alloc(2, 4, 128)
            hal = af.alloc(2, 4, 32)
            qt = ab.alloc(4, 512)
            kt = ab.alloc(4, 512)
            vt = ab.alloc(4, 512)
            gt = ab.alloc(4, 512)
            ut = ab.alloc(4, 544)
            fs = ab.alloc(16, 128)
            Qf = ab.alloc(4, 512)
            Qb = ab.alloc(4, 512)
            Qg = ab.alloc(4, 512)
            gf = [ab.alloc(512) for _ in range(2)]
            Kb2 = ab.alloc(4, 128)
            Bbf = ab.alloc(4, 128)
            Finbf = ab.alloc(2, 4, 128)
            AT = [ab.alloc(4, 128) for _ in range(2)]
            cat = ab.alloc(8, 512)
            diag = ab.alloc(124, 128)
            wabo = ab.alloc(2, 8, 512)
            halo = ab.alloc(2, 4, 32)
            ld = {k: newsem() for k in ("qt", "kt", "vt", "gt", "ut", "fs", "tpos", "ex", "wabo")}
            ld["halo"] = newsem(sw=True)
            st_x2 = [newsem(sw=True), newsem(sw=True)]

            for sg_ in range(2):
                for d in range(2):
                    for h in range(4):
                        o = ((sg_ * 2 + d) * 4 + h) * 4
                        dcol = (sg_ * 2 + d) * 4
                        ACT(coef[:, o:o + 4], cmask[:, dcol:dcol + 4], AF.Exp, ["cmask", "lg"], ["coef"],
                            scale=lg[:, d * 4 + h:d * 4 + h + 1])
                        TT_(coef[:, o:o + 4], coef[:, o:o + 4], cmask[:, 16 + dcol:16 + dcol + 4], ALU.mult,
                            ["coef", "cmask"], ["coef"])
            stage = cv[:, :, 0:288]
            CVK = [("cv", ct_) for ct_ in range(4)]

            def p2a_setup(sg_):
                if True:
                    for i in range(4):
                        DMA("sp", stage, EXout[sg_][i * 512:(i + 1) * 512, 0:288].rearrange(
                            "(h p) e -> p h e", p=128), ld["ex"], [("EXout", sg_)], CVK)
                        for h in range(4):
                            cf = ((sg_ * 2 + 0) * 4 + h) * 4 + i
                            cb_ = ((sg_ * 2 + 1) * 4 + h) * 4 + i
                            if i == 0:
                                TS(Fin[:, sg_, h, :], stage[:, h, 0:128], coef[:, cf:cf + 1], None, ALU.mult, None,
                                   CVK + ["coef"], [("Fin", sg_)])
                                TS(Bin[:, sg_, h, :], stage[:, h, 128:256], coef[:, cb_:cb_ + 1], None, ALU.mult, None,
                                   CVK + ["coef"], [("Bin", sg_)])
                            else:
                                STT(Fin[:, sg_, h, :], stage[:, h, 0:128], coef[:, cf:cf + 1], Fin[:, sg_, h, :], ALU.mult,
                                    ALU.add, CVK + ["coef", ("Fin", sg_)], [("Fin", sg_)])
                                STT(Bin[:, sg_, h, :], stage[:, h, 128:256], coef[:, cb_:cb_ + 1], Bin[:, sg_, h, :],
                                    ALU.mult, ALU.add, CVK + ["coef", ("Bin", sg_)], [("Bin", sg_)])
                        ml = 32 + (sg_ * 2 + 0) * 4 + i
                        mr = 32 + (sg_ * 2 + 1) * 4 + i
                        if i == 0:
                            TS(hal[:, sg_, :, 0:16], stage[:, :, 271:287], cmask[:, ml:ml + 1], None, ALU.mult, None,
                               CVK + ["cmask"], [("hal", sg_)])
                            TS(hal[:, sg_, :, 16:32], stage[:, :, 256:272], cmask[:, mr:mr + 1], None, ALU.mult, None,
                               CVK + ["cmask"], [("hal", sg_)])
                        else:
                            STT(hal[:, sg_, :, 0:16], stage[:, :, 271:287], cmask[:, ml:ml + 1], hal[:, sg_, :, 0:16],
                                ALU.mult, ALU.add, CVK + ["cmask", ("hal", sg_)], [("hal", sg_)])
                            STT(hal[:, sg_, :, 16:32], stage[:, :, 256:272], cmask[:, mr:mr + 1], hal[:, sg_, :, 16:32],
                                ALU.mult, ALU.add, CVK + ["cmask", ("hal", sg_)], [("hal", sg_)])
                    CP(Finbf[:, sg_], Fin[:, sg_], [("Fin", sg_)], [("Finbf", sg_)])
                    CP(halo[:, sg_], hal[:, sg_], [("hal", sg_)], [("halo", sg_)])
                    sbb = sg_ * 2080
                    DMA("pool", U0[:, sbb + 1:sbb + 16].rearrange("(h p) t -> p h t", p=128), halo[:, sg_, :, 0:15],
                        ld["halo"], [("halo", sg_)], ["U0h"])
                    DMA("pool", U0[:, sbb + 16 + 2048:sbb + 16 + 2048 + 15].rearrange("(h p) t -> p h t", p=128),
                        halo[:, sg_, :, 16:31], ld["halo"], [("halo", sg_)], ["U0h"])
            DMA("sp", wabo, WABO.rearrange("g p (k c) -> p g k c", c=512), ld["wabo"], ["WABO"], ["wabo"])
            for ct in range(4):
                for w in range(31):
                    TS(diag[:, ct * 31 + w, :], identf, convw[:, ct * 31 + w:ct * 31 + w + 1], None, ALU.mult, None,
                       ["identf", "convw"], ["diag"])

            def p2a_loads(t):
                cs = slice(t * TT, (t + 1) * TT)
                DMA("sp", qt, QT0[:, cs].rearrange("(h p) t -> p h t", p=128), ld["qt"], [("QT0", t)], ["qt"])
                DMA("sp", kt, KT0[:, cs].rearrange("(h p) t -> p h t", p=128), ld["kt"], [("KT0", t)], ["kt"])
                DMA("sp", vt, V0[cs, :].rearrange("(c p) e -> p c e", p=128), ld["vt"], [("V0", t)], ["vt"])
                DMA("sp", fs, FS[t].rearrange("p (a e) -> p a e", e=128), ld["fs"], [("FS", t)], ["fs"])
                DMA("sp", tpos, tpos_d[:, cs], ld["tpos"], [], ["tpos"])

            def p2a_load_ut(t):
                ub = (t // 4) * 2080 + 1 + (t % 4) * TT
                DMA("sp", ut[:, :, 0:542], U0[:, ub:ub + 542].rearrange("(h p) t -> p h t", p=128),
                    ld["ut"], [("U0", tt_) for tt_ in range(NT)] + ["U0h"], ["ut"])

            def p2a_load_gt(t):
                cs = slice(t * TT, (t + 1) * TT)
                DMA("sp", gt, G0[:, cs].rearrange("(h p) t -> p h t", p=128), ld["gt"], [("G0", t)], ["gt"])

            ORDER = [3, 2, 1, 0, 7, 6, 5, 4]
            p2a_setup(0)
            load_x(r, X1, ORDER[0], 0, "X1")
            p2a_loads(ORDER[0])
            p2a_load_ut(ORDER[0])
            p2a_load_gt(ORDER[0])
            for oi_, t in enumerate(ORDER):
                xs = oi_ % 2
                nxt = ORDER[oi_ + 1] if oi_ + 1 < NT else None
                cs = slice(t * TT, (t + 1) * TT)
                if nxt is not None:
                    load_x(r, X1, nxt, 1 - xs, "X1")
                sg_ = t // 4
                if t % 4 == 3:
                    CP(Bst, Bin[:, sg_], [("Bin", sg_)], ["Bst"])
                    ACT(Bbf, Bst, AF.Copy, ["Bst"], ["Bbf"])
                for h in range(4):
                    gi = h % 2
                    ACT(gf[gi], tpos, AF.Exp, ["tpos", "lg"], [("gf", gi)], scale=lg[:, h:h + 1])
                    TT_(Qg[:, h, :], qt[:, h, :], gf[gi], ALU.mult, ["qt", ("gf", gi)], [("Qg", h)])
                    TT_(Qf[:, h, :], qt[:, h, :], rowf[:, h, :], ALU.mult, ["qt", "rowf"], [("Qf", h)])
                    TT_(Qb[:, h, :], qt[:, h, :], rowb[:, h, :], ALU.mult, ["qt", "rowb"], [("Qb", h)])
                for c in range(3, -1, -1):
                    cc = slice(c * 128, (c + 1) * 128)
                    sb = c % 2
                    ct = 3 - c
                    for h in range(4):
                        PE(PS[7][:, h * 128:(h + 1) * 128], kt[:, h, cc], IDENT, True, True, ["kt", "cb"], [psk(7)])
                    for h in range(4):
                        PE(PS[0][:, h * 128:(h + 1) * 128], kt[:, h, cc], qt[:, h, cc], True, True, ["kt", "qt"],
                           [psk(0)])
                    for h in range(4):
                        TS(Kb2[:, h, :], PS[7][:, h * 128:(h + 1) * 128], kcol[:, 4 + h:5 + h], None, ALU.mult, None,
                           [psk(7), "kcol"], [("Kb2", h)])
                    TT_(AT[sb], PS[0][:, :].rearrange("p (h i) -> p h i", i=128),
                        Dt[:, :, 0:128], ALU.mult, [psk(0), "Dt"], [("AT", sb)])
                    for w in range(0, 16):
                        PE(ps(1), diag[:, ct * 31 + w, :], ut[:, ct, w:w + 512], w == 0, w == 30, ["diag", "ut"], [psk(1)])
                    for h in range(4):
                        hs = slice(h * 128, (h + 1) * 128)
                        ob = 2 + h
                        PE(PS[ob][:, cc], vt[:, c, hs], AT[sb][:, h, :], True, False, ["vt", ("AT", sb)], [psk(ob)])
                        PE(PS[ob][:, cc], fs[:, c * 4 + h, :], Qf[:, h, cc], False, False, ["fs", ("Qf", h)], [psk(ob)])
                        PE(PS[ob][:, cc], Bbf[:, h, :], Qb[:, h, cc], False, False, ["Bbf", ("Qb", h)], [psk(ob)])
                        PE(PS[ob][:, cc], Finbf[:, sg_, h, :], Qg[:, h, cc], False, True, [("Finbf", sg_), ("Qg", h)], [psk(ob)])
                    for h in range(4):
                        hs = slice(h * 128, (h + 1) * 128)
                        PE(PS[6][:, hs], Kb2[:, h, :], vt[:, c, hs], True, True, [("Kb2", h), "vt"], [psk(6)])
                    for w in range(16, 31):
                        PE(ps(1), diag[:, ct * 31 + w, :], ut[:, ct, w:w + 512], w == 0, w == 30, ["diag", "ut"], [psk(1)])
                    for h in range(4):
                        hs = slice(h * 128, (h + 1) * 128)
                        STT(Bst[:, h, :], Bst[:, h, :], cd[:, 4 + h:5 + h], PS[6][:, hs], ALU.mult, ALU.add,
                            ["Bst", psk(6), "cd"], ["Bst"])
                    ACT(Bbf, Bst, AF.Copy, ["Bst"], ["Bbf"])
                    ACT(cv[:, ct, :], ps(1), AF.Identity, [psk(1), "gvec"], [("cv", ct)], bias=gvec[:, 52 + ct:53 + ct])
                if nxt is not None:
                    p2a_loads(nxt)
                    if nxt != 7:
                        p2a_load_ut(nxt)
                for h in range(4):
                    ob = 2 + h
                    si = rr(r, "sq", 4)
                    ACT(r.sq[si], ps(ob), AF.Square, [psk(ob)], [("sq", si)])
                    PE(ps(0), ONES, r.sq[si], True, True, [("sq", si), "cb"], [psk(0)])
                    ri = rstd_from(r, 0, 1.0 / 128, None)
                    tb = rr(r, "tmp", 3)
                    STT(r.tmp[tb], ps(ob), gvec[:, 48 + h:49 + h], r.rstd[ri], ALU.mult, ALU.mult,
                        [psk(ob), ("rstd", ri), "gvec"], [("tmp", tb)])
                    TT_(cat[:, h, :], r.tmp[tb], gt[:, h, :], ALU.mult, [("tmp", tb), "gt"], [("cat", h)])
                if nxt is not None:
                    p2a_load_gt(nxt)
                for ct in range(4):
                    si = rr(r, "sq", 4)
                    ACT(r.sq[si], cv[:, ct, :], AF.Square, [("cv", ct)], [("sq", si)])
                    PE(ps(1), ONES, r.sq[si], ct == 0, ct == 3, [("sq", si), "cb"], [psk(1)])
                ri = rstd_from(r, 1, 1.0 / 512, None)
                for ct in range(4):
                    tb = rr(r, "tmp", 3)
                    STT(r.tmp[tb], cv[:, ct, :], gvec[:, 56 + ct:57 + ct], r.rstd[ri], ALU.mult, ALU.mult,
                        [("cv", ct), ("rstd", ri), "gvec"], [("tmp", tb)])
                    ACT(cat[:, 4 + ct, :], r.tmp[tb], AF.Silu, [("tmp", tb)], [("cat", 4 + ct)])
                if nxt == 7:
                    p2a_setup(1)
                    p2a_load_ut(7)
                for m in range(8):
                    b = m % 2
                    for c8 in range(8):
                        PE(ps(b), wabo[:, m // 4, c8, (m % 4) * 128:(m % 4 + 1) * 128], cat[:, c8, :], c8 == 0, c8 == 7,
                           ["wabo", ("cat", c8)], [psk(b)])
                    TT_(r.xres[xs][:, m, :], ps(b), r.xres[xs][:, m, :], ALU.add, [psk(b), ("xres", xs)], [("xres", xs)])
                store_x(r, X2, t, xs, "X2", st_x2)
                if not small:
                    issue_casts(max(0, min(10, len(cast_q) - 56)))
            stop_point("P2a")
            prog.barrier()
            release_phase_sems()

            af.top = afm
            ab.top = abm
            r = alloc_rowlocal(2)
            alloc_ffn(r)
            rope = af.alloc(2, 512)
            rope_sem = newsem()
            wproj = [ab.alloc(8, 512) for _ in range(2)]
            wprojsem = [newsem() for _ in range(2)]
            Q1s = ab.alloc(8, 512)
            K1s = ab.alloc(8, 512)
            V1s = ab.alloc(4, 1024)
            tmpb = [ab.alloc(512) for _ in range(2)]
            st2 = {k: newsem(sw=True) for k in ("q", "k", "v")}
            st2["x4"] = [newsem(sw=True), newsem(sw=True)]
            nproj[0] = 0
            kvsem = [newsem(persist=True) for _ in range(16)]

            def gatherA(h):
                prog.add("pool", lambda e: e.collective_compute(
                    "AllGather", ALU.bypass, replica_groups=[[0, 1, 2, 3], [4, 5, 6, 7]],
                    ins=[KVinA[h].opt()], outs=[KVoutA[h].opt()]),
                    [("KVdone", 0, tl_, kv_) for tl_ in range(4) for kv_ in "kv"], [("KVoutA", h)],
                    dma=(kvsem[2 * h], 1), persist=True)

            def gatherB(h):
                prog.add("pool", lambda e: e.collective_compute(
                    "AllGather", ALU.bypass, replica_groups=[[0, 1], [2, 3], [4, 5], [6, 7]],
                    ins=[KVinB[h].opt()], outs=[KVoutB[h].opt()]),
                    [("KVdone", 1, tl_, kv_) for tl_ in range(4) for kv_ in "kv"], [("KVoutB", h)],
                    dma=(kvsem[2 * h + 1], 1), persist=True)

            def qk_post(b, gcol, out_ap, outkey):
                si = rr(r, "sq", 4)
                ACT(r.sq[si], ps(b), AF.Square, [psk(b)], [("sq", si)])
                PE(ps(6), BONES, r.sq[si], True, True, [("sq", si), "cb"], [psk(6)])
                ri = rstd_from(r, 6, 1.0 / 64, None)
                tb = rr(r, "tmp", 3)
                qn = r.tmp[tb]
                STT(qn, ps(b), gvec[:, gcol:gcol + 1], r.rstd[ri], ALU.mult, ALU.mult,
                    [psk(b), ("rstd", ri), "gvec"], [("tmp", tb)])
                bi = r.cnt["sg"] % 2
                r.cnt["sg"] += 1
                ACT(tmpb[bi], qn, AF.Copy, [("tmp", tb)], [("tmpb", bi)])
                PE(ps(7), R1, tmpb[bi], True, True, [("tmpb", bi), "cb"], [psk(7)])
                tb2 = rr(r, "tmp", 3)
                TT_(r.tmp[tb2], ps(7), rope[:, 1, :], ALU.mult, [psk(7), "rope"], [("tmp", tb2)])
                TT_(qn, qn, rope[:, 0, :], ALU.mult, [("tmp", tb), "rope"], [("tmp", tb)])
                TT_(out_ap, qn, r.tmp[tb2], ALU.add, [("tmp", tb), ("tmp", tb2)], [outkey])

            load_x(r, X2, 0, 0, "X2")
            for t in range(NT):
                xs = t % 2
                cs = slice(t * TT, (t + 1) * TT)
                if t + 1 < NT:
                    load_x(r, X2, t + 1, 1 - xs, "X2")
                rmsnorm(r, xs, 16)
                ffn(r, 0, 1, xs, fuse_norm=True)
                rmsnorm(r, xs, 24, pre=True)
                ffn(r, 1, 0, xs, fuse_norm=True)
                DMA("sp", rope, rope1_d[:, :, cs].rearrange("a p t -> p a t"), rope_sem, [], ["rope"])
                store_x(r, X4, t, xs, "X4", st2["x4"])
                rmsnorm(r, xs, 32, pre=True)
                gslot = {}
                gslot[0] = load_wproj(WC, "WC", 0)
                gslot[1] = load_wproj(WC, "WC", 1)
                pc = {"pb": 0, "sb": 0, "rb": 0, "rs": 0, "qn": 0}

                def mk_proj2(g, ft, first, tm=False):
                    st = {}

                    def s0():
                        if first and g + 1 < 6 and g >= 1:
                            gslot[g + 1] = load_wproj(WC, "WC", g + 1)
                        st["b"] = pc["pb"] % 4
                        pc["pb"] += 1
                        if tm:
                            proj_tm(gslot[g], ft, st["b"])
                        else:
                            proj_fm(gslot[g], ft, st["b"])
                    return st, s0

                def qk_item(g, ft, first, gcol, out_t, okey):
                    st, s0 = mk_proj2(g, ft, first)
                    hh = (g % 2) * 4 + ft

                    def s1():
                        b = st["b"]
                        si = rr(r, "sq", 4)
                        st["sb"] = 4 + pc["sb"] % 2
                        pc["sb"] += 1
                        ACT(r.sq[si], ps(b), AF.Square, [psk(b)], [("sq", si)])
                        PE(ps(st["sb"]), BONES, r.sq[si], True, True, [("sq", si), "cb"], [psk(st["sb"])])

                    def s2():
                        ri = pc["rs"] % 2
                        pc["rs"] += 1
                        st["ri"] = ri
                        sb = st["sb"]
                        ACT(r.rstd[ri], ps(sb), AF.Ln, [psk(sb), "misc0"], [("rstd", ri)], scale=1.0 / 64,
                            bias=misc[:, 0:1])
                        ACT(r.rstd[ri], r.rstd[ri], AF.Exp, [("rstd", ri)], [("rstd", ri)], scale=-0.5)

                    def s3():
                        b = st["b"]
                        ri = st["ri"]
                        qi = pc["qn"] % 2
                        pc["qn"] += 1
                        st["qi"] = qi
                        st["rb"] = 6 + pc["rb"] % 2
                        pc["rb"] += 1
                        qn = r.tmp[qi]
                        STT(qn, ps(b), gvec[:, gcol:gcol + 1], r.rstd[ri], ALU.mult, ALU.mult,
                            [psk(b), ("rstd", ri), "gvec"], [("tmp", qi)])
                        bi = r.cnt["sg"] % 2
                        r.cnt["sg"] += 1
                        ACT(tmpb[bi], qn, AF.Copy, [("tmp", qi)], [("tmpb", bi)])
                        PE(ps(st["rb"]), R1, tmpb[bi], True, True, [("tmpb", bi), "cb"], [psk(st["rb"])])
                        TT_(qn, qn, rope[:, 0, :], ALU.mult, [("tmp", qi), "rope"], [("tmp", qi)])

                    def s4():
                        qi = st["qi"]
                        TT_(r.tmp[2], ps(st["rb"]), rope[:, 1, :], ALU.mult, [psk(st["rb"]), "rope"], [("tmp", 2)])
                        TT_(out_t[:, hh, :], r.tmp[qi], r.tmp[2], ALU.add, [("tmp", qi), ("tmp", 2)], [(okey, hh)])
                    return [s0, s1, s2, s3, s4]

                def v_item(g, c, first):
                    st, s0 = mk_proj2(g, c, first, tm=True)

                    def s1():
                        b = st["b"]
                        ACT(V1s[:, c, (g - 4) * 512:(g - 3) * 512], ps(b), AF.Copy, [psk(b)], [("V1s", c)])
                    return [s0, s1]

                items = []
                for g in range(6):
                    for ft in range(4):
                        if g < 2:
                            items.append(qk_item(g, ft, ft == 0, 60, Q1s, "Q1s"))
                        elif g < 4:
                            items.append(qk_item(g, ft, ft == 0, 61, K1s, "K1s"))
                        else:
                            items.append(v_item(g, ft, ft == 0))
                run_items(items)
                DMA("pool", QT1[:, :, cs].rearrange("h p t -> p h t"), Q1s, st2["q"],
                    [("Q1s", h) for h in range(8)], [("QT1", t)])
                KVs = KVinA if t < 4 else KVinB
                tl = t % 4
                for h in range(8):
                    DMA("pool", KVs[h][0:128, tl * TT:(tl + 1) * TT], K1s[:, h, :], st2["k"],
                        [("K1s", h_) for h_ in range(8)], [("KVdone", t // 4, tl, "k")] if h == 7 else [])
                for c in range(4):
                    for h in range(8):
                        dst = KVs[h][128:256, :].rearrange("r (a e) -> (r a) e", e=128)[
                            tl * TT + c * 128:tl * TT + (c + 1) * 128, :]
                        DMA("pool", dst, V1s[:, c, h * 128:(h + 1) * 128], st2["v"],
                            [("V1s", c_) for c_ in range(4)],
                            [("KVdone", t // 4, tl, "v")] if (h == 7 and c == 3) else [])
                if t == NT - 1:
                    gatherA(0)
                    gatherB(0)
            stop_point("P2b")
            prog.barrier()
            release_phase_sems()

            af.top = afm
            ab.top = abm
            ones_f = af.alloc(128)
            acc = [af.alloc(1024) for _ in range(2)]
            rc = af.alloc(1024)
            t01 = af.alloc(1024)
            obuf = [af.alloc(512) for _ in range(2)]
            rstd3 = af.alloc(512)
            KTa = [ab.alloc(6, 2048) for _ in range(2)]
            Va = [ab.alloc(6, 16, 128) for _ in range(2)]
            Qa = [ab.alloc(T) for _ in range(2)]
            PT = [ab.alloc(1024) for _ in range(3)]
            sq3 = [ab.alloc(512) for _ in range(2)]
            ONs = [ab.alloc(512) for _ in range(4)]
            ldk = [[newsem() for _ in range(2)] for _ in range(2)]
            ldv = [[newsem() for _ in range(2)] for _ in range(2)]
            ldq = [newsem() for _ in range(2)]
            st_on = [newsem(sw=True) for _ in range(4)]
            prog.add("dve", lambda e: e.memset(ones_f, 1.0), [], ["ones_f"])
            SPAIR = [(0, 1), (2, 3)]
            PVPAIR = [(4, 5), (6, 7)]

            def pair_ap(p):
                return PSall[:, p[0] * 512:p[0] * 512 + 1024]

            def pair_keys(p):
                return [psk(p[0]), psk(p[1])]

            state = {"ring": 0, "pt": 0, "on": 0, "ptp": 0}
            steps = []
            pending = None
            for h in range(8):
                for qc in range(8):
                    for kidx in range(64 if qc < 4 else 32):
                        steps.append(("qk", h, qc, kidx))
                        if kidx == 1 and pending is not None:
                            steps.append(pending)
                            pending = None
                    pending = ("sum", h, qc, 0)
            steps.append(pending)

            def load_head(h):
                hs = h % 2
                DMA("sp", Qa[hs], QT1[h], ldq[hs], [("QT1", t) for t in range(NT)], [("Qa", hs)])
                for blk in range(6):
                    part = 0 if blk < 4 else 1
                    src = KVoutA[h] if blk < 4 else KVoutB[h]
                    key = ("KVoutA", h) if blk < 4 else ("KVoutB", h)
                    b0 = (blk if blk < 4 else blk - 4) * 256
                    DMA("sp", KTa[hs][:, blk, :], src[b0:b0 + 128, :], ldk[hs][part], [key], [("KTa", hs, part)])
                    DMA("sp", Va[hs][:, blk, :, :],
                        src[b0 + 128:b0 + 256, :].rearrange("r (a e) -> (r a) e", e=128).rearrange(
                            "(kt p) e -> p kt e", p=128),
                        ldv[hs][part], [key], [("Va", hs, part)])

            def produce(st):
                kind, h, qc, kidx = st
                hs = h % 2
                slot = state["ring"] % 2
                state["ring"] += 1
                sp = SPAIR[slot]
                if kind == "qk":
                    blk, ktile = divmod(kidx + (0 if qc < 4 else 64), 16)
                    ks = slice(ktile * 128, (ktile + 1) * 128)
                    qs = slice(qc * 512, (qc + 1) * 512)
                    pi = state["ptp"] % 3
                    state["ptp"] += 1
                    for comp in range(2):
                        dsl = slice(comp * 64, (comp + 1) * 64)
                        PE(ps(sp[comp]), KTa[hs][dsl, blk, ks], Qa[hs][dsl, qs], True, True,
                           [("KTa", hs, 0 if qc < 4 else 1), ("Qa", hs)],
                           [psk(sp[comp])] + ([("PT", pi)] if comp == 0 else []))
                else:
                    par = qc % 2
                    for comp in range(2):
                        PE(ps(sp[comp]), ones_f, acc[par][:, comp * 512:(comp + 1) * 512], True, True,
                           [("acc", par), "ones_f"], [psk(sp[comp])])
                return slot

            deferred = []

            def part2(h, qc):
                par = qc % 2
                pv = PVPAIR[par]
                qs = slice(qc * 512, (qc + 1) * 512)
                PE(ps(pv[0]), ONES, sq3[par], True, True, [("sq3", par), "cb"], [psk(pv[0])])
                ACT(rstd3, ps(pv[0]), AF.Ln, [psk(pv[0]), "misc0"], ["rstd3"], scale=1.0 / 128, bias=misc[:, 0:1])
                ACT(rstd3, rstd3, AF.Exp, ["rstd3"], ["rstd3"], scale=-0.5)
                oi = state["on"] % 4
                state["on"] += 1
                STT(ONs[oi], obuf[par], misc[:, 2:3], rstd3, ALU.mult, ALU.mult, [("obuf", par), "rstd3", "m2"],
                    [("ONs", oi)])
                DMA("pool", ON[h][:, qs], ONs[oi], st_on[oi], [("ONs", oi)], [("ON", qc)])

            def consume(st, slot, idx):
                kind, h, qc, kidx = st
                hs = h % 2
                sp = SPAIR[slot]
                par = qc % 2
                pv = PVPAIR[par]
                if kind == "qk":
                    blk, ktile = divmod(kidx + (0 if qc < 4 else 64), 16)
                    nk = 64 if qc < 4 else 32
                    pi = state["pt"] % 3
                    state["pt"] += 1
                    ACT(PT[pi], pair_ap(sp), AF.Exp, pair_keys(sp) + ["m3", "m4"], [("PT", pi)], scale=0.125,
                        bias=misc[:, 3:4])
                    if kidx == 0:
                        CP(acc[par], PT[pi], [("PT", pi)], [("acc", par)])
                    else:
                        TT_(acc[par], acc[par], PT[pi], ALU.add, [("PT", pi), ("acc", par)], [("acc", par)])
                    for comp in range(2):
                        half = PT[pi][:, comp * 512:(comp + 1) * 512]
                        PE(ps(pv[comp]), Va[hs][:, blk, ktile, :], half, kidx == 0, kidx == nk - 1,
                           [("Va", hs, 0 if qc < 4 else 1), ("PT", pi)], [psk(pv[comp])])
                else:
                    ACT(rc, pair_ap(sp), AF.Ln, pair_keys(sp), ["rc"])
                    ACT(rc, rc, AF.Exp, ["rc"], ["rc"], scale=-1.0)
                    TT_(t01, pair_ap(pv), rc, ALU.mult, pair_keys(pv) + ["rc"], ["t01"])
                    STT(obuf[par], t01[:, 512:1024], misc[:, 1:2], t01[:, 0:512], ALU.mult, ALU.add, ["t01", "m1"],
                        [("obuf", par)])
                    ACT(sq3[par], obuf[par], AF.Square, [("obuf", par)], [("sq3", par)])
                    deferred.append((idx + 3, h, qc))

            load_head(0)
            if not small:
                issue_casts(len(cast_q))
            nst = len(steps)
            slots = {}
            slots[0] = produce(steps[0])
            for i in range(nst):
                st = steps[i]
                if st[0] == "qk" and st[2] == 0 and st[3] == 0 and st[1] + 1 < 8:
                    gatherA(st[1] + 1)
                    gatherB(st[1] + 1)
                    load_head(st[1] + 1)
                nxt = steps[i + 1] if i + 1 < nst else None
                late = nxt is not None and nxt[0] == "sum" and st[0] == "qk" and st[1:3] == nxt[1:3]
                if nxt is not None and not late:
                    slots[i + 1] = produce(nxt)
                consume(st, slots[i], i)
                if nxt is not None and late:
                    slots[i + 1] = produce(nxt)
                while deferred and deferred[0][0] <= i:
                    _, dh, dq = deferred.pop(0)
                    part2(dh, dq)
            while deferred:
                _, dh, dq = deferred.pop(0)
                part2(dh, dq)
            stop_point("P3")
            prog.barrier()
            release_phase_sems()

            af.top = afm
            ab.top = abm
            r = alloc_rowlocal(2)
            alloc_ffn(r)
            wco = ab.alloc(2, 8, 512)
            ont = [ab.alloc(8, 512) for _ in range(2)]
            ldo = [newsem() for _ in range(2)]
            ldw = newsem()
            st_y = [newsem(sw=True), newsem(sw=True)]
            DMA("sp", wco, WCO.rearrange("g p (k c) -> p g k c", c=512), ldw, ["WCO"], ["wco"])
            load_x(r, X4, 0, 0, "X4")
            DMA("sp", ont[0], ON[:, :, 0:TT].rearrange("h p t -> p h t"), ldo[0], [("ON", qc) for qc in range(8)],
                [("ont", 0)])
            for t in range(NT):
                xs = t % 2
                if t + 1 < NT:
                    load_x(r, X4, t + 1, 1 - xs, "X4")
                    DMA("sp", ont[1 - xs], ON[:, :, (t + 1) * TT:(t + 2) * TT].rearrange("h p t -> p h t"), ldo[1 - xs],
                        [("ON", qc) for qc in range(8)], [("ont", 1 - xs)])
                for m in range(8):
                    b = m % 2
                    for c8 in range(8):
                        PE(ps(b), wco[:, m // 4, c8, (m % 4) * 128:(m % 4 + 1) * 128], ont[xs][:, c8, :], c8 == 0, c8 == 7,
                           ["wco", ("ont", xs)], [psk(b)])
                    if m >= 1:
                        norm_stat(r, xs, m - 1)
                    TT_(r.xres[xs][:, m, :], ps(b), r.xres[xs][:, m, :], ALU.add, [psk(b), ("xres", xs)], [("xres", xs)])
                norm_stat(r, xs, 7)
                rmsnorm(r, xs, 40, pre=True)
                ffn(r, 1, 1, xs)
                store_x(r, yT, t, xs, "yT", st_y)


        except _Stop:
            pass
        prog.analyze()
        with nc.Block() as block:
            @block.tensor
            def _(e):
                prog.emit("pe", e, esems)

            @block.scalar
            def _(e):
                prog.emit("act", e, esems)

            @block.vector
            def _(e):
                prog.emit("dve", e, esems)

            @block.gpsimd
            def _(e):
                prog.emit("pool", e, esems)

            @block.sync
            def _(e):
                prog.emit("sp", e, esems)
    return nc


def _rope_tables(half, rep, pos):
    inv = (np.float32(10000.0) ** (-np.arange(half, dtype=np.float32) / np.float32(half))).astype(np.float32)
    pos = np.asarray(pos, dtype=np.float32)
    ang = (pos[:, None] * inv[None, :]).astype(np.float32)
    cos = np.cos(ang).astype(np.float32).T
    sin = np.sin(ang).astype(np.float32).T
    C = np.concatenate([cos, cos], axis=0)
    S = np.concatenate([-sin, sin], axis=0)
    C = np.tile(C, (rep, 1))
    S = np.tile(S, (rep, 1))
    return np.ascontiguousarray(np.stack([C, S], axis=0))


def _consts():
    p = np.arange(128)
    ident = np.eye(128, dtype=np.float32)
    ones = np.ones((128, 128), np.float32)
    bones = np.zeros((128, 128), np.float32)
    bones[:64, :64] = 1
    bones[64:, 64:] = 1
    R0 = np.zeros((128, 128), np.float32)
    R0[(p + 64) % 128, p] = 1
    R1 = np.zeros((128, 128), np.float32)
    src = np.where((p % 64) < 32, p + 32, p - 32)
    R1[src, p] = 1
    j = p[:, None].astype(np.float32)
    i = np.arange(128)[None, :].astype(np.float32)
    M1 = np.maximum(i - j, 0)
    M2 = np.maximum(j - i, 0)
    MLE = (j <= i).astype(np.float32)
    MGT = (j > i).astype(np.float32)
    IO1 = np.broadcast_to(i + 1, (128, 128))
    IOB = np.broadcast_to(128 - i, (128, 128))
    t4 = lambda a: np.tile(a, (1, 4))
    colA = (127 - p).astype(np.float32)[:, None]
    colB = p.astype(np.float32)[:, None]
    ch = np.broadcast_to((128.0 * np.arange(32))[None, :], (128, 32)).astype(np.float32)
    cst = np.concatenate([ident, ones, bones, R0, R1, t4(M1), t4(M2), t4(MLE), t4(MGT), t4(IO1), t4(IOB),
                          colA, colB, ch], axis=1).astype(np.float32)
    tpos = np.broadcast_to(((np.arange(T) % 2048).astype(np.float32) + 1.0)[None, :], (128, T))
    return np.ascontiguousarray(cst), np.ascontiguousarray(tpos)


_NC_CACHE = {}


def _host_inputs(inputs):
    f = lambda a: np.ascontiguousarray(np.asarray(a, dtype=np.float32))
    xp = f(inputs["x_prompt"])
    xsm = f(inputs["x_sample"])
    norm_g = f(inputs["norm_g"])
    gvec = np.zeros((128, 64), np.float32)
    for l in range(2):
        for i in range(3):
            gvec[:, (l * 3 + i) * 8:(l * 3 + i) * 8 + 8] = norm_g[l, i].reshape(8, 128).T
    gvec[:, 48:52] = f(inputs["ab_ret_norm_g"])[0].reshape(4, 128).T
    gvec[:, 52:56] = f(inputs["ab_conv_b"])[0].reshape(4, 128).T
    gvec[:, 56:60] = f(inputs["ab_conv_norm_g"])[0].reshape(4, 128).T
    gq = f(inputs["c_q_norm_g"])[0]
    gk = f(inputs["c_k_norm_g"])[0]
    gvec[:, 60] = np.tile(gq, 2)
    gvec[:, 61] = np.tile(gk, 2)
    gvec[:, 62] = f(inputs["c_subln_g"])[0]
    cw = f(inputs["ab_conv_w"])[0]
    convw = np.ascontiguousarray(cw.T.reshape(4, 128, 31).transpose(1, 0, 2).reshape(128, 124))
    decay = np.ascontiguousarray(np.broadcast_to(f(inputs["ab_decay"])[0].reshape(1, 8), (128, 8)))
    lam = np.ascontiguousarray(np.broadcast_to(f(inputs["c_lambda"])[0].reshape(1, 256), (128, 256)))
    gqk = np.ascontiguousarray(np.broadcast_to(np.concatenate([gq, gk])[None, :], (128, 128)))
    cst, tpos = _consts()
    shared = {
        "ffn_w_in": f(inputs["ffn_w_in"]), "ffn_w_out": f(inputs["ffn_w_out"]),
        "ab_w_in": f(inputs["ab_w_in"])[0], "ab_w_out": f(inputs["ab_w_out"])[0],
        "c_w_in": f(inputs["c_w_in"])[0], "c_w_out": f(inputs["c_w_out"])[0],
        "gvec": gvec, "convw": convw, "decay": decay, "lam": lam, "gqk": gqk, "cst": cst, "tpos": tpos,
    }
    in_maps = []
    SEG = 2048
    for c in range(8):
        g, j = divmod(c, 4)
        hb = j % 2
        pidx = 2 * g + j // 2
        segA = xsm[g, j * SEG:(j + 1) * SEG]
        segB = xp[pidx, hb * SEG:(hb + 1) * SEG]
        xt = np.concatenate([segA, segB], axis=0).T
        pos = np.concatenate([j * SEG + np.arange(SEG), hb * SEG + np.arange(SEG)]).astype(np.float32)
        tab = np.zeros(64, np.float32)
        for i in range(4):
            if i < j:
                tab[0 + i] = SEG * (j - 1 - i)
                tab[16 + 0 + i] = 1.0
            if i > j:
                tab[4 + i] = SEG * (i - j - 1)
                tab[16 + 4 + i] = 1.0
            if hb == 1 and i == j - 1:
                tab[16 + 8 + i] = 1.0
            if hb == 0 and i == j + 1:
                tab[16 + 12 + i] = 1.0
            if i == j - 1:
                tab[32 + 0 + i] = 1.0
            if i == j + 1:
                tab[32 + 4 + i] = 1.0
            if hb == 1 and i == j - 1:
                tab[32 + 8 + i] = 1.0
            if hb == 0 and i == j + 1:
                tab[32 + 12 + i] = 1.0
        m = dict(shared)
        m["xT"] = np.ascontiguousarray(xt)
        m["rope0"] = _rope_tables(64, 1, pos)
        m["rope1"] = _rope_tables(32, 2, pos)
        m["cmask"] = np.ascontiguousarray(np.broadcast_to(tab[None, :], (128, 64)))
        in_maps.append(m)
    return in_maps


def kernel(**inputs):
    in_maps = _host_inputs(inputs)
    if "nc" not in _NC_CACHE:
        _NC_CACHE["nc"] = build_program()
    nc = _NC_CACHE["nc"]
    res = run_bass_kernel_spmd(nc, in_maps, core_ids=list(range(8)))
    outs = [np.asarray(res.results[c]["yT"], dtype=np.float32) for c in range(8)]
    SEG = 2048
    y_prompt = np.zeros((4, 4096, D), np.float32)
    y_sample = np.zeros((2, 8192, D), np.float32)
    for c in range(8):
        g, j = divmod(c, 4)
        hb = j % 2
        pidx = 2 * g + j // 2
        o = outs[c].T
        y_sample[g, j * SEG:(j + 1) * SEG] = o[0:SEG]
        y_prompt[pidx, hb * SEG:(hb + 1) * SEG] = o[SEG:2 * SEG]
    return (y_prompt, y_sample)
```
